# Optimizing a Trainium2 kernel written in Bass

```python
import math
import jax, jax.numpy as jnp
from jax import lax
import numpy as np

D_MODEL = 2048
BATCH = 16
SEQ = 2048
DEPTH = 1
DEC_BATCH = 4
DEC_SEQ = 8192
PAST_LEN = 128

MIX_WIDTH = D_MODEL
ATTN_WIDTH = MIX_WIDTH // 2
SSM_WIDTH = MIX_WIDTH - ATTN_WIDTH
HEAD_DIM = 128
N_HEADS = ATTN_WIDTH // HEAD_DIM
N_KV_HEADS = 2
GQA_GROUP = N_HEADS // N_KV_HEADS
KV_WIDTH = N_KV_HEADS * HEAD_DIM
ROPE_AXIS_DIM = HEAD_DIM // 2
ROPE_PAIRS = ROPE_AXIS_DIM // 2
ROPE_THETA = 10000.0
GRID_W = 64
Q_BLOCK = 128
SSM_GROUP = 16
N_SSM_GROUPS = SSM_WIDTH // SSM_GROUP
SSM_STATE = 64
DT_MIN = 0.001
DT_MAX = 0.1
NORM_EPS = 1e-6
IN_COLS = ATTN_WIDTH + 2 * KV_WIDTH + ATTN_WIDTH + SSM_WIDTH + SSM_WIDTH

kernel_name = 'hymba_axial_gqa_s5_bidir_encoder'


def rms_norm(x, w):
    xf = x.astype(jnp.float32)
    return xf * lax.rsqrt(jnp.mean(xf * xf, axis=-1, keepdims=True) + NORM_EPS) * w.astype(jnp.float32)


def axial_rope_tables(n_tokens):
    n_rows = n_tokens // GRID_W
    row = jnp.repeat(jnp.arange(n_rows, dtype=jnp.float32), GRID_W)
    col = jnp.tile(jnp.arange(GRID_W, dtype=jnp.float32), n_rows)
    inv_freq = 1.0 / (ROPE_THETA ** (jnp.arange(ROPE_PAIRS, dtype=jnp.float32) / ROPE_PAIRS))
    ang_r = row[:, None] * inv_freq[None, :]
    ang_c = col[:, None] * inv_freq[None, :]
    ang = jnp.concatenate([ang_r, ang_r, ang_c, ang_c], axis=-1)
    return jnp.cos(ang), jnp.sin(ang)


def apply_axial_rope(x, cos, sin):
    x4 = x.reshape(x.shape[:-1] + (2, 2, ROPE_PAIRS))
    rot = jnp.stack([-x4[..., 1, :], x4[..., 0, :]], axis=-2).reshape(x.shape)
    return x * cos[None, :, None, :] + rot * sin[None, :, None, :]


def block_attention(q, k, v):
    b, s = q.shape[0], q.shape[1]
    n_blocks = s // Q_BLOCK
    qb = q.reshape(b, n_blocks, Q_BLOCK, N_KV_HEADS, GQA_GROUP, HEAD_DIM).transpose(1, 0, 2, 3, 4, 5)
    scale = HEAD_DIM ** -0.5

    def one_block(q_blk):
        scores = jnp.einsum('bqkgd,bskd->bkgqs', q_blk, k) * scale
        probs = jax.nn.softmax(scores, axis=-1)
        return jnp.einsum('bkgqs,bskd->bqkgd', probs, v)

    out = lax.map(one_block, qb)
    return out.transpose(1, 0, 2, 3, 4, 5).reshape(b, s, ATTN_WIDTH)


def _ssm_combine(left, right):
    a_l, b_l = left
    a_r, b_r = right
    return a_r * a_l, a_r * b_l + b_r


def s5_branch(u, lambda_re, lambda_im, log_step, b_re, b_im, c_re, c_im, d_skip):
    bsz, length = u.shape[0], u.shape[1]
    uf = u.astype(jnp.float32)
    ug = uf.reshape(bsz, length, N_SSM_GROUPS, SSM_GROUP)
    bu = lax.complex(jnp.einsum('blgc,gpc->blgp', ug, b_re.astype(jnp.float32)),
                     jnp.einsum('blgc,gpc->blgp', ug, b_im.astype(jnp.float32)))
    y = uf * d_skip.astype(jnp.float32)
    for direction in range(2):
        lam = lax.complex(lambda_re[direction].astype(jnp.float32), lambda_im[direction].astype(jnp.float32))
        dt = jnp.exp(log_step[direction].astype(jnp.float32))[:, None]
        lam_bar = jnp.exp(lam * dt)
        b_scale = (lam_bar - 1.0) / lam
        a_seq = jnp.broadcast_to(lam_bar, (1, length, N_SSM_GROUPS, SSM_STATE))
        _, states = lax.associative_scan(_ssm_combine, (a_seq, bu * b_scale),
                                         reverse=(direction == 1), axis=1)
        y_dir = (jnp.einsum('blgp,gcp->blgc', jnp.real(states), c_re[direction].astype(jnp.float32))
                 - jnp.einsum('blgp,gcp->blgc', jnp.imag(states), c_im[direction].astype(jnp.float32)))
        y = y + y_dir.reshape(bsz, length, SSM_WIDTH)
    return y


def encoder_layer(x, norm_w, w_in, q_norm_w, k_norm_w, lambda_re, lambda_im, log_step,
                  b_re, b_im, c_re, c_im, d_skip, w_glu, attn_out_norm_w, ssm_out_norm_w, w_out):
    bsz, s = x.shape[0], x.shape[1]
    h = rms_norm(x, norm_w).astype(x.dtype)
    z = jnp.einsum('bsd,de->bse', h, w_in)
    o0 = 0
    o1 = o0 + ATTN_WIDTH
    o2 = o1 + KV_WIDTH
    o3 = o2 + KV_WIDTH
    o4 = o3 + ATTN_WIDTH
    o5 = o4 + SSM_WIDTH
    q = z[..., o0:o1].reshape(bsz, s, N_HEADS, HEAD_DIM)
    k = z[..., o1:o2].reshape(bsz, s, N_KV_HEADS, HEAD_DIM)
    v = z[..., o2:o3].reshape(bsz, s, N_KV_HEADS, HEAD_DIM).astype(jnp.float32)
    g_attn = z[..., o3:o4].astype(jnp.float32)
    u_ssm = z[..., o4:o5]
    g_ssm = z[..., o5:].astype(jnp.float32)

    cos, sin = axial_rope_tables(s)
    q = apply_axial_rope(rms_norm(q, q_norm_w), cos, sin)
    k = apply_axial_rope(rms_norm(k, k_norm_w), cos, sin)
    q = q.reshape(bsz, s, N_KV_HEADS, GQA_GROUP, HEAD_DIM)
    o_attn = block_attention(q, k, v) * jax.nn.silu(g_attn)

    y_ssm = jax.nn.gelu(s5_branch(u_ssm, lambda_re, lambda_im, log_step, b_re, b_im, c_re, c_im, d_skip))
    y_ssm = y_ssm * jax.nn.sigmoid(jnp.einsum('bse,ef->bsf', y_ssm, w_glu.astype(jnp.float32)))
    o_ssm = y_ssm * jax.nn.silu(g_ssm)

    merged = jnp.concatenate([rms_norm(o_attn, attn_out_norm_w), rms_norm(o_ssm, ssm_out_norm_w)],
                             axis=-1).astype(x.dtype)
    return x + jnp.einsum('bse,ed->bsd', merged, w_out).astype(x.dtype)


def setup_inputs(seed: int = 0) -> dict:
    key = jax.random.key(seed)
    ks = jax.random.split(key, 20)
    f32 = jnp.float32
    G, P, C = N_SSM_GROUPS, SSM_STATE, SSM_GROUP
    x_prompt = jax.random.normal(ks[0], (BATCH, SEQ, D_MODEL), f32)
    x_sample = jax.random.normal(ks[1], (DEC_BATCH, DEC_SEQ, D_MODEL), f32)
    norm_w = 1.0 + 0.02 * jax.random.normal(ks[2], (DEPTH, D_MODEL), f32)
    w_in = jax.random.normal(ks[3], (DEPTH, D_MODEL, IN_COLS), f32) * D_MODEL ** -0.5
    q_norm_w = 1.0 + 0.02 * jax.random.normal(ks[4], (DEPTH, HEAD_DIM), f32)
    k_norm_w = 1.0 + 0.02 * jax.random.normal(ks[5], (DEPTH, HEAD_DIM), f32)
    lambda_re = -0.5 + 0.01 * jax.random.normal(ks[6], (DEPTH, 2, G, P), f32)
    lambda_im = jnp.broadcast_to(math.pi * jnp.arange(P, dtype=f32), (DEPTH, 2, G, P)) \
        + 0.01 * jax.random.normal(ks[7], (DEPTH, 2, G, P), f32)
    log_step = jax.random.uniform(ks[8], (DEPTH, 2, G), f32) * (math.log(DT_MAX) - math.log(DT_MIN)) \
        + math.log(DT_MIN)
    b_re = jax.random.normal(ks[9], (DEPTH, G, P, C), f32) * (2 * C) ** -0.5
    b_im = jax.random.normal(ks[10], (DEPTH, G, P, C), f32) * (2 * C) ** -0.5
    c_re = jax.random.normal(ks[11], (DEPTH, 2, G, C, P), f32) * (2 * P) ** -0.5
    c_im = jax.random.normal(ks[12], (DEPTH, 2, G, C, P), f32) * (2 * P) ** -0.5
    d_skip = jax.random.normal(ks[13], (DEPTH, SSM_WIDTH), f32)
    w_glu = jax.random.normal(ks[14], (DEPTH, SSM_WIDTH, SSM_WIDTH), f32) * SSM_WIDTH ** -0.5
    attn_out_norm_w = 1.0 + 0.02 * jax.random.normal(ks[15], (DEPTH, ATTN_WIDTH), f32)
    ssm_out_norm_w = 1.0 + 0.02 * jax.random.normal(ks[16], (DEPTH, SSM_WIDTH), f32)
    w_out = jax.random.normal(ks[17], (DEPTH, MIX_WIDTH, D_MODEL), f32) * MIX_WIDTH ** -0.5
    return {'x_prompt': x_prompt, 'x_sample': x_sample, 'norm_w': norm_w, 'w_in': w_in,
            'q_norm_w': q_norm_w, 'k_norm_w': k_norm_w, 'lambda_re': lambda_re,
            'lambda_im': lambda_im, 'log_step': log_step, 'b_re': b_re, 'b_im': b_im,
            'c_re': c_re, 'c_im': c_im, 'd_skip': d_skip, 'w_glu': w_glu,
            'attn_out_norm_w': attn_out_norm_w, 'ssm_out_norm_w': ssm_out_norm_w, 'w_out': w_out}


def reference(x_prompt, x_sample, norm_w, w_in, q_norm_w, k_norm_w, lambda_re, lambda_im, log_step,
              b_re, b_im, c_re, c_im, d_skip, w_glu, attn_out_norm_w, ssm_out_norm_w, w_out):
    hp = x_prompt
    hs = x_sample
    for l in range(DEPTH):
        p = (norm_w[l], w_in[l], q_norm_w[l], k_norm_w[l], lambda_re[l], lambda_im[l], log_step[l],
             b_re[l], b_im[l], c_re[l], c_im[l], d_skip[l], w_glu[l], attn_out_norm_w[l],
             ssm_out_norm_w[l], w_out[l])
        hp = encoder_layer(hp, *p)
        hs = encoder_layer(hs, *p)
    y_prompt = hp
    y_sample = hs
    return (y_prompt, y_sample)
```

```python
import math
import numpy as np
import concourse.bass as bass
import concourse.mybir as mybir
from concourse.bass_utils import run_bass_kernel_spmd

F32 = mybir.dt.float32
BF16 = mybir.dt.bfloat16
I32 = mybir.dt.int32
AF = mybir.ActivationFunctionType
ALU = mybir.AluOpType

D = 2048
NDT = 16
IN_COLS = 4608
EPS = 1e-6
TWO_PI_S = 6.283185
TWO_PI = 2.0 * math.pi
C_Q, C_K, C_V, C_GA, C_U, C_GS = 0, 1024, 1280, 1536, 2560, 3584
AFREE = 2 * 2 * 32 * 65


class Res:
    __slots__ = ("name", "w", "rd", "sem", "cnt")

    def __init__(self, name):
        self.name = name
        self.w = {}
        self.rd = {}
        self.sem = None
        self.cnt = 0


class Ctx:
    def __init__(self, nc):
        self.nc = nc
        self.eng = {"pe": nc.tensor, "act": nc.scalar, "dve": nc.vector, "pool": nc.gpsimd, "sp": nc.sync}
        self.sem = {k: nc.alloc_semaphore("sem_" + k) for k in self.eng}
        self.cnt = {k: 0 for k in self.eng}
        self.waited = {k: {} for k in self.eng}
        self.nres = 0
        self.allres = []

    def res(self, name=None):
        self.nres += 1
        r = Res(name or "r%d" % self.nres)
        self.allres.append(r)
        return r

    def _wait(self, eng, tok, same_ok=True):
        sem, val = tok
        if sem is self.sem[eng] and not same_ok:
            return
        k = id(sem)
        if self.waited[eng].get(k, 0) >= val:
            return
        self.eng[eng].wait_ge(sem, val)
        self.waited[eng][k] = val

    def _deps(self, eng, r, w, wo=()):
        for x in r:
            for t in x.w.values():
                self._wait(eng, t)
        for x in w:
            for t in x.w.values():
                self._wait(eng, t)
            for t in x.rd.values():
                self._wait(eng, t, same_ok=False)
        for x in wo:
            for t in x.w.values():
                self._wait(eng, t, same_ok=False)
            for t in x.rd.values():
                self._wait(eng, t, same_ok=False)

    def _mark(self, tok, r, w):
        k = id(tok[0])
        for x in r:
            if x in w:
                continue
            x.rd[k] = tok
        for x in w:
            x.w = {k: tok}
            x.rd = {}

    def op(self, eng, fn, r=(), w=(), wo=()):
        self._deps(eng, r, w, wo)
        ins = fn(self.eng[eng])
        self.cnt[eng] += 1
        ins.then_inc(self.sem[eng], 1)
        self._mark((self.sem[eng], self.cnt[eng]), r, tuple(w) + tuple(wo))

    def dma(self, q, out, in_, r=(), w=(), **kw):
        self._deps(q, r, w)
        own = w[0] if w else r[0]
        if own.sem is None:
            own.sem = self.nc.alloc_semaphore("dsem_" + own.name)
        ins = self.eng[q].dma_start(out=out, in_=in_, **kw)
        own.cnt += 16
        ins.then_inc(own.sem, 16)
        self._mark((own.sem, own.cnt), r, w)

    def barrier(self):
        toks = [(self.sem[k], self.cnt[k]) for k in self.eng if self.cnt[k] > 0]
        for x in self.allres:
            if x.sem is not None and x.cnt > 0:
                toks.append((x.sem, x.cnt))
        for e in self.eng:
            for t in toks:
                self._wait(e, t)


class Stream:
    def __init__(self, C, q, slots, srcs, r_extra=(), **kw):
        self.C, self.q, self.slots, self.srcs, self.r_extra, self.kw = C, q, slots, srcs, tuple(r_extra), kw
        self.issued = 0

    def get(self, k):
        n = len(self.slots)
        while self.issued < min(len(self.srcs), k + n):
            j = self.issued
            t, res = self.slots[j % n]
            src = self.srcs[j]
            view = None
            if isinstance(src, tuple):
                src, view = src
            dst = view(t) if view else t[:]
            for x in self.r_extra:
                for tok in x.w.values():
                    self.C._wait(self.q, tok)
            self.C.dma(self.q, dst, src, w=(res,), **self.kw)
            self.issued += 1
        return self.slots[k % n]


def dram_ap(t, offset, pat):
    return bass.AP(t.tensor, offset, [list(p) for p in pat])


def build(cfg):
    NP, LP, LSO, LSX = cfg["np"], cfg["lp"], cfg["lso"], cfg["lsx"]
    nc = bass.Bass("TRN2", target_bir_lowering=False)
    C = Ctx(nc)
    din = lambda name, shape: nc.dram_tensor(name, list(shape), F32, kind="ExternalInput").ap()
    x_p = din("x_p", [NP * LP, D])
    x_so = din("x_so", [LSO, D])
    x_sx = din("x_sx", [max(LSX, 128), D])
    flag_d = din("flag", [128, 4])
    norm_w = din("norm_w", [1, D])
    w_in = din("w_in", [1, D, IN_COLS])
    q_norm_w = din("q_norm_w", [1, 128])
    k_norm_w = din("k_norm_w", [1, 128])
    lambda_re = din("lambda_re", [1, 2, 64, 64])
    lambda_im = din("lambda_im", [1, 2, 64, 64])
    log_step = din("log_step", [1, 2, 64])
    b_re = din("b_re", [1, 64, 64, 16])
    b_im = din("b_im", [1, 64, 64, 16])
    c_re = din("c_re", [1, 2, 64, 16, 64])
    c_im = din("c_im", [1, 2, 64, 16, 64])
    d_skip = din("d_skip", [1, 1024])
    w_glu = din("w_glu", [1, 1024, 1024])
    attn_out_norm_w = din("attn_out_norm_w", [1, 1024])
    ssm_out_norm_w = din("ssm_out_norm_w", [1, 1024])
    w_out = din("w_out", [1, D, D])
    NTOK = NP * LP + LSO
    y_all = nc.dram_tensor("y_all", [NTOK, D], F32, kind="ExternalOutput").ap()
    wA = nc.dram_tensor("wA", [8, 128, NDT * 256], BF16, kind="Internal").ap()
    wB = nc.dram_tensor("wB", [5, 128, NDT * 512], BF16, kind="Internal").ap()
    wO = nc.dram_tensor("wO", [4, 128, NDT * 512], BF16, kind="Internal").ap()
    wG = nc.dram_tensor("wG", [4, 128, 8 * 256], BF16, kind="Internal").ap()
    ms_d = nc.dram_tensor("ms_scr", [8, 128, NTOK], BF16, kind="Internal").ap()

    jobs = []
    for j in range(NP):
        jobs.append(dict(kind="P", xo=x_p[j * LP:(j + 1) * LP, :], xx=None, n_own=LP // 128, n_oth=0, tok0=j * LP))
    jobs.append(dict(kind="S", xo=x_so, xx=x_sx, n_own=LSO // 128, n_oth=LSX // 128, tok0=NP * LP))
    MAXU = max(j["n_own"] for j in jobs) * 2
    MAXS = max(j["n_own"] + j["n_oth"] for j in jobs)

    sb = lambda name, shape, dt=F32: nc.alloc_sbuf_tensor(name, list(shape), dt)
    dbg = cfg.get("debug", False)
    dumped = {}

    def dump(name, ap, res):
        if not dbg or name in dumped:
            return
        shp = list(ap.shape)
        t_ = nc.dram_tensor("dbg_" + name, shp, ap.dtype, kind="ExternalOutput").ap()
        dumped[name] = t_
        C.dma("sp", t_, ap, r=(res,))
    banks = [nc.alloc_psum_tensor("bank%d" % i, [128, 512], F32) for i in range(8)]
    bres = [C.res("bank%d" % i) for i in range(8)]

    def bcast(ap, shape, axis):
        return ap.unsqueeze(axis).broadcast_to(list(shape))

    wc = C.res("wcast")
    for b in range(8):
        c0 = (C_U + 256 * b) if b < 4 else (C_GS + 256 * (b - 4))
        C.dma("pool", wA[b].rearrange("p (dt c) -> p dt c", dt=NDT), w_in[0, :, c0:c0 + 256].rearrange("(dt p) c -> p dt c", p=128), w=(wc,))
    for b, c0 in enumerate((C_Q, C_Q + 512, C_K, C_GA, C_GA + 512)):
        C.dma("pool", wB[b].rearrange("p (dt c) -> p dt c", dt=NDT), w_in[0, :, c0:c0 + 512].rearrange("(dt p) c -> p dt c", p=128), w=(wc,))
    for b in range(4):
        C.dma("pool", wO[b].rearrange("p (dt c) -> p dt c", dt=NDT), w_out[0, :, 512 * b:512 * b + 512].rearrange("(dt p) c -> p dt c", p=128), w=(wc,))
        C.dma("pool", wG[b].rearrange("p (et c) -> p et c", et=8), w_glu[0, :, 256 * b:256 * b + 256].rearrange("(et p) c -> p et c", p=128), w=(wc,))

    cst = C.res("const")
    ident = sb("ident", [128, 128]); identb = sb("identb", [128, 128], BF16)
    ones_t = sb("ones_t", [128, 128]); onesb = sb("onesb", [128, 128], BF16)
    C.op("pool", lambda e: e.memset(ones_t[:], 1.0), w=(cst,))
    C.op("pool", lambda e: e.memset(onesb[:], 1.0), w=(cst,))
    C.op("pool", lambda e: e.affine_select(out=ident[:], in_=ones_t[:], pattern=[[-1, 128]], compare_op=ALU.is_equal,
                                           fill=0.0, base=0, channel_multiplier=1), w=(cst,))
    C.op("dve", lambda e: e.tensor_copy(out=identb[:], in_=ident[:]), r=(), w=(cst,))
    flag = sb("flag_sb", [128, 4]); nwcol = sb("nwcol", [128, NDT])
    qnw = sb("qnw", [128, 128]); knw = sb("knw", [128, 128])
    dcol = sb("dcol", [128, 8]); snwcol = sb("snwcol", [128, 8])
    mhalf = sb("mhalf", [128, 1])
    C.op("pool", lambda e: e.memset(mhalf[:], -0.5), w=(cst,))
    ld = C.res("cload")
    C.dma("sp", flag[:], flag_d[:, :], w=(ld,))
    C.dma("sp", nwcol[:], dram_ap(norm_w, 0, [[1, 128], [128, NDT]]), w=(ld,), allow_slow_non_contiguous=True)
    C.dma("sp", qnw[:], dram_ap(q_norm_w, 0, [[0, 128], [1, 128]]), w=(ld,))
    C.dma("sp", knw[:], dram_ap(k_norm_w, 0, [[0, 128], [1, 128]]), w=(ld,))
    C.dma("sp", dcol[:], dram_ap(d_skip, 0, [[1, 128], [128, 8]]), w=(ld,), allow_slow_non_contiguous=True)
    C.dma("sp", snwcol[:], dram_ap(ssm_out_norm_w, 0, [[1, 128], [128, 8]]), w=(ld,), allow_slow_non_contiguous=True)

    fq = sb("fq", [128, 32]); jt = sb("jt", [128, 32]); pidx = sb("pidx", [128, 1])
    hicol = sb("hicol", [128, 1]); colpos = sb("colpos", [128, 1])
    C.op("pool", lambda e: e.iota(jt[:], pattern=[[1, 32]], base=0, channel_multiplier=0, allow_small_or_imprecise_dtypes=True), w=(cst,))
    C.op("pool", lambda e: e.iota(pidx[:], pattern=[[0, 1]], base=0, channel_multiplier=1, allow_small_or_imprecise_dtypes=True), w=(cst,))
    C.op("act", lambda e: e.activation(out=fq[:], in_=jt[:], func=AF.Exp, scale=-math.log(10000.0) / 32.0), w=(cst,))
    C.op("dve", lambda e: e.tensor_scalar_mul(out=fq[:], in0=fq[:], scalar1=1.0 / TWO_PI), w=(cst,))
    C.op("dve", lambda e: e.tensor_single_scalar(out=hicol[:], in_=pidx[:], scalar=64.0, op=ALU.is_ge), w=(cst,))
    C.op("dve", lambda e: e.scalar_tensor_tensor(out=colpos[:], in0=hicol[:], scalar=-64.0, in1=pidx[:], op0=ALU.mult, op1=ALU.add), w=(cst,))

    def cyc_sin(eng_w, dst, q, tf, ti, shift=0.0):
        if shift != 0.0:
            C.op("dve", lambda e: e.tensor_scalar_add(out=q, in0=q, scalar1=shift), w=eng_w)
        C.op("dve", lambda e: e.tensor_copy(out=ti, in_=q), w=eng_w)
        C.op("dve", lambda e: e.tensor_copy(out=tf, in_=ti), w=eng_w)
        C.op("dve", lambda e: e.tensor_sub(out=q, in0=q, in1=tf), w=eng_w)
        C.op("act", lambda e: e.activation(out=dst, in_=q, func=AF.Sin, scale=TWO_PI_S), w=eng_w)

    xts = [(sb("xt%d" % i, [128, D]), C.res("xt%d" % i)) for i in range(2)]
    hT = sb("hT", [128, NDT, 128], BF16); hres = C.res("hT")
    stat = sb("stat", [128, 8]); stres = C.res("stat")
    dg = sb("dg", [128, 128]); rrep = sb("rrep", [128, 128]); rres = C.res("rrep")
    import contextlib
    ssm_stack = contextlib.ExitStack()
    ssb = lambda name, shape, dt=F32: ssm_stack.enter_context(nc.sbuf_tensor(name, list(shape), dt))
    Bp = ssb("Bp", [128, 32, 2, 2, 128], BF16)
    Cp = ssb("Cp", [128, 32, 2, 2, 128], BF16)
    Lr2 = ssb("Lr2", [128, 2, 2, 32]); LiN = ssb("LiN", [128, 2, 32]); LiP = ssb("LiP", [128, 2, 32])
    Ls_r = ssb("Ls_r", [128, 2, 32]); Ls_n = ssb("Ls_n", [128, 32]); Ls_p = ssb("Ls_p", [128, 32])
    sres = C.res("ssmw")
    with contextlib.ExitStack() as st_:
        lre = st_.enter_context(nc.sbuf_tensor("lre", [128, 32, 2], F32))
        lim = st_.enter_context(nc.sbuf_tensor("lim", [128, 32, 2], F32))
        lst = st_.enter_context(nc.sbuf_tensor("lst", [128, 32, 2], F32))
        Br = st_.enter_context(nc.sbuf_tensor("Br", [128, 32, 16], F32))
        Bi = st_.enter_context(nc.sbuf_tensor("Bi", [128, 32, 16], F32))
        Cn = st_.enter_context(nc.sbuf_tensor("Cn", [128, 2, 2, 8, 2, 64], F32))
        s1 = st_.enter_context(nc.sbuf_tensor("s1", [128, 32, 2], F32))
        s2 = st_.enter_context(nc.sbuf_tensor("s2", [128, 32, 2], F32))
        s3 = st_.enter_context(nc.sbuf_tensor("s3", [128, 32, 2], F32))
        si = st_.enter_context(nc.sbuf_tensor("si", [128, 32, 2], I32))
        lbr = st_.enter_context(nc.sbuf_tensor("lbr", [128, 32, 2], F32))
        lbi = st_.enter_context(nc.sbuf_tensor("lbi", [128, 32, 2], F32))
        bsr = st_.enter_context(nc.sbuf_tensor("bsr", [128, 32, 2], F32))
        bsi = st_.enter_context(nc.sbuf_tensor("bsi", [128, 32, 2], F32))
        Bpr = st_.enter_context(nc.sbuf_tensor("Bpr", [128, 32, 2, 2, 16], F32))
        t16 = st_.enter_context(nc.sbuf_tensor("t16", [128, 32, 2, 16], F32))
        Mc = st_.enter_context(nc.sbuf_tensor("Mc", [128, 4, 8, 16], F32))
        onesm = st_.enter_context(nc.sbuf_tensor("onesm", [128, 4, 8, 16], F32))
        pre = st_.enter_context(nc.sbuf_tensor("pre", [128, 4, 128], F32))
        Trep = st_.enter_context(nc.sbuf_tensor("Trep", [128, 2, 8, 2, 128], F32))
        for gp in range(2):
            ps_ = slice(64 * gp, 64 * gp + 64)
            for d in range(2):
                C.dma("sp", lre[ps_, :, d], dram_ap(lambda_re, gp * 64 + d * 4096, [[1, 64], [128, 32]]), w=(ld,), allow_slow_non_contiguous=True)
                C.dma("sp", lim[ps_, :, d], dram_ap(lambda_im, gp * 64 + d * 4096, [[1, 64], [128, 32]]), w=(ld,), allow_slow_non_contiguous=True)
                C.dma("sp", lst[ps_, :, d], dram_ap(log_step, gp + d * 64, [[0, 64], [2, 32]]), w=(ld,), allow_slow_non_contiguous=True)
            C.dma("sp", Br[ps_, :, :], dram_ap(b_re, gp * 1024, [[16, 64], [2048, 32], [1, 16]]), w=(ld,))
            C.dma("sp", Bi[ps_, :, :], dram_ap(b_im, gp * 1024, [[16, 64], [2048, 32], [1, 16]]), w=(ld,))
        for ri, csrc in enumerate((c_re, c_im)):
            for dup in range(2):
                for d in range(2):
                    C.dma("sp", Cn[:, ri, d, :, dup, :], dram_ap(csrc, d * 65536, [[64, 128], [8192, 8], [1, 64]]), w=(ld,))
        W = (sres,)
        R = (ld, cst)
        C.op("act", lambda e: e.activation(out=s1[:], in_=lst[:], func=AF.Exp), r=R, w=W)
        C.op("dve", lambda e: e.tensor_mul(out=s2[:], in0=lre[:], in1=s1[:]), r=R, w=W)
        C.op("dve", lambda e: e.scalar_tensor_tensor(out=s3[:], in0=lim[:], scalar=1.0 / TWO_PI, in1=s1[:], op0=ALU.mult, op1=ALU.mult), r=R, w=W)
        C.op("act", lambda e: e.activation(out=s2[:], in_=s2[:], func=AF.Exp), w=W)
        C.op("dve", lambda e: e.tensor_copy(out=lbi[:], in_=s3[:]), w=W)
        cyc_sin(W, lbi[:], lbi[:], s1[:], si[:])
        cyc_sin(W, lbr[:], s3[:], s1[:], si[:], shift=0.25)
        C.op("dve", lambda e: e.tensor_mul(out=lbr[:], in0=lbr[:], in1=s2[:]), w=W)
        C.op("dve", lambda e: e.tensor_mul(out=lbi[:], in0=lbi[:], in1=s2[:]), w=W)
        C.op("dve", lambda e: e.tensor_mul(out=s1[:], in0=lre[:], in1=lre[:]), w=W)
        C.op("dve", lambda e: e.tensor_mul(out=s2[:], in0=lim[:], in1=lim[:]), w=W)
        C.op("dve", lambda e: e.tensor_add(out=s1[:], in0=s1[:], in1=s2[:]), w=W)
        C.op("dve", lambda e: e.reciprocal(out=s1[:], in_=s1[:]), w=W)
        C.op("dve", lambda e: e.tensor_scalar_add(out=s2[:], in0=lbr[:], scalar1=-1.0), w=W)
        C.op("dve", lambda e: e.tensor_mul(out=bsr[:], in0=s2[:], in1=lre[:]), w=W)
        C.op("dve", lambda e: e.tensor_mul(out=s3[:], in0=lbi[:], in1=lim[:]), w=W)
        C.op("dve", lambda e: e.tensor_add(out=bsr[:], in0=bsr[:], in1=s3[:]), w=W)
        C.op("dve", lambda e: e.tensor_mul(out=bsr[:], in0=bsr[:], in1=s1[:]), w=W)
        C.op("dve", lambda e: e.tensor_mul(out=bsi[:], in0=lbi[:], in1=lre[:]), w=W)
        C.op("dve", lambda e: e.tensor_mul(out=s3[:], in0=s2[:], in1=lim[:]), w=W)
        C.op("dve", lambda e: e.tensor_sub(out=bsi[:], in0=bsi[:], in1=s3[:]), w=W)
        C.op("dve", lambda e: e.tensor_mul(out=bsi[:], in0=bsi[:], in1=s1[:]), w=W)
        for d in range(2):
            for r in range(2):
                C.op("dve", lambda e, d=d, r=r: e.tensor_copy(out=Lr2[:, d, r, :], in_=lbr[:, :, d]), w=W)
            C.op("dve", lambda e, d=d: e.tensor_copy(out=LiP[:, d, :], in_=lbi[:, :, d]), w=W)
            C.op("dve", lambda e, d=d: e.tensor_scalar_mul(out=LiN[:, d, :], in0=lbi[:, :, d], scalar1=-1.0), w=W)
        for r in range(2):
            C.op("dve", lambda e, r=r: e.tensor_scalar_mul(out=Ls_r[:, r, :], in0=lbr[:, :, 0], scalar1=flag[:, 0:1]), w=W)
            C.op("dve", lambda e, r=r: e.scalar_tensor_tensor(out=Ls_r[:, r, :], in0=lbr[:, :, 1], scalar=flag[:, 1:2], in1=Ls_r[:, r, :],
                                                              op0=ALU.mult, op1=ALU.add), w=W)
        C.op("dve", lambda e: e.tensor_scalar_mul(out=Ls_p[:], in0=lbi[:, :, 0], scalar1=flag[:, 0:1]), w=W)
        C.op("dve", lambda e: e.scalar_tensor_tensor(out=Ls_p[:], in0=lbi[:, :, 1], scalar=flag[:, 1:2], in1=Ls_p[:], op0=ALU.mult, op1=ALU.add), w=W)
        C.op("dve", lambda e: e.tensor_scalar_mul(out=Ls_n[:], in0=Ls_p[:], scalar1=-1.0), w=W)
        for d in range(2):
            bsr_b = bcast(bsr[:, :, d], [128, 32, 16], 2); bsi_b = bcast(bsi[:, :, d], [128, 32, 16], 2)
            C.op("dve", lambda e, d=d, a=bsr_b: e.tensor_mul(out=Bpr[:, :, d, 0, :], in0=Br[:], in1=a), w=W)
            C.op("dve", lambda e, d=d, a=bsi_b: e.tensor_mul(out=t16[:, :, d, :], in0=Bi[:], in1=a), w=W)
            C.op("dve", lambda e, d=d: e.tensor_sub(out=Bpr[:, :, d, 0, :], in0=Bpr[:, :, d, 0, :], in1=t16[:, :, d, :]), w=W)
            C.op("dve", lambda e, d=d, a=bsr_b: e.tensor_mul(out=Bpr[:, :, d, 1, :], in0=Bi[:], in1=a), w=W)
            C.op("dve", lambda e, d=d, a=bsi_b: e.tensor_mul(out=t16[:, :, d, :], in0=Br[:], in1=a), w=W)
            C.op("dve", lambda e, d=d: e.tensor_add(out=Bpr[:, :, d, 1, :], in0=Bpr[:, :, d, 1, :], in1=t16[:, :, d, :]), w=W)
        C.op("pool", lambda e: e.memset(onesm[:], 1.0), w=W)
        for gp in range(2):
            ps_ = slice(64 * gp, 64 * gp + 64)
            C.op("pool", lambda e, ps_=ps_, gp=gp: e.affine_select(out=Mc[ps_], in_=onesm[ps_], pattern=[[-2, 4], [1, 8], [0, 16]],
                                                                   compare_op=ALU.is_equal, fill=0.0, base=-gp, channel_multiplier=0), w=W)
        k = 0
        for d in range(2):
            for r in range(2):
                for i in range(32):
                    slot = k % 4
                    C.op("dve", lambda e, i=i, d=d, r=r, slot=slot: e.tensor_mul(
                        out=pre[:, slot, :].rearrange("p (g c) -> p g c", g=8), in0=Mc[:, i % 4, :, :],
                        in1=bcast(Bpr[:, i, d, r, :], [128, 8, 16], 1)), w=W)
                    bk = (k // 4) % 2
                    C.op("pe", lambda e, slot=slot, bk=bk: e.transpose(banks[bk][:, slot * 128:(slot + 1) * 128], pre[:, slot, :], ident[:]),
                         r=W + (cst,), w=(bres[bk],))
                    if slot == 3:
                        i0 = i - 3
                        C.op("act", lambda e, bk=bk, i0=i0, d=d, r=r: e.activation(
                            out=Bp[:, i0:i0 + 4, d, r, :], in_=banks[bk][:, :].rearrange("p (a x) -> p a x", a=4), func=AF.Copy),
                            r=(bres[bk],), w=W)
                    k += 1
        C.op("dve", lambda e: e.tensor_scalar_mul(out=Cn[:, 1], in0=Cn[:, 1], scalar1=-1.0), r=R, w=W)
        k = 0
        for d in range(2):
            for gt in range(8):
                for r in range(2):
                    bk = (k // 4) % 2; slot = k % 4
                    C.op("pe", lambda e, bk=bk, slot=slot, r=r, d=d, gt=gt: e.transpose(
                        banks[bk][:, slot * 128:(slot + 1) * 128], Cn[:, r, d, gt, :, :].rearrange("p a b -> p (a b)"), ident[:]),
                        r=W + (cst,), w=(bres[bk],))
                    C.op("act", lambda e, bk=bk, slot=slot, r=r, d=d, gt=gt: e.activation(
                        out=Trep[:, d, gt, r, :], in_=banks[bk][:, slot * 128:(slot + 1) * 128], func=AF.Copy), r=(bres[bk],), w=W)
                    k += 1
        for d in range(2):
            for r in range(2):
                for i in range(32):
                    C.op("dve", lambda e, i=i, d=d, r=r: e.tensor_mul(
                        out=Cp[:, i, d, r, :].rearrange("p (g c) -> p g c", g=8),
                        in0=Trep[:, d, i // 4, r, :].rearrange("p (g c) -> p g c", g=8), in1=Mc[:, i % 4, :, :]), w=W)
        dump("pidx", pidx[:], cst); dump("hicol", hicol[:], cst); dump("colpos", colpos[:], cst); dump("fq", fq[:], cst)
        dump("lbr", lbr[:], sres); dump("lbi", lbi[:], sres); dump("bsr", bsr[:], sres); dump("bsi", bsi[:], sres)
        dump("Bpr", Bpr[:], sres); dump("Mc", Mc[:], sres); dump("Trep", Trep[:], sres)
        dump("Bp", Bp[:], sres); dump("Cp", Cp[:], sres)
        C.barrier()

    UT = 32
    NU = 128 // UT
    ACOL = UT + 1
    RS, DS, IS = 32 * ACOL, 2 * 32 * ACOL, ACOL
    AFR = 2 * DS
    MAXU2 = max(j["n_own"] for j in jobs) * NU
    Abuf = [ssb("A%d" % k, [128, 2, 2, 32, ACOL]) for k in range(2)]
    ares = [C.res("A%d" % k) for k in range(2)]
    T1 = ssb("T1", [128, 2, 2, 32]); T2 = ssb("T2", [128, 2, 2, 32])
    Xbf = ssb("Xbf", [128, 2, 2, 32, UT], BF16)
    bnd = ssb("bnd", [128, MAXU2 + 1, 2, 32], BF16)
    Ef = ssb("Ef", [128, 2, 32]); Sx = ssb("Sx", [128, 2, 32])
    wbs = [(ssb("wb%d" % i, [128, NDT, 256], BF16), C.res("wb%d" % i)) for i in range(3)]
    uTs = [ssb("uT%d" % k, [128, 8, 128], BF16) for k in range(2)]; ures = [C.res("uT%d" % k) for k in range(2)]
    uTx = [ssb("uTx%d" % k, [128, 8, 128], BF16) for k in range(2)]; uxres = [C.res("uTx%d" % k) for k in range(2)]
    gsTs = [ssb("gsT%d" % k, [128, 8, 128], BF16) for k in range(2)]; gres = [C.res("gsT%d" % k) for k in range(2)]
    yT = ssb("yT", [128, 8, 128]); yres = C.res("yT")
    g1 = ssb("g1", [128, 8, 128]); g2 = ssb("g2", [128, 8, 128]); gb = ssb("gb", [128, 8, 128], BF16); g3 = gb
    gwres = C.res("gelu")
    msT = ssb("msT", [128, 8, 128], BF16); mres = C.res("msT")
    xbres = C.res("Xbf"); eres = C.res("carry")
    psT, psL, psB, psY, psM = (0, 1), (2, 3), (4, 5), 6, 7
    junk = hT[:].rearrange("p a b -> p (a b)")

    def a_ap(ab, off, pat):
        return bass.AP(Abuf[ab][:].tensor, off, [[AFR, 128]] + [list(p) for p in pat])

    def front(xt, xres, need_rep):
        C.op("act", lambda e: e.activation(out=junk, in_=xt[:], func=AF.Square, accum_out=stat[:, 0:1]), r=(xres,), w=(hres, stres))
        C.op("dve", lambda e: e.tensor_scalar(out=stat[:, 0:1], in0=stat[:, 0:1], scalar1=1.0 / D, scalar2=EPS, op0=ALU.mult, op1=ALU.add), w=(stres,))
        C.op("act", lambda e: e.activation(out=stat[:, 0:1], in_=stat[:, 0:1], func=AF.Sqrt), w=(stres,))
        C.op("dve", lambda e: e.reciprocal(out=stat[:, 0:1], in_=stat[:, 0:1]), w=(stres,))
        for q in range(4):
            bk = psT[q % 2]
            for kk in range(4):
                dt_ = 4 * q + kk
                C.op("pe", lambda e, bk=bk, kk=kk, dt_=dt_: e.transpose(banks[bk][:, kk * 128:(kk + 1) * 128], xt[:, dt_ * 128:(dt_ + 1) * 128], ident[:]),
                     r=(xres, cst), w=(bres[bk],))
            C.op("dve", lambda e, bk=bk, q=q: e.tensor_tensor(out=hT[:, 4 * q:4 * q + 4, :], in0=banks[bk][:, :].rearrange("p (a t) -> p a t", a=4),
                                                              in1=bcast(nwcol[:, 4 * q:4 * q + 4], [128, 4, 128], 2), op=ALU.mult),
                 r=(bres[bk], ld), w=(hres,))
        if need_rep:
            C.op("dve", lambda e: e.tensor_scalar_mul(out=dg[:], in0=ident[:], scalar1=stat[:, 0:1]), r=(stres, cst), w=(rres,))
            C.op("pe", lambda e: e.matmul(banks[psM][:, 0:128], lhsT=ones_t[:], rhs=dg[:], start=True, stop=True), r=(rres, cst), w=(bres[psM],))
            C.op("act", lambda e: e.activation(out=rrep[:], in_=banks[psM][:, 0:128], func=AF.Copy), r=(bres[psM],), w=(rres,))

    def front_n(xt, xres):
        C.op("act", lambda e: e.activation(out=junk, in_=xt[:], func=AF.Square, accum_out=stat[:, 0:1]), r=(xres,), w=(hres, stres))
        C.op("pool", lambda e: e.tensor_scalar(out=stat[:, 0:1], in0=stat[:, 0:1], scalar1=1.0 / D, scalar2=EPS, op0=ALU.mult, op1=ALU.add), w=(stres,))
        C.op("pool", lambda e: e.tensor_tensor(out=stat[:, 0:1], in0=stat[:, 0:1], in1=mhalf[:], op=ALU.pow), r=(cst,), w=(stres,))
        C.op("act", lambda e: e.activation(out=xt[:], in_=xt[:], func=AF.Copy, scale=stat[:, 0:1]), r=(stres,), w=(xres,))
        for q in range(4):
            bk = psT[q % 2]
            for kk in range(4):
                dt_ = 4 * q + kk
                C.op("pe", lambda e, bk=bk, kk=kk, dt_=dt_: e.transpose(banks[bk][:, kk * 128:(kk + 1) * 128], xt[:, dt_ * 128:(dt_ + 1) * 128], ident[:]),
                     r=(xres, cst), w=(bres[bk],))
            for kk in range(4):
                dt_ = 4 * q + kk
                C.op("act", lambda e, bk=bk, kk=kk, dt_=dt_: e.activation(out=hT[:, dt_, :], in_=banks[bk][:, kk * 128:(kk + 1) * 128], func=AF.Copy, scale=nwcol[:, dt_:dt_ + 1]),
                     r=(bres[bk], ld), w=(hres,))

    def lin_feat(wt, wres, bk, e_off, ne, nk=NDT, rhs=None, rres_=None):
        rhs = hT if rhs is None else rhs
        rres_ = hres if rres_ is None else rres_
        for e_ in range(ne):
            for dt_ in range(nk):
                C.op("pe", lambda e, e_=e_, dt_=dt_: e.matmul(banks[bk][:, (e_off + e_) * 128:(e_off + e_ + 1) * 128], lhsT=wt[:, dt_, e_ * 128:(e_ + 1) * 128],
                                                               rhs=rhs[:, dt_, :], start=(dt_ == 0), stop=(dt_ == nk - 1)),
                     r=(wres, rres_), w=(bres[bk],))

    def wsrc(wbf, c0, n=256):
        return wbf[:, c0:c0 + n].rearrange("(dt p) c -> p dt c", p=128)

    def glu_view(t):
        return t[:, 0:8, :]

    wk = [0]

    def feat_1024(ws, dsts, func=AF.Copy):
        for half in range(2):
            bk = psL[half]
            for b2 in range(2):
                wt, wres = ws.get(wk[0]); wk[0] += 1
                lin_feat(wt, wres, bk, 2 * b2, 2)
            for dst, dres, sc in dsts:
                if sc is None:
                    C.op("act", lambda e, bk=bk, half=half, dst=dst: e.activation(out=dst[:, 4 * half:4 * half + 4, :], in_=banks[bk][:, :].rearrange("p (a t) -> p a t", a=4), func=func),
                         r=(bres[bk],), w=(dres,))
                else:
                    C.op("act", lambda e, bk=bk, half=half, dst=dst, sc=sc: e.activation(out=dst[:, 4 * half:4 * half + 4, :], in_=banks[bk][:, :].rearrange("p (a t) -> p a t", a=4), func=func, scale=sc),
                         r=(bres[bk], ld), w=(dres,))

    bk_tog = [0]

    def bmm(ab, uT_, ures_, tc0, d, dslot, col0, mode, uT_b=None, ures_b=None):
        PB = 512 // UT
        for r in range(2):
            for pb in range(32 // PB):
                bk = psB[bk_tog[0] % 2]; bk_tog[0] += 1
                for ii in range(PB):
                    i = pb * PB + ii
                    if mode == "set":
                        C.op("pe", lambda e, bk=bk, ii=ii, i=i, r=r: e.matmul(banks[bk][:, ii * UT:(ii + 1) * UT], lhsT=Bp[:, i, d, r, :],
                                                                              rhs=uT_[:, i // 4, tc0:tc0 + UT], start=True, stop=True),
                             r=(sres, ures_), w=(bres[bk],))
                    else:
                        C.op("pe", lambda e, bk=bk, ii=ii, i=i, r=r: e.matmul(banks[bk][:, ii * UT:(ii + 1) * UT], lhsT=Bp[:, i, 0, r, :],
                                                                              rhs=uT_[:, i // 4, tc0:tc0 + UT], start=True, stop=False),
                             r=(sres, ures_), w=(bres[bk],))
                        C.op("pe", lambda e, bk=bk, ii=ii, i=i, r=r: e.matmul(banks[bk][:, ii * UT:(ii + 1) * UT], lhsT=Bp[:, i, 1, r, :],
                                                                              rhs=uT_b[:, i // 4, tc0:tc0 + UT], start=False, stop=True),
                             r=(sres, ures_b), w=(bres[bk],))
                dst = Abuf[ab][:, dslot, r, pb * PB:(pb + 1) * PB, col0:col0 + UT]
                src = banks[bk][:, :].rearrange("p (a t) -> p a t", a=PB)
                C.op("act", lambda e, dst=dst, src=src: e.activation(out=dst, in_=src, func=AF.Copy), r=(bres[bk],), w=(ares[ab],))

    rT1 = C.res("T1"); rT2a = C.res("T2a"); rT2b = C.res("T2b")

    def step_single(ab, dslot, cur, nxt, lr, ln, lp):
        base = dslot * DS
        c = a_ap(ab, base + cur, [[RS, 2], [IS, 32]]); n = a_ap(ab, base + nxt, [[RS, 2], [IS, 32]])
        c0 = a_ap(ab, base + cur, [[IS, 32]]); c1 = a_ap(ab, base + RS + cur, [[IS, 32]])
        t1 = T1[:, 0]; t2 = T2[:, 0]
        A_ = (ares[ab],)
        C.op("dve", lambda e: e.tensor_mul(out=t1, in0=c, in1=lr), r=A_, wo=(rT1,))
        C.op("dve", lambda e: e.tensor_mul(out=t2[:, 0, :], in0=c1, in1=ln), r=A_, wo=(rT2a,))
        C.op("dve", lambda e: e.tensor_mul(out=t2[:, 1, :], in0=c0, in1=lp), r=A_, wo=(rT2b,))
        C.op("dve", lambda e: e.tensor_add(out=n, in0=n, in1=t1), r=(rT1,), w=A_)
        C.op("dve", lambda e: e.tensor_add(out=n, in0=n, in1=t2), r=(rT2a, rT2b), w=A_)

    def step_both(ab, j):
        sc = DS + UT - 2 * j; sn_ = DS + UT - 2 - 2 * j
        c = a_ap(ab, j, [[sc, 2], [RS, 2], [IS, 32]]); n = a_ap(ab, j + 1, [[sn_, 2], [RS, 2], [IS, 32]])
        c0 = a_ap(ab, j, [[sc, 2], [IS, 32]]); c1 = a_ap(ab, j + RS, [[sc, 2], [IS, 32]])
        A_ = (ares[ab],)
        C.op("dve", lambda e: e.tensor_mul(out=T1[:], in0=c, in1=Lr2[:]), r=A_, wo=(rT1,))
        C.op("dve", lambda e: e.tensor_mul(out=T2[:, :, 0, :], in0=c1, in1=LiN[:]), r=A_, wo=(rT2a,))
        C.op("dve", lambda e: e.tensor_mul(out=T2[:, :, 1, :], in0=c0, in1=LiP[:]), r=A_, wo=(rT2b,))
        C.op("dve", lambda e: e.tensor_add(out=n, in0=n, in1=T1[:]), r=(rT1,), w=A_)
        C.op("dve", lambda e: e.tensor_add(out=n, in0=n, in1=T2[:]), r=(rT2a, rT2b), w=A_)

    for job in jobs:
        n_own, n_oth = job["n_own"], job["n_oth"]
        isS = job["kind"] == "S"
        JK = job["kind"] + str(job["tok0"])
        order = [("x", t) for t in range(n_oth)] + [("o", t) for t in range(n_own - 1, -1, -1)]
        xsrc = [(job["xx"] if kd == "x" else job["xo"])[t * 128:(t + 1) * 128, :] for kd, t in order]
        xs = Stream(C, "sp", xts, xsrc)
        ws = Stream(C, "sp", wbs, [wA[b].rearrange("p (dt c) -> p dt c", dt=NDT) for _ in order for b in range(4)], r_extra=(wc,))
        wk[0] = 0
        C.op("dve", lambda e: e.memset(Sx[:], 0.0), w=(eres,))
        seq = []
        for n, (kd, t) in enumerate(order):
            for u in (range(NU) if kd == "x" else range(NU - 1, -1, -1)):
                seq.append((n, kd, t, u))

        def pre1(n):
            xt, xres = xs.get(n)
            front_n(xt, xres)
            if order[n][0] == "x":
                feat_1024(ws, [(uTs[n % 2], ures[n % 2], flag[:, 0:1]), (uTx[n % 2], uxres[n % 2], flag[:, 1:2])])
            else:
                feat_1024(ws, [(uTs[n % 2], ures[n % 2], None)])

        def B1(k):
            n, kd, t, u = seq[k]
            ab = k % 2
            if kd == "x":
                bmm(ab, uTs[n % 2], ures[n % 2], u * UT, 0, 0, 1, "acc", uTx[n % 2], uxres[n % 2])
            else:
                bmm(ab, uTs[n % 2], ures[n % 2], u * UT, 1, 1, 0, "set")

        pre1(0)
        B1(0)
        bnd_init = False
        for k, (n, kd, t, u) in enumerate(seq):
            ab = k % 2
            if k + 1 < len(seq):
                B1(k + 1)
            if k % NU == 1 and n + 1 < len(order):
                pre1(n + 1)
            if kd == "x":
                C.op("dve", lambda e, ab=ab: e.tensor_copy(out=Abuf[ab][:, 0, :, :, 0], in_=Sx[:]), r=(eres,), w=(ares[ab],))
                for j in range(UT):
                    step_single(ab, 0, j, j + 1, Ls_r[:], Ls_n[:], Ls_p[:])
                C.op("dve", lambda e, ab=ab: e.tensor_copy(out=Sx[:], in_=Abuf[ab][:, 0, :, :, UT]), r=(ares[ab],), w=(eres,))
            else:
                U = NU * t + u
                if not bnd_init:
                    U_last = NU * n_own
                    if isS and n_oth > 0:
                        C.op("dve", lambda e, U_last=U_last: e.tensor_scalar_mul(out=Sx[:], in0=Sx[:], scalar1=flag[:, 1:2]), r=(ld,), w=(eres,))
                        C.op("dve", lambda e, U_last=U_last: e.tensor_copy(out=bnd[:, U_last, :, :], in_=Sx[:]), w=(eres,))
                    else:
                        C.op("dve", lambda e, U_last=U_last: e.memset(Sx[:], 0.0), w=(eres,))
                        C.op("dve", lambda e, U_last=U_last: e.tensor_copy(out=bnd[:, U_last, :, :], in_=Sx[:]), w=(eres,))
                    bnd_init = True
                C.op("dve", lambda e, ab=ab: e.tensor_copy(out=Abuf[ab][:, 1, :, :, UT], in_=Sx[:]), r=(eres,), w=(ares[ab],))
                for j in range(UT):
                    step_single(ab, 1, UT - j, UT - 1 - j, Lr2[:, 1], LiN[:, 1], LiP[:, 1])
                C.op("dve", lambda e, ab=ab: e.tensor_copy(out=Sx[:], in_=Abuf[ab][:, 1, :, :, 0]), r=(ares[ab],), w=(eres,))
                C.op("dve", lambda e, U=U: e.tensor_copy(out=bnd[:, U, :, :], in_=Sx[:]), w=(eres,))
            if kd == "x" and (k + 1 == len(seq) or seq[k + 1][1] != "x"):
                C.op("dve", lambda e: e.tensor_scalar_mul(out=Ef[:], in0=Sx[:], scalar1=flag[:, 0:1]), r=(ld,), w=(eres,))
        if not (isS and n_oth > 0):
            C.op("dve", lambda e: e.memset(Ef[:], 0.0), w=(eres,))
        xs = Stream(C, "sp", xts, [job["xo"][t * 128:(t + 1) * 128, :] for t in range(n_own)])
        blkU = [(wA[b].rearrange("p (dt c) -> p dt c", dt=NDT), None) for b in range(4)]
        blkG = [(wA[4 + b].rearrange("p (dt c) -> p dt c", dt=NDT), None) for b in range(4)]
        blkL = [(wG[b].rearrange("p (et c) -> p et c", et=8), glu_view) for b in range(4)]
        srcs = blkU + blkG
        for k in range(n_own * NU):
            t_, u_ = divmod(k, NU)
            if u_ == 0 and t_ >= 1:
                srcs = srcs + blkL
            if u_ == 2 and t_ + 1 < n_own:
                srcs = srcs + blkU + blkG
        srcs = srcs + blkL
        ws = Stream(C, "sp", wbs, srcs, r_extra=(wc,))
        wk[0] = 0

        def pre2(t):
            xt, xres = xs.get(t)
            front_n(xt, xres)
            feat_1024(ws, [(uTs[t % 2], ures[t % 2], None)])
            feat_1024(ws, [(gsTs[t % 2], gres[t % 2], None)], func=AF.Silu)

        def B2(k):
            t, u = divmod(k, NU)
            bmm(k % 2, uTs[t % 2], ures[t % 2], u * UT, 0, 0, 1, "set")
            bmm(k % 2, uTs[t % 2], ures[t % 2], u * UT, 1, 1, 0, "set")

        def post2a(t):
            uT = uTs[t % 2]
            dump("yT_%s%d" % (job["kind"], t), yT[:], yres)
            GW = (gwres,)
            PQ = "pool"
            C.op(PQ, lambda e: e.tensor_tensor(out=g1[:], in0=uT[:], in1=bcast(dcol[:], [128, 8, 128], 2), op=ALU.mult), r=(ures[t % 2], ld), w=GW)
            C.op(PQ, lambda e: e.tensor_add(out=yT[:], in0=yT[:], in1=g1[:]), r=GW, w=(yres,))
            C.op("act", lambda e: e.activation(out=g1[:], in_=yT[:], func=AF.Square), r=(yres,), w=GW)
            C.op(PQ, lambda e: e.tensor_scalar(out=g1[:], in0=g1[:], scalar1=0.044715, scalar2=1.0, op0=ALU.mult, op1=ALU.add), w=GW)
            C.op(PQ, lambda e: e.tensor_mul(out=g1[:], in0=g1[:], in1=yT[:]), r=(yres,), w=GW)
            C.op("act", lambda e: e.activation(out=g1[:], in_=g1[:], func=AF.Sigmoid, scale=1.5957691216057308), w=GW)
            C.op(PQ, lambda e: e.tensor_mul(out=g2[:], in0=g1[:], in1=yT[:]), r=(yres,), w=GW)
            C.op("act", lambda e: e.activation(out=gb[:], in_=g2[:], func=AF.Copy), w=GW)

        def post2b(t):
            GW = (gwres,)
            PQ = "pool"
            for half in range(2):
                bk = psL[half]
                for b2 in range(2):
                    wt, wres = ws.get(wk[0]); wk[0] += 1
                    lin_feat(wt, wres, bk, 2 * b2, 2, nk=8, rhs=gb, rres_=gwres)
                C.op("act", lambda e, bk=bk, half=half: e.activation(out=g1[:, 4 * half:4 * half + 4, :], in_=banks[bk][:, :].rearrange("p (a t) -> p a t", a=4), func=AF.Sigmoid),
                     r=(bres[bk],), w=GW)
            C.op(PQ, lambda e: e.tensor_mul(out=g2[:], in0=g2[:], in1=g1[:]), w=GW)
            C.op(PQ, lambda e: e.tensor_mul(out=g2[:], in0=g2[:], in1=gsTs[t % 2][:]), r=(gres[t % 2],), w=GW)
            C.op("act", lambda e: e.activation(out=g3[:], in_=g2[:], func=AF.Square), w=GW)
            for et in range(8):
                C.op("pe", lambda e, et=et: e.matmul(banks[psM][:, 128:256], lhsT=onesb[:], rhs=g3[:, et, :], start=(et == 0), stop=(et == 7)),
                     r=(gwres, cst), w=(bres[psM],))
            C.op("act", lambda e: e.activation(out=dg[:], in_=banks[psM][:, 128:256], func=AF.Copy, scale=1.0 / 1024), r=(bres[psM],), w=(rres,))
            C.op(PQ, lambda e: e.tensor_scalar_add(out=dg[:], in0=dg[:], scalar1=EPS), w=(rres,))
            C.op(PQ, lambda e: e.tensor_tensor(out=dg[:], in0=dg[:], in1=bcast(mhalf[:, 0], [128, 128], 1) if False else mhalf[:].broadcast_to([128, 128]), op=ALU.pow), r=(cst,), w=(rres,))
            C.op(PQ, lambda e: e.tensor_tensor(out=g2[:], in0=g2[:], in1=bcast(dg[:], [128, 8, 128], 1), op=ALU.mult), r=(rres,), w=GW)
            C.op(PQ, lambda e: e.tensor_tensor(out=msT[:], in0=g2[:], in1=bcast(snwcol[:], [128, 8, 128], 2), op=ALU.mult), r=(gwres, ld), w=(mres,))
            dump("msT_%s%d" % (job["kind"], t), msT[:], mres)
            tok = job["tok0"] + t * 128
            C.dma("sp", ms_d[:, :, tok:tok + 128].rearrange("e p t -> p e t"), msT[:], r=(mres,))

        def Cmm(k):
            t, u = divmod(k, NU)
            for gt in range(8):
                for q in range(4):
                    i = 4 * gt + q
                    tp = dict(tile_position=(0, 96)) if q == 3 else {}
                    kk = 0
                    for d in range(2):
                        for r in range(2):
                            C.op("pe", lambda e, gt=gt, i=i, q=q, d=d, r=r, kk=kk, tp=tp: e.matmul(
                                banks[psY][32 * q:32 * q + 32, gt * UT:(gt + 1) * UT], lhsT=Cp[:, i, d, r, 32 * q:32 * q + 32],
                                rhs=Xbf[:, d, r, i, :], start=(kk == 0), stop=(kk == 3), **tp),
                                 r=(sres, xbres), w=(bres[psY],))
                            kk += 1
            C.op("act", lambda e, u=u: e.activation(out=yT[:, :, u * UT:(u + 1) * UT], in_=banks[psY][:, 0:8 * UT].rearrange("p (a t) -> p a t", a=8), func=AF.Copy),
                 r=(bres[psY],), w=(yres,))
            if u == NU - 1:
                post2a(t)

        pre2(0)
        B2(0)
        nk2 = n_own * NU
        for k in range(nk2):
            t, u = divmod(k, NU)
            ab = k % 2
            if k + 1 < nk2:
                B2(k + 1)
            if k >= 1:
                Cmm(k - 1)
            if u == 0 and t >= 1:
                post2b(t - 1)
            if u == 2 and t + 1 < n_own:
                pre2(t + 1)
            C.op("dve", lambda e, ab=ab: e.tensor_copy(out=Abuf[ab][:, 0, :, :, 0], in_=Ef[:]), r=(eres,), w=(ares[ab],))
            C.op("dve", lambda e, ab=ab, k=k: e.tensor_copy(out=Abuf[ab][:, 1, :, :, UT], in_=bnd[:, k + 1, :, :]), r=(eres,), w=(ares[ab],))
            for j in range(UT):
                step_both(ab, j)
            C.op("dve", lambda e, ab=ab: e.tensor_copy(out=Ef[:], in_=Abuf[ab][:, 0, :, :, UT]), r=(ares[ab],), w=(eres,))
            C.op("pool", lambda e, ab=ab: e.tensor_copy(out=Xbf[:, 0], in_=Abuf[ab][:, 0, :, :, 1:UT + 1]), r=(ares[ab],), w=(xbres,))
            C.op("pool", lambda e, ab=ab: e.tensor_copy(out=Xbf[:, 1], in_=Abuf[ab][:, 1, :, :, 0:UT]), r=(ares[ab],), w=(xbres,))
        Cmm(nk2 - 1)
        post2b(n_own - 1)

    C.barrier()
    ssm_stack.close()

    KT = sb("KT", [128, 2, MAXS * 128], BF16)
    anw = sb("anw", [128, 1024])
    C.dma("sp", anw[:], dram_ap(attn_out_norm_w, 0, [[0, 128], [1, 1024]]), w=(ld,))
    V1 = sb("V1", [128, MAXS, 2, 130], BF16)
    wb2 = [(sb("wc%d" % i, [128, NDT, 512], BF16), C.res("wc%d" % i)) for i in range(2)]
    QT = sb("QT", [128, 2, 512], BF16); qtres = C.res("QT")
    qf = sb("qf", [128, 4, 128]); qn = sb("qn", [128, 4, 128]); qt2 = sb("qt2", [128, 4, 128]); qr = sb("qr", [128, 4, 128], BF16)
    qres = C.res("qwork")
    jk = sb("jk", [128, 128], BF16)
    ga = sb("ga", [128, 1024], BF16); gares = C.res("ga")
    oa = sb("oa", [128, 1024]); oares = C.res("oa")
    Mb = sb("Mb", [128, 1024], BF16)
    MT = sb("MT", [128, NDT, 128], BF16); mtres = C.res("MT"); mt2res = C.res("MT2")
    PTs = [(sb("PT%d" % i, [128, 512], BF16), C.res("PT%d" % i)) for i in range(3)]
    youts = [(sb("yo%d" % i, [128, D]), C.res("yo%d" % i)) for i in range(2)]
    cs = sb("cs", [128, 128]); sn = sb("sn", [128, 128]); rpres = C.res("rope")
    tq = sb("tq", [128, 64]); tf = sb("tf", [128, 64]); ti = sb("ti", [128, 64], I32); pc = sb("pc", [128, 2])
    kvres = C.res("kv")
    psS, psO = (4, 5), (6, 7)
    C.op("pool", lambda e: e.memset(V1[:], 1.0), w=(kvres,))
    SCALE = 128.0 ** -0.5

    def make_rope(kind, t):
        W = (rpres,)
        R = (cst, ld)
        if kind == "x":
            C.op("dve", lambda e: e.tensor_scalar_add(out=pc[:, 0:1], in0=hicol[:], scalar1=float(2 * t)), r=R, w=W)
            C.op("dve", lambda e: e.tensor_mul(out=pc[:, 0:1], in0=pc[:, 0:1], in1=flag[:, 2:3]), r=R, w=W)
            C.op("dve", lambda e: e.scalar_tensor_tensor(out=pc[:, 0:1], in0=flag[:, 1:2], scalar=float((LSO + LSX) // 64 - 1), in1=pc[:, 0:1], op0=ALU.mult, op1=ALU.add), r=R, w=W)
            C.op("dve", lambda e: e.tensor_mul(out=pc[:, 1:2], in0=colpos[:], in1=flag[:, 2:3]), r=R, w=W)
            C.op("dve", lambda e: e.scalar_tensor_tensor(out=pc[:, 1:2], in0=flag[:, 1:2], scalar=63.0, in1=pc[:, 1:2], op0=ALU.mult, op1=ALU.add), r=R, w=W)
        else:
            C.op("dve", lambda e: e.tensor_scalar_add(out=pc[:, 0:1], in0=hicol[:], scalar1=float(2 * t)), r=R, w=W)
            if kind == "s":
                C.op("dve", lambda e: e.scalar_tensor_tensor(out=pc[:, 0:1], in0=flag[:, 0:1], scalar=float(2 * (LSO // 128)), in1=pc[:, 0:1], op0=ALU.mult, op1=ALU.add), r=R, w=W)
            C.op("dve", lambda e: e.tensor_copy(out=pc[:, 1:2], in_=colpos[:]), r=R, w=W)
        for shift, which in ((0.0, "sin"), (0.25, "cos")):
            C.op("dve", lambda e: e.tensor_scalar(out=tq[:, 0:32], in0=fq[:], scalar1=pc[:, 0:1], scalar2=shift, op0=ALU.mult, op1=ALU.add), r=R, w=W)
            C.op("dve", lambda e: e.tensor_scalar(out=tq[:, 32:64], in0=fq[:], scalar1=pc[:, 1:2], scalar2=shift, op0=ALU.mult, op1=ALU.add), r=R, w=W)
            C.op("dve", lambda e: e.tensor_copy(out=ti[:], in_=tq[:]), w=W)
            C.op("dve", lambda e: e.tensor_copy(out=tf[:], in_=ti[:]), w=W)
            C.op("dve", lambda e: e.tensor_sub(out=tq[:], in0=tq[:], in1=tf[:]), w=W)
            C.op("act", lambda e: e.activation(out=tf[:], in_=tq[:], func=AF.Sin, scale=TWO_PI_S), w=W)
            tf3 = tf[:].rearrange("p (a j) -> p a j", a=2)
            if which == "cos":
                c5 = cs[:].rearrange("p (a x j) -> p a x j", a=2, x=2)
                for x_ in range(2):
                    C.op("dve", lambda e, x_=x_: e.tensor_copy(out=c5[:, :, x_, :], in_=tf3), w=W)
            else:
                s5 = sn[:].rearrange("p (a x j) -> p a x j", a=2, x=2)
                C.op("dve", lambda e: e.tensor_scalar_mul(out=s5[:, :, 0, :], in0=tf3, scalar1=-1.0), w=W)
                C.op("dve", lambda e: e.tensor_copy(out=s5[:, :, 1, :], in_=tf3), w=W)

    def qk_norm_rope(bk, nh, wrow):
        W = (qres,)
        C.op("act", lambda e: e.activation(out=qf[:, 0:nh, :], in_=banks[bk][:, 0:nh * 128].rearrange("p (h d) -> p h d", h=nh), func=AF.Copy, scale=stat[:, 0:1]),
             r=(bres[bk], stres), w=W)
        for h in range(nh):
            C.op("act", lambda e, h=h: e.activation(out=jk[:], in_=qf[:, h, :], func=AF.Square, accum_out=stat[:, 1 + h:2 + h]), w=W + (stres,))
        C.op("dve", lambda e: e.tensor_scalar(out=stat[:, 1:1 + nh], in0=stat[:, 1:1 + nh], scalar1=1.0 / 128, scalar2=EPS, op0=ALU.mult, op1=ALU.add), w=W + (stres,))
        C.op("act", lambda e: e.activation(out=stat[:, 1:1 + nh], in_=stat[:, 1:1 + nh], func=AF.Sqrt), w=W + (stres,))
        C.op("dve", lambda e: e.reciprocal(out=stat[:, 1:1 + nh], in_=stat[:, 1:1 + nh]), w=W + (stres,))
        for h in range(nh):
            C.op("dve", lambda e, h=h: e.scalar_tensor_tensor(out=qn[:, h, :], in0=qf[:, h, :], scalar=stat[:, 1 + h:2 + h], in1=wrow[:], op0=ALU.mult, op1=ALU.mult),
                 r=(ld,), w=W)
        qn5 = qn[:, 0:nh, :].rearrange("p h (a x j) -> p h a x j", a=2, x=2)
        t25 = qt2[:, 0:nh, :].rearrange("p h (a x j) -> p h a x j", a=2, x=2)
        s5 = sn[:].rearrange("p (a x j) -> p a x j", a=2, x=2)
        C.op("dve", lambda e: e.tensor_tensor(out=t25[:, :, :, 0, :], in0=qn5[:, :, :, 1, :], in1=bcast(s5[:, :, 0, :], [128, nh, 2, 32], 1), op=ALU.mult), r=(rpres,), w=W)
        C.op("dve", lambda e: e.tensor_tensor(out=t25[:, :, :, 1, :], in0=qn5[:, :, :, 0, :], in1=bcast(s5[:, :, 1, :], [128, nh, 2, 32], 1), op=ALU.mult), r=(rpres,), w=W)
        C.op("dve", lambda e: e.tensor_tensor(out=qn[:, 0:nh, :], in0=qn[:, 0:nh, :], in1=bcast(cs[:], [128, nh, 128], 1), op=ALU.mult), r=(rpres,), w=W)
        dump("qf_%d" % nh, qf[:], qres); dump("qnc_%d" % nh, qn[:], qres); dump("qt2_%d" % nh, qt2[:], qres)
        dump("cs_%d" % nh, cs[:], rpres); dump("sn_%d" % nh, sn[:], rpres); dump("pc_%d" % nh, pc[:], rpres)
        C.op("dve", lambda e: e.tensor_add(out=qr[:, 0:nh, :], in0=qn[:, 0:nh, :], in1=qt2[:, 0:nh, :]), w=W)
        dump("qr_%d" % nh, qr[:], qres)

    def lin_tok(wt, wres, bk, lhs, lres, nk=NDT):
        for dt_ in range(nk):
            C.op("pe", lambda e, dt_=dt_: e.matmul(banks[bk][:, :], lhsT=lhs[:, dt_, :], rhs=wt[:, dt_, :], start=(dt_ == 0), stop=(dt_ == nk - 1)),
                 r=(wres, lres), w=(bres[bk],))

    def wsrc2(wbf, c0):
        return wbf[:, c0:c0 + 512].rearrange("(dt p) c -> p dt c", p=128)

    for job in jobs:
        n_own, n_oth = job["n_own"], job["n_oth"]
        isS = job["kind"] == "S"
        nst = n_own + n_oth
        order = [("o", t) for t in range(n_own)] + [("x", t) for t in range(n_oth)]
        xsrc = [(job["xx"] if kd == "x" else job["xo"])[t * 128:(t + 1) * 128, :] for kd, t in order]
        xs = Stream(C, "sp", xts, xsrc)
        ws = Stream(C, "sp", wb2, [wB[2].rearrange("p (dt c) -> p dt c", dt=NDT) for _ in order], r_extra=(wc,))
        for n, (kd, t) in enumerate(order):
            xt, xres = xs.get(n)
            front(xt, xres, False)
            make_rope("x" if kd == "x" else ("s" if isS else "p"), t)
            wt, wres = ws.get(n)
            bk = psL[n % 2]
            lin_tok(wt, wres, bk, hT, hres)
            qk_norm_rope(bk, 2, knw)
            C.op("act", lambda e, bk=bk, n=n: e.activation(out=V1[:, n, :, 0:128], in_=banks[bk][:, 256:512].rearrange("p (h d) -> p h d", h=2), func=AF.Copy, scale=stat[:, 0:1]),
                 r=(bres[bk], stres), w=(kvres,))
            tb = psT[n % 2]
            tbv = banks[tb][:, 0:128].bitcast(BF16)
            for h in range(2):
                C.op("pe", lambda e, h=h, tbv=tbv: e.transpose(tbv[:, h * 128:(h + 1) * 128], qr[:, h, :], identb[:]), r=(qres, cst), w=(bres[tb],))
            C.op("dve", lambda e, tbv=tbv, n=n: e.tensor_copy(out=KT[:, :, n * 128:(n + 1) * 128], in_=tbv.rearrange("p (h t) -> p h t", h=2)), r=(bres[tb],), w=(kvres,))
        xs = Stream(C, "sp", xts, [job["xo"][t * 128:(t + 1) * 128, :] for t in range(n_own)])
        srcs = []
        for _ in range(n_own):
            srcs += [wB[b].rearrange("p (dt c) -> p dt c", dt=NDT) for b in (0, 1, 3, 4)]
            srcs += [wO[b].rearrange("p (dt c) -> p dt c", dt=NDT) for b in range(4)]
        ws = Stream(C, "sp", wb2, srcs, r_extra=(wc,))
        pt_i = 0
        for t in range(n_own):
            xt, xres = xs.get(t)
            front(xt, xres, False)
            make_rope("s" if isS else "p", t)
            for qb in range(2):
                wt, wres = ws.get(8 * t + qb)
                bk = psL[qb]
                lin_tok(wt, wres, bk, hT, hres)
                qk_norm_rope(bk, 4, qnw)
                tb = psT[qb]
                tbv = banks[tb][:, 0:256].bitcast(BF16)
                for h in range(4):
                    C.op("pe", lambda e, h=h, tbv=tbv: e.transpose(tbv[:, h * 128:(h + 1) * 128], qr[:, h, :], identb[:]), r=(qres, cst), w=(bres[tb],))
                C.op("dve", lambda e, tbv=tbv, qb=qb: e.tensor_copy(out=QT[:, qb, :], in_=tbv), r=(bres[tb],), w=(qtres,))
            for g_ in range(2):
                wt, wres = ws.get(8 * t + 2 + g_)
                bk = psL[g_]
                lin_tok(wt, wres, bk, hT, hres)
                C.op("act", lambda e, bk=bk, g_=g_: e.activation(out=ga[:, g_ * 512:(g_ + 1) * 512], in_=banks[bk][:, :], func=AF.Silu, scale=stat[:, 0:1]),
                     r=(bres[bk], stres), w=(gares,))
            for kvh in range(2):
                for st in range(nst):
                    sbk = psS[st % 2]
                    C.op("pe", lambda e, sbk=sbk, st=st, kvh=kvh: e.matmul(banks[sbk][:, :], lhsT=KT[:, kvh, st * 128:(st + 1) * 128], rhs=QT[:, kvh, :], start=True, stop=True),
                         r=(kvres, qtres), w=(bres[sbk],))
                    pt, ptres = PTs[pt_i % 3]; pt_i += 1
                    C.op("act", lambda e, sbk=sbk, pt=pt: e.activation(out=pt[:], in_=banks[sbk][:, :], func=AF.Exp, scale=SCALE), r=(bres[sbk],), w=(ptres,))
                    for h in range(4):
                        ob = psO[h // 2]; off = (h % 2) * 129
                        C.op("pe", lambda e, ob=ob, off=off, h=h, pt=pt, st=st, kvh=kvh: e.matmul(
                            banks[ob][:, off:off + 129], lhsT=pt[:, h * 128:(h + 1) * 128], rhs=V1[:, st, kvh, 0:129],
                            start=(st == 0 and h % 2 == 0), stop=(st == nst - 1), skip_group_check=True), r=(ptres, kvres), w=(bres[ob],))
                for h in range(4):
                    ob = psO[h // 2]; off = (h % 2) * 129; head = 4 * kvh + h
                    C.op("dve", lambda e, ob=ob, off=off, h=h: e.reciprocal(out=stat[:, 5 + (h % 2):6 + (h % 2)], in_=banks[ob][:, off + 128:off + 129]), r=(bres[ob],), w=(stres,))
                    C.op("dve", lambda e, ob=ob, off=off, h=h, head=head: e.scalar_tensor_tensor(
                        out=oa[:, head * 128:(head + 1) * 128], in0=banks[ob][:, off:off + 128], scalar=stat[:, 5 + (h % 2):6 + (h % 2)],
                        in1=ga[:, head * 128:(head + 1) * 128], op0=ALU.mult, op1=ALU.mult), r=(bres[ob], gares, stres), w=(oares,))
            dump("KT_%s" % job["kind"], KT[:], kvres); dump("V1_%s" % job["kind"], V1[:], kvres); dump("QT_%s%d" % (job["kind"], t), QT[:], qtres)
            dump("ga_%s%d" % (job["kind"], t), ga[:], gares); dump("oa_%s%d" % (job["kind"], t), oa[:], oares)
            C.op("act", lambda e: e.activation(out=Mb[:], in_=oa[:], func=AF.Square, accum_out=stat[:, 7:8]), r=(oares,), w=(mtres, stres))
            C.op("dve", lambda e: e.tensor_scalar(out=stat[:, 7:8], in0=stat[:, 7:8], scalar1=1.0 / 1024, scalar2=EPS, op0=ALU.mult, op1=ALU.add), w=(stres,))
            C.op("act", lambda e: e.activation(out=stat[:, 7:8], in_=stat[:, 7:8], func=AF.Sqrt), w=(stres,))
            C.op("dve", lambda e: e.reciprocal(out=stat[:, 7:8], in_=stat[:, 7:8]), w=(stres,))
            C.op("dve", lambda e: e.scalar_tensor_tensor(out=Mb[:], in0=oa[:], scalar=stat[:, 7:8], in1=anw[:], op0=ALU.mult, op1=ALU.mult), r=(oares, ld), w=(mtres,))
            for q in range(2):
                tb = psT[q]
                tbv = banks[tb][:, 0:256].bitcast(BF16)
                for kk in range(4):
                    et = 4 * q + kk
                    C.op("pe", lambda e, tbv=tbv, kk=kk, et=et: e.transpose(tbv[:, kk * 128:(kk + 1) * 128], Mb[:, et * 128:(et + 1) * 128], identb[:]), r=(mtres, cst), w=(bres[tb],))
                C.op("dve", lambda e, tbv=tbv, q=q: e.tensor_copy(out=MT[:, 4 * q:4 * q + 4, :], in_=tbv.rearrange("p (a t) -> p a t", a=4)), r=(bres[tb],), w=(mtres,))
            tok = job["tok0"] + t * 128
            C.dma("sp", MT[:, 8:16, :], ms_d[:, :, tok:tok + 128].rearrange("e p t -> p e t"), w=(mt2res,))
            dump("MT_%s%d" % (job["kind"], t), MT[:], mt2res)
            yo, yres_ = youts[t % 2]
            for cb in range(4):
                wt, wres = ws.get(8 * t + 4 + cb)
                bk = psL[cb % 2]
                for dt_ in range(NDT):
                    C.op("pe", lambda e, dt_=dt_, bk=bk, wt=wt: e.matmul(banks[bk][:, :], lhsT=MT[:, dt_, :], rhs=wt[:, dt_, :], start=(dt_ == 0), stop=(dt_ == NDT - 1)),
                         r=(wres, mtres, mt2res), w=(bres[bk],))
                C.op("dve", lambda e, bk=bk, cb=cb, yo=yo, xt=xt: e.tensor_tensor(out=yo[:, cb * 512:(cb + 1) * 512], in0=banks[bk][:, :], in1=xt[:, cb * 512:(cb + 1) * 512], op=ALU.add),
                     r=(bres[bk], xres), w=(yres_,))
            C.dma("sp", y_all[tok:tok + 128, :], yo[:], r=(yres_,))
    C.barrier()
    return nc


_CACHE = {}


def _get_nc(cfg_key, cfg):
    if cfg_key not in _CACHE:
        _CACHE[cfg_key] = build(cfg)
    return _CACHE[cfg_key]


def run_layer(x_prompt, x_sample, weights, n_cores=8, debug=False):
    B, LP, _ = x_prompt.shape
    BS, LS, _ = x_sample.shape
    NP = B // n_cores
    half = LS // 2
    cfg = dict(np=NP, lp=LP, lso=half, lsx=half, debug=debug)
    nc = build(cfg)
    in_maps = []
    for c in range(n_cores):
        seq, h = c // 2, c % 2
        own = x_sample[seq, h * half:(h + 1) * half]
        oth = x_sample[seq, (1 - h) * half:(2 - h) * half]
        if h == 0:
            oth = oth[::-1]
        m = {"x_p": np.ascontiguousarray(x_prompt[c * NP:(c + 1) * NP].reshape(NP * LP, D)),
             "x_so": np.ascontiguousarray(own), "x_sx": np.ascontiguousarray(oth),
             "flag": np.tile(np.array([[h, 1 - h, 2 * h - 1, 0]], np.float32), (128, 1))}
        for k, v in weights.items():
            m[k] = np.ascontiguousarray(v, dtype=np.float32)
        in_maps.append(m)
    res = run_bass_kernel_spmd(nc, in_maps, core_ids=list(range(n_cores)))
    y_p = np.empty_like(x_prompt)
    y_s = np.empty_like(x_sample)
    for c in range(n_cores):
        seq, h = c // 2, c % 2
        ya = res.results[c]["y_all"]
        y_p[c * NP:(c + 1) * NP] = ya[:NP * LP].reshape(NP, LP, D)
        y_s[seq, h * half:(h + 1) * half] = ya[NP * LP:]
    if debug:
        return y_p, y_s, res.results
    return y_p, y_s


def kernel(x_prompt, x_sample, **weights):
    x_prompt = np.asarray(x_prompt, dtype=np.float32)
    x_sample = np.asarray(x_sample, dtype=np.float32)
    weights = {k: np.asarray(v, dtype=np.float32) for k, v in weights.items()}
    return run_layer(x_prompt, x_sample, weights, n_cores=8)
```

```python
import math
import numpy as np
import concourse.bass as bass
import concourse.mybir as mybir
from concourse.bass_utils import run_bass_kernel_spmd

F32 = mybir.dt.float32
BF16 = mybir.dt.bfloat16
I32 = mybir.dt.int32
AF = mybir.ActivationFunctionType
ALU = mybir.AluOpType

D = 2048
NDT = 16
IN_COLS = 4608
EPS = 1e-6
TWO_PI_S = 6.283185
TWO_PI = 2.0 * math.pi
C_Q, C_K, C_V, C_GA, C_U, C_GS = 0, 1024, 1280, 1536, 2560, 3584
AFREE = 2 * 2 * 32 * 65


class Res:
    __slots__ = ("name", "w", "rd", "sem", "cnt")

    def __init__(self, name):
        self.name = name
        self.w = {}
        self.rd = {}
        self.sem = None
        self.cnt = 0


class Ctx:
    def __init__(self, nc):
        self.nc = nc
        self.eng = {"pe": nc.tensor, "act": nc.scalar, "dve": nc.vector, "pool": nc.gpsimd, "sp": nc.sync}
        self.sem = {k: nc.alloc_semaphore("sem_" + k) for k in self.eng}
        self.cnt = {k: 0 for k in self.eng}
        self.waited = {k: {} for k in self.eng}
        self.nres = 0
        self.allres = []

    def res(self, name=None):
        self.nres += 1
        r = Res(name or "r%d" % self.nres)
        self.allres.append(r)
        return r

    def _wait(self, eng, tok, same_ok=True):
        sem, val = tok
        if sem is self.sem[eng] and not same_ok:
            return
        k = id(sem)
        if self.waited[eng].get(k, 0) >= val:
            return
        self.eng[eng].wait_ge(sem, val)
        self.waited[eng][k] = val

    def _deps(self, eng, r, w, wo=()):
        for x in r:
            for t in x.w.values():
                self._wait(eng, t)
        for x in w:
            for t in x.w.values():
                self._wait(eng, t)
            for t in x.rd.values():
                self._wait(eng, t, same_ok=False)
        for x in wo:
            for t in x.w.values():
                self._wait(eng, t, same_ok=False)
            for t in x.rd.values():
                self._wait(eng, t, same_ok=False)

    def _mark(self, tok, r, w):
        k = id(tok[0])
        for x in r:
            if x in w:
                continue
            x.rd[k] = tok
        for x in w:
            x.w = {k: tok}
            x.rd = {}

    def op(self, eng, fn, r=(), w=(), wo=()):
        self._deps(eng, r, w, wo)
        ins = fn(self.eng[eng])
        self.cnt[eng] += 1
        ins.then_inc(self.sem[eng], 1)
        self._mark((self.sem[eng], self.cnt[eng]), r, tuple(w) + tuple(wo))

    def dma(self, q, out, in_, r=(), w=(), **kw):
        self._deps(q, r, w)
        own = w[0] if w else r[0]
        if own.sem is None:
            own.sem = self.nc.alloc_semaphore("dsem_" + own.name)
        ins = self.eng[q].dma_start(out=out, in_=in_, **kw)
        own.cnt += 16
        ins.then_inc(own.sem, 16)
        self._mark((own.sem, own.cnt), r, w)

    def barrier(self):
        toks = [(self.sem[k], self.cnt[k]) for k in self.eng if self.cnt[k] > 0]
        for x in self.allres:
            if x.sem is not None and x.cnt > 0:
                toks.append((x.sem, x.cnt))
        for e in self.eng:
            for t in toks:
                self._wait(e, t)


class Stream:
    def __init__(self, C, q, slots, srcs, r_extra=(), **kw):
        self.C, self.q, self.slots, self.srcs, self.r_extra, self.kw = C, q, slots, srcs, tuple(r_extra), kw
        self.issued = 0

    def get(self, k):
        n = len(self.slots)
        while self.issued < min(len(self.srcs), k + n):
            j = self.issued
            t, res = self.slots[j % n]
            src = self.srcs[j]
            view = None
            if isinstance(src, tuple):
                src, view = src
            dst = view(t) if view else t[:]
            for x in self.r_extra:
                for tok in x.w.values():
                    self.C._wait(self.q, tok)
            self.C.dma(self.q, dst, src, w=(res,), **self.kw)
            self.issued += 1
        return self.slots[k % n]


def dram_ap(t, offset, pat):
    return bass.AP(t.tensor, offset, [list(p) for p in pat])


def build(cfg):
    NP, LP, LSO, LSX = cfg["np"], cfg["lp"], cfg["lso"], cfg["lsx"]
    nc = bass.Bass("TRN2", target_bir_lowering=False)
    C = Ctx(nc)
    din = lambda name, shape: nc.dram_tensor(name, list(shape), F32, kind="ExternalInput").ap()
    x_p = din("x_p", [NP * LP, D])
    x_so = din("x_so", [LSO, D])
    x_sx = din("x_sx", [max(LSX, 128), D])
    flag_d = din("flag", [128, 4])
    norm_w = din("norm_w", [1, D])
    w_in = din("w_in", [1, D, IN_COLS])
    q_norm_w = din("q_norm_w", [1, 128])
    k_norm_w = din("k_norm_w", [1, 128])
    lambda_re = din("lambda_re", [1, 2, 64, 64])
    lambda_im = din("lambda_im", [1, 2, 64, 64])
    log_step = din("log_step", [1, 2, 64])
    b_re = din("b_re", [1, 64, 64, 16])
    b_im = din("b_im", [1, 64, 64, 16])
    c_re = din("c_re", [1, 2, 64, 16, 64])
    c_im = din("c_im", [1, 2, 64, 16, 64])
    d_skip = din("d_skip", [1, 1024])
    w_glu = din("w_glu", [1, 1024, 1024])
    attn_out_norm_w = din("attn_out_norm_w", [1, 1024])
    ssm_out_norm_w = din("ssm_out_norm_w", [1, 1024])
    w_out = din("w_out", [1, D, D])
    NTOK = NP * LP + LSO
    y_all = nc.dram_tensor("y_all", [NTOK, D], F32, kind="ExternalOutput").ap()
    wA = nc.dram_tensor("wA", [8, 128, NDT * 256], BF16, kind="Internal").ap()
    wB = nc.dram_tensor("wB", [5, 128, NDT * 512], BF16, kind="Internal").ap()
    wO = nc.dram_tensor("wO", [4, 128, NDT * 512], BF16, kind="Internal").ap()
    wG = nc.dram_tensor("wG", [4, 128, 8 * 256], BF16, kind="Internal").ap()
    ms_d = nc.dram_tensor("ms_scr", [8, 128, NTOK], BF16, kind="Internal").ap()

    jobs = []
    for j in range(NP):
        jobs.append(dict(kind="P", xo=x_p[j * LP:(j + 1) * LP, :], xx=None, n_own=LP // 128, n_oth=0, tok0=j * LP))
    jobs.append(dict(kind="S", xo=x_so, xx=x_sx, n_own=LSO // 128, n_oth=LSX // 128, tok0=NP * LP))
    MAXU = max(j["n_own"] for j in jobs) * 2
    MAXS = max(j["n_own"] + j["n_oth"] for j in jobs)

    sb = lambda name, shape, dt=F32: nc.alloc_sbuf_tensor(name, list(shape), dt)
    dbg = cfg.get("debug", False)
    dumped = {}

    def dump(name, ap, res):
        if not dbg or name in dumped:
            return
        shp = list(ap.shape)
        t_ = nc.dram_tensor("dbg_" + name, shp, ap.dtype, kind="ExternalOutput").ap()
        dumped[name] = t_
        C.dma("sp", t_, ap, r=(res,))
    banks = [nc.alloc_psum_tensor("bank%d" % i, [128, 512], F32) for i in range(8)]
    bres = [C.res("bank%d" % i) for i in range(8)]

    def bcast(ap, shape, axis):
        return ap.unsqueeze(axis).broadcast_to(list(shape))

    wc = C.res("wcast")
    for b in range(8):
        c0 = (C_U + 256 * b) if b < 4 else (C_GS + 256 * (b - 4))
        C.dma("pool", wA[b].rearrange("p (dt c) -> p dt c", dt=NDT), w_in[0, :, c0:c0 + 256].rearrange("(dt p) c -> p dt c", p=128), w=(wc,))
    for b, c0 in enumerate((C_Q, C_Q + 512, C_K, C_GA, C_GA + 512)):
        C.dma("pool", wB[b].rearrange("p (dt c) -> p dt c", dt=NDT), w_in[0, :, c0:c0 + 512].rearrange("(dt p) c -> p dt c", p=128), w=(wc,))
    for b in range(4):
        C.dma("pool", wO[b].rearrange("p (dt c) -> p dt c", dt=NDT), w_out[0, :, 512 * b:512 * b + 512].rearrange("(dt p) c -> p dt c", p=128), w=(wc,))
        C.dma("pool", wG[b].rearrange("p (et c) -> p et c", et=8), w_glu[0, :, 256 * b:256 * b + 256].rearrange("(et p) c -> p et c", p=128), w=(wc,))

    cst = C.res("const")
    ident = sb("ident", [128, 128]); identb = sb("identb", [128, 128], BF16)
    ones_t = sb("ones_t", [128, 128]); onesb = sb("onesb", [128, 128], BF16)
    C.op("pool", lambda e: e.memset(ones_t[:], 1.0), w=(cst,))
    C.op("pool", lambda e: e.memset(onesb[:], 1.0), w=(cst,))
    C.op("pool", lambda e: e.affine_select(out=ident[:], in_=ones_t[:], pattern=[[-1, 128]], compare_op=ALU.is_equal,
                                           fill=0.0, base=0, channel_multiplier=1), w=(cst,))
    C.op("dve", lambda e: e.tensor_copy(out=identb[:], in_=ident[:]), r=(), w=(cst,))
    flag = sb("flag_sb", [128, 4]); nwcol = sb("nwcol", [128, NDT])
    qnw = sb("qnw", [128, 128]); knw = sb("knw", [128, 128])
    dcol = sb("dcol", [128, 8]); snwcol = sb("snwcol", [128, 8])
    mhalf = sb("mhalf", [128, 1])
    C.op("pool", lambda e: e.memset(mhalf[:], -0.5), w=(cst,))
    ld = C.res("cload")
    C.dma("sp", flag[:], flag_d[:, :], w=(ld,))
    C.dma("sp", nwcol[:], dram_ap(norm_w, 0, [[1, 128], [128, NDT]]), w=(ld,), allow_slow_non_contiguous=True)
    C.dma("sp", qnw[:], dram_ap(q_norm_w, 0, [[0, 128], [1, 128]]), w=(ld,))
    C.dma("sp", knw[:], dram_ap(k_norm_w, 0, [[0, 128], [1, 128]]), w=(ld,))
    C.dma("sp", dcol[:], dram_ap(d_skip, 0, [[1, 128], [128, 8]]), w=(ld,), allow_slow_non_contiguous=True)
    C.dma("sp", snwcol[:], dram_ap(ssm_out_norm_w, 0, [[1, 128], [128, 8]]), w=(ld,), allow_slow_non_contiguous=True)

    fq = sb("fq", [128, 32]); jt = sb("jt", [128, 32]); pidx = sb("pidx", [128, 1])
    hicol = sb("hicol", [128, 1]); colpos = sb("colpos", [128, 1])
    C.op("pool", lambda e: e.iota(jt[:], pattern=[[1, 32]], base=0, channel_multiplier=0, allow_small_or_imprecise_dtypes=True), w=(cst,))
    C.op("pool", lambda e: e.iota(pidx[:], pattern=[[0, 1]], base=0, channel_multiplier=1, allow_small_or_imprecise_dtypes=True), w=(cst,))
    C.op("act", lambda e: e.activation(out=fq[:], in_=jt[:], func=AF.Exp, scale=-math.log(10000.0) / 32.0), w=(cst,))
    C.op("dve", lambda e: e.tensor_scalar_mul(out=fq[:], in0=fq[:], scalar1=1.0 / TWO_PI), w=(cst,))
    C.op("dve", lambda e: e.tensor_single_scalar(out=hicol[:], in_=pidx[:], scalar=64.0, op=ALU.is_ge), w=(cst,))
    C.op("dve", lambda e: e.scalar_tensor_tensor(out=colpos[:], in0=hicol[:], scalar=-64.0, in1=pidx[:], op0=ALU.mult, op1=ALU.add), w=(cst,))

    def cyc_sin(eng_w, dst, q, tf, ti, shift=0.0):
        if shift != 0.0:
            C.op("dve", lambda e: e.tensor_scalar_add(out=q, in0=q, scalar1=shift), w=eng_w)
        C.op("dve", lambda e: e.tensor_copy(out=ti, in_=q), w=eng_w)
        C.op("dve", lambda e: e.tensor_copy(out=tf, in_=ti), w=eng_w)
        C.op("dve", lambda e: e.tensor_sub(out=q, in0=q, in1=tf), w=eng_w)
        C.op("act", lambda e: e.activation(out=dst, in_=q, func=AF.Sin, scale=TWO_PI_S), w=eng_w)

    xts = [(sb("xt%d" % i, [128, D]), C.res("xt%d" % i)) for i in range(2)]
    hT = sb("hT", [128, NDT, 128], BF16); hres = C.res("hT")
    stat = sb("stat", [128, 8]); stres = C.res("stat")
    dg = sb("dg", [128, 128]); rrep = sb("rrep", [128, 128]); rres = C.res("rrep")
    import contextlib
    ssm_stack = contextlib.ExitStack()
    ssb = lambda name, shape, dt=F32: ssm_stack.enter_context(nc.sbuf_tensor(name, list(shape), dt))
    Bp = ssb("Bp", [128, 32, 2, 2, 128], BF16)
    Cp = ssb("Cp", [128, 32, 2, 2, 128], BF16)
    Lr2 = ssb("Lr2", [128, 2, 2, 32]); LiN = ssb("LiN", [128, 2, 32]); LiP = ssb("LiP", [128, 2, 32])
    Ls_r = ssb("Ls_r", [128, 2, 32]); Ls_n = ssb("Ls_n", [128, 32]); Ls_p = ssb("Ls_p", [128, 32])
    LiS = ssb("LiS", [128, 2, 2, 32]); Ls_s = ssb("Ls_s", [128, 2, 32])
    sres = C.res("ssmw")
    with contextlib.ExitStack() as st_:
        lre = st_.enter_context(nc.sbuf_tensor("lre", [128, 32, 2], F32))
        lim = st_.enter_context(nc.sbuf_tensor("lim", [128, 32, 2], F32))
        lst = st_.enter_context(nc.sbuf_tensor("lst", [128, 32, 2], F32))
        Br = st_.enter_context(nc.sbuf_tensor("Br", [128, 32, 16], F32))
        Bi = st_.enter_context(nc.sbuf_tensor("Bi", [128, 32, 16], F32))
        Cn = st_.enter_context(nc.sbuf_tensor("Cn", [128, 2, 2, 8, 2, 64], F32))
        s1 = st_.enter_context(nc.sbuf_tensor("s1", [128, 32, 2], F32))
        s2 = st_.enter_context(nc.sbuf_tensor("s2", [128, 32, 2], F32))
        s3 = st_.enter_context(nc.sbuf_tensor("s3", [128, 32, 2], F32))
        si = st_.enter_context(nc.sbuf_tensor("si", [128, 32, 2], I32))
        lbr = st_.enter_context(nc.sbuf_tensor("lbr", [128, 32, 2], F32))
        lbi = st_.enter_context(nc.sbuf_tensor("lbi", [128, 32, 2], F32))
        bsr = st_.enter_context(nc.sbuf_tensor("bsr", [128, 32, 2], F32))
        bsi = st_.enter_context(nc.sbuf_tensor("bsi", [128, 32, 2], F32))
        Bpr = st_.enter_context(nc.sbuf_tensor("Bpr", [128, 32, 2, 2, 16], F32))
        t16 = st_.enter_context(nc.sbuf_tensor("t16", [128, 32, 2, 16], F32))
        Mc = st_.enter_context(nc.sbuf_tensor("Mc", [128, 4, 8, 16], F32))
        onesm = st_.enter_context(nc.sbuf_tensor("onesm", [128, 4, 8, 16], F32))
        pre = st_.enter_context(nc.sbuf_tensor("pre", [128, 4, 128], F32))
        Trep = st_.enter_context(nc.sbuf_tensor("Trep", [128, 2, 8, 2, 128], F32))
        for gp in range(2):
            ps_ = slice(64 * gp, 64 * gp + 64)
            for d in range(2):
                C.dma("sp", lre[ps_, :, d], dram_ap(lambda_re, gp * 64 + d * 4096, [[1, 64], [128, 32]]), w=(ld,), allow_slow_non_contiguous=True)
                C.dma("sp", lim[ps_, :, d], dram_ap(lambda_im, gp * 64 + d * 4096, [[1, 64], [128, 32]]), w=(ld,), allow_slow_non_contiguous=True)
                C.dma("sp", lst[ps_, :, d], dram_ap(log_step, gp + d * 64, [[0, 64], [2, 32]]), w=(ld,), allow_slow_non_contiguous=True)
            C.dma("sp", Br[ps_, :, :], dram_ap(b_re, gp * 1024, [[16, 64], [2048, 32], [1, 16]]), w=(ld,))
            C.dma("sp", Bi[ps_, :, :], dram_ap(b_im, gp * 1024, [[16, 64], [2048, 32], [1, 16]]), w=(ld,))
        for ri, csrc in enumerate((c_re, c_im)):
            for dup in range(2):
                for d in range(2):
                    C.dma("sp", Cn[:, ri, d, :, dup, :], dram_ap(csrc, d * 65536, [[64, 128], [8192, 8], [1, 64]]), w=(ld,))
        W = (sres,)
        R = (ld, cst)
        C.op("act", lambda e: e.activation(out=s1[:], in_=lst[:], func=AF.Exp), r=R, w=W)
        C.op("dve", lambda e: e.tensor_mul(out=s2[:], in0=lre[:], in1=s1[:]), r=R, w=W)
        C.op("dve", lambda e: e.scalar_tensor_tensor(out=s3[:], in0=lim[:], scalar=1.0 / TWO_PI, in1=s1[:], op0=ALU.mult, op1=ALU.mult), r=R, w=W)
        C.op("act", lambda e: e.activation(out=s2[:], in_=s2[:], func=AF.Exp), w=W)
        C.op("dve", lambda e: e.tensor_copy(out=lbi[:], in_=s3[:]), w=W)
        cyc_sin(W, lbi[:], lbi[:], s1[:], si[:])
        cyc_sin(W, lbr[:], s3[:], s1[:], si[:], shift=0.25)
        C.op("dve", lambda e: e.tensor_mul(out=lbr[:], in0=lbr[:], in1=s2[:]), w=W)
        C.op("dve", lambda e: e.tensor_mul(out=lbi[:], in0=lbi[:], in1=s2[:]), w=W)
        C.op("dve", lambda e: e.tensor_mul(out=s1[:], in0=lre[:], in1=lre[:]), w=W)
        C.op("dve", lambda e: e.tensor_mul(out=s2[:], in0=lim[:], in1=lim[:]), w=W)
        C.op("dve", lambda e: e.tensor_add(out=s1[:], in0=s1[:], in1=s2[:]), w=W)
        C.op("dve", lambda e: e.reciprocal(out=s1[:], in_=s1[:]), w=W)
        C.op("dve", lambda e: e.tensor_scalar_add(out=s2[:], in0=lbr[:], scalar1=-1.0), w=W)
        C.op("dve", lambda e: e.tensor_mul(out=bsr[:], in0=s2[:], in1=lre[:]), w=W)
        C.op("dve", lambda e: e.tensor_mul(out=s3[:], in0=lbi[:], in1=lim[:]), w=W)
        C.op("dve", lambda e: e.tensor_add(out=bsr[:], in0=bsr[:], in1=s3[:]), w=W)
        C.op("dve", lambda e: e.tensor_mul(out=bsr[:], in0=bsr[:], in1=s1[:]), w=W)
        C.op("dve", lambda e: e.tensor_mul(out=bsi[:], in0=lbi[:], in1=lre[:]), w=W)
        C.op("dve", lambda e: e.tensor_mul(out=s3[:], in0=s2[:], in1=lim[:]), w=W)
        C.op("dve", lambda e: e.tensor_sub(out=bsi[:], in0=bsi[:], in1=s3[:]), w=W)
        C.op("dve", lambda e: e.tensor_mul(out=bsi[:], in0=bsi[:], in1=s1[:]), w=W)
        for d in range(2):
            for r in range(2):
                C.op("dve", lambda e, d=d, r=r: e.tensor_copy(out=Lr2[:, d, r, :], in_=lbr[:, :, d]), w=W)
            C.op("dve", lambda e, d=d: e.tensor_copy(out=LiP[:, d, :], in_=lbi[:, :, d]), w=W)
            C.op("dve", lambda e, d=d: e.tensor_scalar_mul(out=LiN[:, d, :], in0=lbi[:, :, d], scalar1=-1.0), w=W)
        for r in range(2):
            C.op("dve", lambda e, r=r: e.tensor_scalar_mul(out=Ls_r[:, r, :], in0=lbr[:, :, 0], scalar1=flag[:, 0:1]), w=W)
            C.op("dve", lambda e, r=r: e.scalar_tensor_tensor(out=Ls_r[:, r, :], in0=lbr[:, :, 1], scalar=flag[:, 1:2], in1=Ls_r[:, r, :],
                                                              op0=ALU.mult, op1=ALU.add), w=W)
        C.op("dve", lambda e: e.tensor_scalar_mul(out=Ls_p[:], in0=lbi[:, :, 0], scalar1=flag[:, 0:1]), w=W)
        C.op("dve", lambda e: e.scalar_tensor_tensor(out=Ls_p[:], in0=lbi[:, :, 1], scalar=flag[:, 1:2], in1=Ls_p[:], op0=ALU.mult, op1=ALU.add), w=W)
        C.op("dve", lambda e: e.tensor_scalar_mul(out=Ls_n[:], in0=Ls_p[:], scalar1=-1.0), w=W)
        for d in range(2):
            C.op("dve", lambda e, d=d: e.tensor_copy(out=LiS[:, d, 0, :], in_=LiN[:, d, :]), w=W)
            C.op("dve", lambda e, d=d: e.tensor_copy(out=LiS[:, d, 1, :], in_=LiP[:, d, :]), w=W)
        C.op("dve", lambda e: e.tensor_copy(out=Ls_s[:, 0, :], in_=Ls_n[:]), w=W)
        C.op("dve", lambda e: e.tensor_copy(out=Ls_s[:, 1, :], in_=Ls_p[:]), w=W)
        for d in range(2):
            bsr_b = bcast(bsr[:, :, d], [128, 32, 16], 2); bsi_b = bcast(bsi[:, :, d], [128, 32, 16], 2)
            C.op("dve", lambda e, d=d, a=bsr_b: e.tensor_mul(out=Bpr[:, :, d, 0, :], in0=Br[:], in1=a), w=W)
            C.op("dve", lambda e, d=d, a=bsi_b: e.tensor_mul(out=t16[:, :, d, :], in0=Bi[:], in1=a), w=W)
            C.op("dve", lambda e, d=d: e.tensor_sub(out=Bpr[:, :, d, 0, :], in0=Bpr[:, :, d, 0, :], in1=t16[:, :, d, :]), w=W)
            C.op("dve", lambda e, d=d, a=bsr_b: e.tensor_mul(out=Bpr[:, :, d, 1, :], in0=Bi[:], in1=a), w=W)
            C.op("dve", lambda e, d=d, a=bsi_b: e.tensor_mul(out=t16[:, :, d, :], in0=Br[:], in1=a), w=W)
            C.op("dve", lambda e, d=d: e.tensor_add(out=Bpr[:, :, d, 1, :], in0=Bpr[:, :, d, 1, :], in1=t16[:, :, d, :]), w=W)
        C.op("pool", lambda e: e.memset(onesm[:], 1.0), w=W)
        for gp in range(2):
            ps_ = slice(64 * gp, 64 * gp + 64)
            C.op("pool", lambda e, ps_=ps_, gp=gp: e.affine_select(out=Mc[ps_], in_=onesm[ps_], pattern=[[-2, 4], [1, 8], [0, 16]],
                                                                   compare_op=ALU.is_equal, fill=0.0, base=-gp, channel_multiplier=0), w=W)
        k = 0
        for d in range(2):
            for r in range(2):
                for i in range(32):
                    slot = k % 4
                    C.op("dve", lambda e, i=i, d=d, r=r, slot=slot: e.tensor_mul(
                        out=pre[:, slot, :].rearrange("p (g c) -> p g c", g=8), in0=Mc[:, i % 4, :, :],
                        in1=bcast(Bpr[:, i, d, r, :], [128, 8, 16], 1)), w=W)
                    bk = (k // 4) % 2
                    C.op("pe", lambda e, slot=slot, bk=bk: e.transpose(banks[bk][:, slot * 128:(slot + 1) * 128], pre[:, slot, :], ident[:]),
                         r=W + (cst,), w=(bres[bk],))
                    if slot == 3:
                        i0 = i - 3
                        C.op("act", lambda e, bk=bk, i0=i0, d=d, r=r: e.activation(
                            out=Bp[:, i0:i0 + 4, d, r, :], in_=banks[bk][:, :].rearrange("p (a x) -> p a x", a=4), func=AF.Copy),
                            r=(bres[bk],), w=W)
                    k += 1
        C.op("dve", lambda e: e.tensor_scalar_mul(out=Cn[:, 1], in0=Cn[:, 1], scalar1=-1.0), r=R, w=W)
        k = 0
        for d in range(2):
            for gt in range(8):
                for r in range(2):
                    bk = (k // 4) % 2; slot = k % 4
                    C.op("pe", lambda e, bk=bk, slot=slot, r=r, d=d, gt=gt: e.transpose(
                        banks[bk][:, slot * 128:(slot + 1) * 128], Cn[:, r, d, gt, :, :].rearrange("p a b -> p (a b)"), ident[:]),
                        r=W + (cst,), w=(bres[bk],))
                    C.op("act", lambda e, bk=bk, slot=slot, r=r, d=d, gt=gt: e.activation(
                        out=Trep[:, d, gt, r, :], in_=banks[bk][:, slot * 128:(slot + 1) * 128], func=AF.Copy), r=(bres[bk],), w=W)
                    k += 1
        for d in range(2):
            for r in range(2):
                for i in range(32):
                    C.op("dve", lambda e, i=i, d=d, r=r: e.tensor_mul(
                        out=Cp[:, i, d, r, :].rearrange("p (g c) -> p g c", g=8),
                        in0=Trep[:, d, i // 4, r, :].rearrange("p (g c) -> p g c", g=8), in1=Mc[:, i % 4, :, :]), w=W)
        dump("pidx", pidx[:], cst); dump("hicol", hicol[:], cst); dump("colpos", colpos[:], cst); dump("fq", fq[:], cst)
        dump("lbr", lbr[:], sres); dump("lbi", lbi[:], sres); dump("bsr", bsr[:], sres); dump("bsi", bsi[:], sres)
        dump("Bpr", Bpr[:], sres); dump("Mc", Mc[:], sres); dump("Trep", Trep[:], sres)
        dump("Bp", Bp[:], sres); dump("Cp", Cp[:], sres)
        C.barrier()

    UT = 32
    NU = 128 // UT
    ACOL = UT + 1
    RS, DS, IS = 32 * ACOL, 2 * 32 * ACOL, ACOL
    AFR = 2 * DS
    MAXU2 = max(j["n_own"] for j in jobs) * NU
    Abuf = [ssb("A%d" % k, [128, 2, 2, 32, ACOL]) for k in range(2)]
    ares = [C.res("A%d" % k) for k in range(2)]
    T1 = ssb("T1", [128, 2, 2, 32]); T2 = ssb("T2", [128, 2, 2, 32])
    Xbf = ssb("Xbf", [128, 2, 2, 32, UT], BF16)
    bnd = ssb("bnd", [128, MAXU2 + 1, 2, 32], BF16)
    Ef = ssb("Ef", [128, 2, 32]); Sx = ssb("Sx", [128, 2, 32])
    wbs = [(ssb("wb%d" % i, [128, NDT, 256], BF16), C.res("wb%d" % i)) for i in range(3)]
    uTs = [ssb("uT%d" % k, [128, 8, 128], BF16) for k in range(2)]; ures = [C.res("uT%d" % k) for k in range(2)]
    uTx = [ssb("uTx%d" % k, [128, 8, 128], BF16) for k in range(2)]; uxres = [C.res("uTx%d" % k) for k in range(2)]
    gsTs = [ssb("gsT%d" % k, [128, 8, 128], BF16) for k in range(2)]; gres = [C.res("gsT%d" % k) for k in range(2)]
    yT = ssb("yT", [128, 8, 128]); yres = C.res("yT")
    g1 = ssb("g1", [128, 8, 128]); g2 = ssb("g2", [128, 8, 128]); gb = ssb("gb", [128, 8, 128], BF16); g3 = gb
    gwres = C.res("gelu")
    msT = ssb("msT", [128, 8, 128], BF16); mres = C.res("msT")
    xbres = C.res("Xbf"); eres = C.res("carry")
    psT, psL, psB, psY, psM = (0, 1), (2, 3), (4, 5), 6, 7
    junk = hT[:].rearrange("p a b -> p (a b)")

    def a_ap(ab, off, pat):
        return bass.AP(Abuf[ab][:].tensor, off, [[AFR, 128]] + [list(p) for p in pat])

    def front(xt, xres, need_rep):
        C.op("act", lambda e: e.activation(out=junk, in_=xt[:], func=AF.Square, accum_out=stat[:, 0:1]), r=(xres,), w=(hres, stres))
        C.op("dve", lambda e: e.tensor_scalar(out=stat[:, 0:1], in0=stat[:, 0:1], scalar1=1.0 / D, scalar2=EPS, op0=ALU.mult, op1=ALU.add), w=(stres,))
        C.op("act", lambda e: e.activation(out=stat[:, 0:1], in_=stat[:, 0:1], func=AF.Sqrt), w=(stres,))
        C.op("dve", lambda e: e.reciprocal(out=stat[:, 0:1], in_=stat[:, 0:1]), w=(stres,))
        for q in range(4):
            bk = psT[q % 2]
            for kk in range(4):
                dt_ = 4 * q + kk
                C.op("pe", lambda e, bk=bk, kk=kk, dt_=dt_: e.transpose(banks[bk][:, kk * 128:(kk + 1) * 128], xt[:, dt_ * 128:(dt_ + 1) * 128], ident[:]),
                     r=(xres, cst), w=(bres[bk],))
            C.op("dve", lambda e, bk=bk, q=q: e.tensor_tensor(out=hT[:, 4 * q:4 * q + 4, :], in0=banks[bk][:, :].rearrange("p (a t) -> p a t", a=4),
                                                              in1=bcast(nwcol[:, 4 * q:4 * q + 4], [128, 4, 128], 2), op=ALU.mult),
                 r=(bres[bk], ld), w=(hres,))
        if need_rep:
            C.op("dve", lambda e: e.tensor_scalar_mul(out=dg[:], in0=ident[:], scalar1=stat[:, 0:1]), r=(stres, cst), w=(rres,))
            C.op("pe", lambda e: e.matmul(banks[psM][:, 0:128], lhsT=ones_t[:], rhs=dg[:], start=True, stop=True), r=(rres, cst), w=(bres[psM],))
            C.op("act", lambda e: e.activation(out=rrep[:], in_=banks[psM][:, 0:128], func=AF.Copy), r=(bres[psM],), w=(rres,))

    def front_n(xt, xres):
        C.op("act", lambda e: e.activation(out=junk, in_=xt[:], func=AF.Square, accum_out=stat[:, 0:1]), r=(xres,), w=(hres, stres))
        C.op("pool", lambda e: e.tensor_scalar(out=stat[:, 0:1], in0=stat[:, 0:1], scalar1=1.0 / D, scalar2=EPS, op0=ALU.mult, op1=ALU.add), w=(stres,))
        C.op("pool", lambda e: e.tensor_tensor(out=stat[:, 0:1], in0=stat[:, 0:1], in1=mhalf[:], op=ALU.pow), r=(cst,), w=(stres,))
        C.op("act", lambda e: e.activation(out=xt[:], in_=xt[:], func=AF.Copy, scale=stat[:, 0:1]), r=(stres,), w=(xres,))
        for q in range(4):
            bk = psT[q % 2]
            for kk in range(4):
                dt_ = 4 * q + kk
                C.op("pe", lambda e, bk=bk, kk=kk, dt_=dt_: e.transpose(banks[bk][:, kk * 128:(kk + 1) * 128], xt[:, dt_ * 128:(dt_ + 1) * 128], ident[:]),
                     r=(xres, cst), w=(bres[bk],))
            for kk in range(4):
                dt_ = 4 * q + kk
                C.op("act", lambda e, bk=bk, kk=kk, dt_=dt_: e.activation(out=hT[:, dt_, :], in_=banks[bk][:, kk * 128:(kk + 1) * 128], func=AF.Copy, scale=nwcol[:, dt_:dt_ + 1]),
                     r=(bres[bk], ld), w=(hres,))

    def lin_feat(wt, wres, bk, e_off, ne, nk=NDT, rhs=None, rres_=None):
        rhs = hT if rhs is None else rhs
        rres_ = hres if rres_ is None else rres_
        for e_ in range(ne):
            for dt_ in range(nk):
                C.op("pe", lambda e, e_=e_, dt_=dt_: e.matmul(banks[bk][:, (e_off + e_) * 128:(e_off + e_ + 1) * 128], lhsT=wt[:, dt_, e_ * 128:(e_ + 1) * 128],
                                                               rhs=rhs[:, dt_, :], start=(dt_ == 0), stop=(dt_ == nk - 1)),
                     r=(wres, rres_), w=(bres[bk],))

    def wsrc(wbf, c0, n=256):
        return wbf[:, c0:c0 + n].rearrange("(dt p) c -> p dt c", p=128)

    def glu_view(t):
        return t[:, 0:8, :]

    wk = [0]

    def feat_1024(ws, dsts, func=AF.Copy):
        for half in range(2):
            bk = psL[half]
            for b2 in range(2):
                wt, wres = ws.get(wk[0]); wk[0] += 1
                lin_feat(wt, wres, bk, 2 * b2, 2)
            for dst, dres, sc in dsts:
                if sc is None:
                    C.op("act", lambda e, bk=bk, half=half, dst=dst: e.activation(out=dst[:, 4 * half:4 * half + 4, :], in_=banks[bk][:, :].rearrange("p (a t) -> p a t", a=4), func=func),
                         r=(bres[bk],), w=(dres,))
                else:
                    C.op("act", lambda e, bk=bk, half=half, dst=dst, sc=sc: e.activation(out=dst[:, 4 * half:4 * half + 4, :], in_=banks[bk][:, :].rearrange("p (a t) -> p a t", a=4), func=func, scale=sc),
                         r=(bres[bk], ld), w=(dres,))

    bk_tog = [0]

    def bmm(ab, uT_, ures_, tc0, d, dslot, col0, mode, uT_b=None, ures_b=None):
        PB = 512 // UT
        for r in range(2):
            for pb in range(32 // PB):
                bk = psB[bk_tog[0] % 2]; bk_tog[0] += 1
                for ii in range(PB):
                    i = pb * PB + ii
                    if mode == "set":
                        C.op("pe", lambda e, bk=bk, ii=ii, i=i, r=r: e.matmul(banks[bk][:, ii * UT:(ii + 1) * UT], lhsT=Bp[:, i, d, r, :],
                                                                              rhs=uT_[:, i // 4, tc0:tc0 + UT], start=True, stop=True),
                             r=(sres, ures_), w=(bres[bk],))
                    else:
                        C.op("pe", lambda e, bk=bk, ii=ii, i=i, r=r: e.matmul(banks[bk][:, ii * UT:(ii + 1) * UT], lhsT=Bp[:, i, 0, r, :],
                                                                              rhs=uT_[:, i // 4, tc0:tc0 + UT], start=True, stop=False),
                             r=(sres, ures_), w=(bres[bk],))
                        C.op("pe", lambda e, bk=bk, ii=ii, i=i, r=r: e.matmul(banks[bk][:, ii * UT:(ii + 1) * UT], lhsT=Bp[:, i, 1, r, :],
                                                                              rhs=uT_b[:, i // 4, tc0:tc0 + UT], start=False, stop=True),
                             r=(sres, ures_b), w=(bres[bk],))
                dst = Abuf[ab][:, dslot, r, pb * PB:(pb + 1) * PB, col0:col0 + UT]
                src = banks[bk][:, :].rearrange("p (a t) -> p a t", a=PB)
                C.op("act", lambda e, dst=dst, src=src: e.activation(out=dst, in_=src, func=AF.Copy), r=(bres[bk],), w=(ares[ab],))

    rT1 = C.res("T1"); rT2a = C.res("T2a"); rT2b = C.res("T2b")

    def step_single(ab, dslot, cur, nxt, lr, lsw):
        base = dslot * DS
        c = a_ap(ab, base + cur, [[RS, 2], [IS, 32]]); n = a_ap(ab, base + nxt, [[RS, 2], [IS, 32]])
        csw = a_ap(ab, base + RS + cur, [[-RS, 2], [IS, 32]])
        t1 = T1[:, 0]; t2 = T2[:, 0]
        A_ = (ares[ab],)
        C.op("dve", lambda e: e.tensor_mul(out=t1, in0=c, in1=lr), r=A_, wo=(rT1,))
        C.op("dve", lambda e: e.tensor_mul(out=t2, in0=csw, in1=lsw), r=A_, wo=(rT2a,))
        C.op("dve", lambda e: e.tensor_add(out=n, in0=n, in1=t1), r=(rT1,), w=A_)
        C.op("dve", lambda e: e.tensor_add(out=n, in0=n, in1=t2), r=(rT2a,), w=A_)

    def step_both(ab, j):
        sc = DS + UT - 2 * j; sn_ = DS + UT - 2 - 2 * j
        c = a_ap(ab, j, [[sc, 2], [RS, 2], [IS, 32]]); n = a_ap(ab, j + 1, [[sn_, 2], [RS, 2], [IS, 32]])
        csw = a_ap(ab, j + RS, [[sc, 2], [-RS, 2], [IS, 32]])
        A_ = (ares[ab],)
        C.op("dve", lambda e: e.tensor_mul(out=T1[:], in0=c, in1=Lr2[:]), r=A_, wo=(rT1,))
        C.op("dve", lambda e: e.tensor_mul(out=T2[:], in0=csw, in1=LiS[:]), r=A_, wo=(rT2a,))
        C.op("dve", lambda e: e.tensor_add(out=n, in0=n, in1=T1[:]), r=(rT1,), w=A_)
        C.op("dve", lambda e: e.tensor_add(out=n, in0=n, in1=T2[:]), r=(rT2a,), w=A_)

    for job in jobs:
        n_own, n_oth = job["n_own"], job["n_oth"]
        isS = job["kind"] == "S"
        JK = job["kind"] + str(job["tok0"])
        order = [("x", t) for t in range(n_oth)] + [("o", t) for t in range(n_own - 1, -1, -1)]
        xsrc = [(job["xx"] if kd == "x" else job["xo"])[t * 128:(t + 1) * 128, :] for kd, t in order]
        xs = Stream(C, "sp", xts, xsrc)
        ws = Stream(C, "sp", wbs, [wA[b].rearrange("p (dt c) -> p dt c", dt=NDT) for _ in order for b in range(4)], r_extra=(wc,))
        wk[0] = 0
        C.op("dve", lambda e: e.memset(Sx[:], 0.0), w=(eres,))
        seq = []
        for n, (kd, t) in enumerate(order):
            for u in (range(NU) if kd == "x" else range(NU - 1, -1, -1)):
                seq.append((n, kd, t, u))

        def pre1(n):
            xt, xres = xs.get(n)
            front_n(xt, xres)
            if order[n][0] == "x":
                feat_1024(ws, [(uTs[n % 2], ures[n % 2], flag[:, 0:1]), (uTx[n % 2], uxres[n % 2], flag[:, 1:2])])
            else:
                feat_1024(ws, [(uTs[n % 2], ures[n % 2], None)])

        def B1(k):
            n, kd, t, u = seq[k]
            ab = k % 2
            if kd == "x":
                bmm(ab, uTs[n % 2], ures[n % 2], u * UT, 0, 0, 1, "acc", uTx[n % 2], uxres[n % 2])
            else:
                bmm(ab, uTs[n % 2], ures[n % 2], u * UT, 1, 1, 0, "set")

        pre1(0)
        B1(0)
        bnd_init = False
        for k, (n, kd, t, u) in enumerate(seq):
            ab = k % 2
            if k + 1 < len(seq):
                if seq[k + 1][0] != n:
                    pre1(seq[k + 1][0])
                B1(k + 1)
            if kd == "x":
                C.op("dve", lambda e, ab=ab: e.tensor_copy(out=Abuf[ab][:, 0, :, :, 0], in_=Sx[:]), r=(eres,), w=(ares[ab],))
                for j in range(UT):
                    step_single(ab, 0, j, j + 1, Ls_r[:], Ls_s[:])
                C.op("dve", lambda e, ab=ab: e.tensor_copy(out=Sx[:], in_=Abuf[ab][:, 0, :, :, UT]), r=(ares[ab],), w=(eres,))
            else:
                U = NU * t + u
                if not bnd_init:
                    U_last = NU * n_own
                    if isS and n_oth > 0:
                        C.op("dve", lambda e, U_last=U_last: e.tensor_scalar_mul(out=Sx[:], in0=Sx[:], scalar1=flag[:, 1:2]), r=(ld,), w=(eres,))
                        C.op("dve", lambda e, U_last=U_last: e.tensor_copy(out=bnd[:, U_last, :, :], in_=Sx[:]), w=(eres,))
                    else:
                        C.op("dve", lambda e, U_last=U_last: e.memset(Sx[:], 0.0), w=(eres,))
                        C.op("dve", lambda e, U_last=U_last: e.tensor_copy(out=bnd[:, U_last, :, :], in_=Sx[:]), w=(eres,))
                    bnd_init = True
                C.op("dve", lambda e, ab=ab: e.tensor_copy(out=Abuf[ab][:, 1, :, :, UT], in_=Sx[:]), r=(eres,), w=(ares[ab],))
                for j in range(UT):
                    step_single(ab, 1, UT - j, UT - 1 - j, Lr2[:, 1], LiS[:, 1])
                C.op("dve", lambda e, ab=ab: e.tensor_copy(out=Sx[:], in_=Abuf[ab][:, 1, :, :, 0]), r=(ares[ab],), w=(eres,))
                C.op("dve", lambda e, U=U: e.tensor_copy(out=bnd[:, U, :, :], in_=Sx[:]), w=(eres,))
            if kd == "x" and (k + 1 == len(seq) or seq[k + 1][1] != "x"):
                C.op("dve", lambda e: e.tensor_scalar_mul(out=Ef[:], in0=Sx[:], scalar1=flag[:, 0:1]), r=(ld,), w=(eres,))
        if not (isS and n_oth > 0):
            C.op("dve", lambda e: e.memset(Ef[:], 0.0), w=(eres,))
        xs = Stream(C, "sp", xts, [job["xo"][t * 128:(t + 1) * 128, :] for t in range(n_own)])
        blkU = [(wA[b].rearrange("p (dt c) -> p dt c", dt=NDT), None) for b in range(4)]
        blkG = [(wA[4 + b].rearrange("p (dt c) -> p dt c", dt=NDT), None) for b in range(4)]
        blkL = [(wG[b].rearrange("p (et c) -> p et c", et=8), glu_view) for b in range(4)]
        srcs = blkU + blkG
        for t in range(n_own):
            if t + 1 < n_own:
                srcs = srcs + blkU + blkG
            srcs = srcs + blkL
        ws = Stream(C, "sp", wbs, srcs, r_extra=(wc,))
        wk[0] = 0

        def pre2(t):
            xt, xres = xs.get(t)
            front_n(xt, xres)
            feat_1024(ws, [(uTs[t % 2], ures[t % 2], None)])
            feat_1024(ws, [(gsTs[t % 2], gres[t % 2], None)], func=AF.Silu)

        def B2(k):
            t, u = divmod(k, NU)
            bmm(k % 2, uTs[t % 2], ures[t % 2], u * UT, 0, 0, 1, "set")
            bmm(k % 2, uTs[t % 2], ures[t % 2], u * UT, 1, 1, 0, "set")

        def post2(t):
            uT = uTs[t % 2]
            dump("yT_%s%d" % (job["kind"], t), yT[:], yres)
            GW = (gwres,)
            PQ = "pool"
            C.op(PQ, lambda e: e.tensor_tensor(out=g1[:], in0=uT[:], in1=bcast(dcol[:], [128, 8, 128], 2), op=ALU.mult), r=(ures[t % 2], ld), w=GW)
            C.op(PQ, lambda e: e.tensor_add(out=yT[:], in0=yT[:], in1=g1[:]), r=GW, w=(yres,))
            C.op("act", lambda e: e.activation(out=g1[:], in_=yT[:], func=AF.Square), r=(yres,), w=GW)
            C.op(PQ, lambda e: e.tensor_scalar(out=g1[:], in0=g1[:], scalar1=0.044715, scalar2=1.0, op0=ALU.mult, op1=ALU.add), w=GW)
            C.op(PQ, lambda e: e.tensor_mul(out=g1[:], in0=g1[:], in1=yT[:]), r=(yres,), w=GW)
            C.op("act", lambda e: e.activation(out=g1[:], in_=g1[:], func=AF.Sigmoid, scale=1.5957691216057308), w=GW)
            C.op(PQ, lambda e: e.tensor_mul(out=g2[:], in0=g1[:], in1=yT[:]), r=(yres,), w=GW)
            C.op("act", lambda e: e.activation(out=gb[:], in_=g2[:], func=AF.Copy), w=GW)
            for half in range(2):
                bk = psL[half]
                for b2 in range(2):
                    wt, wres = ws.get(wk[0]); wk[0] += 1
                    lin_feat(wt, wres, bk, 2 * b2, 2, nk=8, rhs=gb, rres_=gwres)
                C.op("act", lambda e, bk=bk, half=half: e.activation(out=g1[:, 4 * half:4 * half + 4, :], in_=banks[bk][:, :].rearrange("p (a t) -> p a t", a=4), func=AF.Sigmoid),
                     r=(bres[bk],), w=GW)
            C.op(PQ, lambda e: e.tensor_mul(out=g2[:], in0=g2[:], in1=g1[:]), w=GW)
            C.op(PQ, lambda e: e.tensor_mul(out=g2[:], in0=g2[:], in1=gsTs[t % 2][:]), r=(gres[t % 2],), w=GW)
            C.op("act", lambda e: e.activation(out=g3[:], in_=g2[:], func=AF.Square), w=GW)
            for et in range(8):
                C.op("pe", lambda e, et=et: e.matmul(banks[psM][:, 128:256], lhsT=onesb[:], rhs=g3[:, et, :], start=(et == 0), stop=(et == 7)),
                     r=(gwres, cst), w=(bres[psM],))
            C.op("act", lambda e: e.activation(out=dg[:], in_=banks[psM][:, 128:256], func=AF.Copy, scale=1.0 / 1024), r=(bres[psM],), w=(rres,))
            C.op(PQ, lambda e: e.tensor_scalar_add(out=dg[:], in0=dg[:], scalar1=EPS), w=(rres,))
            C.op(PQ, lambda e: e.tensor_tensor(out=dg[:], in0=dg[:], in1=bcast(mhalf[:, 0], [128, 128], 1) if False else mhalf[:].broadcast_to([128, 128]), op=ALU.pow), r=(cst,), w=(rres,))
            C.op(PQ, lambda e: e.tensor_tensor(out=g2[:], in0=g2[:], in1=bcast(dg[:], [128, 8, 128], 1), op=ALU.mult), r=(rres,), w=GW)
            C.op(PQ, lambda e: e.tensor_tensor(out=msT[:], in0=g2[:], in1=bcast(snwcol[:], [128, 8, 128], 2), op=ALU.mult), r=(gwres, ld), w=(mres,))
            dump("msT_%s%d" % (job["kind"], t), msT[:], mres)
            tok = job["tok0"] + t * 128
            C.dma("sp", ms_d[:, :, tok:tok + 128].rearrange("e p t -> p e t"), msT[:], r=(mres,))

        pre2(0)
        B2(0)
        nk2 = n_own * NU
        for k in range(nk2):
            t, u = divmod(k, NU)
            ab = k % 2
            if k + 1 < nk2:
                if u == NU - 1:
                    pre2(t + 1)
                B2(k + 1)
            C.op("dve", lambda e, ab=ab: e.tensor_copy(out=Abuf[ab][:, 0, :, :, 0], in_=Ef[:]), r=(eres,), w=(ares[ab],))
            C.op("dve", lambda e, ab=ab, k=k: e.tensor_copy(out=Abuf[ab][:, 1, :, :, UT], in_=bnd[:, k + 1, :, :]), r=(eres,), w=(ares[ab],))
            for j in range(UT):
                step_both(ab, j)
            C.op("dve", lambda e, ab=ab: e.tensor_copy(out=Ef[:], in_=Abuf[ab][:, 0, :, :, UT]), r=(ares[ab],), w=(eres,))
            C.op("pool", lambda e, ab=ab: e.tensor_copy(out=Xbf[:, 0], in_=Abuf[ab][:, 0, :, :, 1:UT + 1]), r=(ares[ab],), w=(xbres,))
            C.op("pool", lambda e, ab=ab: e.tensor_copy(out=Xbf[:, 1], in_=Abuf[ab][:, 1, :, :, 0:UT]), r=(ares[ab],), w=(xbres,))
            for gt in range(8):
                kk = 0
                for i in range(4 * gt, 4 * gt + 4):
                    for d in range(2):
                        for r in range(2):
                            C.op("pe", lambda e, gt=gt, i=i, d=d, r=r, kk=kk: e.matmul(banks[psY][:, gt * UT:(gt + 1) * UT], lhsT=Cp[:, i, d, r, :],
                                                                                      rhs=Xbf[:, d, r, i, :], start=(kk == 0), stop=(kk == 15)),
                                 r=(sres, xbres), w=(bres[psY],))
                            kk += 1
            C.op("act", lambda e, u=u: e.activation(out=yT[:, :, u * UT:(u + 1) * UT], in_=banks[psY][:, 0:8 * UT].rearrange("p (a t) -> p a t", a=8), func=AF.Copy),
                 r=(bres[psY],), w=(yres,))
            if u == NU - 1:
                post2(t)

    C.barrier()
    ssm_stack.close()

    KT = sb("KT", [128, 2, MAXS * 128], BF16)
    anw = sb("anw", [128, 1024])
    C.dma("sp", anw[:], dram_ap(attn_out_norm_w, 0, [[0, 128], [1, 1024]]), w=(ld,))
    V1 = sb("V1", [128, MAXS, 2, 130], BF16)
    wb2 = [(sb("wc%d" % i, [128, NDT, 512], BF16), C.res("wc%d" % i)) for i in range(2)]
    QT = sb("QT", [128, 2, 512], BF16); qtres = C.res("QT")
    qf = sb("qf", [128, 4, 128]); qn = sb("qn", [128, 4, 128]); qt2 = sb("qt2", [128, 4, 128]); qr = sb("qr", [128, 4, 128], BF16)
    qres = C.res("qwork")
    jk = sb("jk", [128, 128], BF16)
    ga = sb("ga", [128, 1024], BF16); gares = C.res("ga")
    oa = sb("oa", [128, 1024]); oares = C.res("oa")
    Mb = sb("Mb", [128, 1024], BF16)
    MT = sb("MT", [128, NDT, 128], BF16); mtres = C.res("MT"); mt2res = C.res("MT2")
    PTs = [(sb("PT%d" % i, [128, 512], BF16), C.res("PT%d" % i)) for i in range(3)]
    youts = [(sb("yo%d" % i, [128, D]), C.res("yo%d" % i)) for i in range(2)]
    cs = sb("cs", [128, 128]); sn = sb("sn", [128, 128]); rpres = C.res("rope")
    tq = sb("tq", [128, 64]); tf = sb("tf", [128, 64]); ti = sb("ti", [128, 64], I32); pc = sb("pc", [128, 2])
    kvres = C.res("kv")
    psS, psO = (4, 5), (6, 7)
    C.op("pool", lambda e: e.memset(V1[:], 1.0), w=(kvres,))
    SCALE = 128.0 ** -0.5

    def make_rope(kind, t):
        W = (rpres,)
        R = (cst, ld)
        if kind == "x":
            C.op("dve", lambda e: e.tensor_scalar_add(out=pc[:, 0:1], in0=hicol[:], scalar1=float(2 * t)), r=R, w=W)
            C.op("dve", lambda e: e.tensor_mul(out=pc[:, 0:1], in0=pc[:, 0:1], in1=flag[:, 2:3]), r=R, w=W)
            C.op("dve", lambda e: e.scalar_tensor_tensor(out=pc[:, 0:1], in0=flag[:, 1:2], scalar=float((LSO + LSX) // 64 - 1), in1=pc[:, 0:1], op0=ALU.mult, op1=ALU.add), r=R, w=W)
            C.op("dve", lambda e: e.tensor_mul(out=pc[:, 1:2], in0=colpos[:], in1=flag[:, 2:3]), r=R, w=W)
            C.op("dve", lambda e: e.scalar_tensor_tensor(out=pc[:, 1:2], in0=flag[:, 1:2], scalar=63.0, in1=pc[:, 1:2], op0=ALU.mult, op1=ALU.add), r=R, w=W)
        else:
            C.op("dve", lambda e: e.tensor_scalar_add(out=pc[:, 0:1], in0=hicol[:], scalar1=float(2 * t)), r=R, w=W)
            if kind == "s":
                C.op("dve", lambda e: e.scalar_tensor_tensor(out=pc[:, 0:1], in0=flag[:, 0:1], scalar=float(2 * (LSO // 128)), in1=pc[:, 0:1], op0=ALU.mult, op1=ALU.add), r=R, w=W)
            C.op("dve", lambda e: e.tensor_copy(out=pc[:, 1:2], in_=colpos[:]), r=R, w=W)
        for shift, which in ((0.0, "sin"), (0.25, "cos")):
            C.op("dve", lambda e: e.tensor_scalar(out=tq[:, 0:32], in0=fq[:], scalar1=pc[:, 0:1], scalar2=shift, op0=ALU.mult, op1=ALU.add), r=R, w=W)
            C.op("dve", lambda e: e.tensor_scalar(out=tq[:, 32:64], in0=fq[:], scalar1=pc[:, 1:2], scalar2=shift, op0=ALU.mult, op1=ALU.add), r=R, w=W)
            C.op("dve", lambda e: e.tensor_copy(out=ti[:], in_=tq[:]), w=W)
            C.op("dve", lambda e: e.tensor_copy(out=tf[:], in_=ti[:]), w=W)
            C.op("dve", lambda e: e.tensor_sub(out=tq[:], in0=tq[:], in1=tf[:]), w=W)
            C.op("act", lambda e: e.activation(out=tf[:], in_=tq[:], func=AF.Sin, scale=TWO_PI_S), w=W)
            tf3 = tf[:].rearrange("p (a j) -> p a j", a=2)
            if which == "cos":
                c5 = cs[:].rearrange("p (a x j) -> p a x j", a=2, x=2)
                for x_ in range(2):
                    C.op("dve", lambda e, x_=x_: e.tensor_copy(out=c5[:, :, x_, :], in_=tf3), w=W)
            else:
                s5 = sn[:].rearrange("p (a x j) -> p a x j", a=2, x=2)
                C.op("dve", lambda e: e.tensor_scalar_mul(out=s5[:, :, 0, :], in0=tf3, scalar1=-1.0), w=W)
                C.op("dve", lambda e: e.tensor_copy(out=s5[:, :, 1, :], in_=tf3), w=W)

    def qk_norm_rope(bk, nh, wrow):
        W = (qres,)
        C.op("act", lambda e: e.activation(out=qf[:, 0:nh, :], in_=banks[bk][:, 0:nh * 128].rearrange("p (h d) -> p h d", h=nh), func=AF.Copy, scale=stat[:, 0:1]),
             r=(bres[bk], stres), w=W)
        for h in range(nh):
            C.op("act", lambda e, h=h: e.activation(out=jk[:], in_=qf[:, h, :], func=AF.Square, accum_out=stat[:, 1 + h:2 + h]), w=W + (stres,))
        C.op("dve", lambda e: e.tensor_scalar(out=stat[:, 1:1 + nh], in0=stat[:, 1:1 + nh], scalar1=1.0 / 128, scalar2=EPS, op0=ALU.mult, op1=ALU.add), w=W + (stres,))
        C.op("act", lambda e: e.activation(out=stat[:, 1:1 + nh], in_=stat[:, 1:1 + nh], func=AF.Sqrt), w=W + (stres,))
        C.op("dve", lambda e: e.reciprocal(out=stat[:, 1:1 + nh], in_=stat[:, 1:1 + nh]), w=W + (stres,))
        for h in range(nh):
            C.op("dve", lambda e, h=h: e.scalar_tensor_tensor(out=qn[:, h, :], in0=qf[:, h, :], scalar=stat[:, 1 + h:2 + h], in1=wrow[:], op0=ALU.mult, op1=ALU.mult),
                 r=(ld,), w=W)
        qn5 = qn[:, 0:nh, :].rearrange("p h (a x j) -> p h a x j", a=2, x=2)
        t25 = qt2[:, 0:nh, :].rearrange("p h (a x j) -> p h a x j", a=2, x=2)
        s5 = sn[:].rearrange("p (a x j) -> p a x j", a=2, x=2)
        C.op("dve", lambda e: e.tensor_tensor(out=t25[:, :, :, 0, :], in0=qn5[:, :, :, 1, :], in1=bcast(s5[:, :, 0, :], [128, nh, 2, 32], 1), op=ALU.mult), r=(rpres,), w=W)
        C.op("dve", lambda e: e.tensor_tensor(out=t25[:, :, :, 1, :], in0=qn5[:, :, :, 0, :], in1=bcast(s5[:, :, 1, :], [128, nh, 2, 32], 1), op=ALU.mult), r=(rpres,), w=W)
        C.op("dve", lambda e: e.tensor_tensor(out=qn[:, 0:nh, :], in0=qn[:, 0:nh, :], in1=bcast(cs[:], [128, nh, 128], 1), op=ALU.mult), r=(rpres,), w=W)
        dump("qf_%d" % nh, qf[:], qres); dump("qnc_%d" % nh, qn[:], qres); dump("qt2_%d" % nh, qt2[:], qres)
        dump("cs_%d" % nh, cs[:], rpres); dump("sn_%d" % nh, sn[:], rpres); dump("pc_%d" % nh, pc[:], rpres)
        C.op("dve", lambda e: e.tensor_add(out=qr[:, 0:nh, :], in0=qn[:, 0:nh, :], in1=qt2[:, 0:nh, :]), w=W)
        dump("qr_%d" % nh, qr[:], qres)

    def lin_tok(wt, wres, bk, lhs, lres, nk=NDT):
        for dt_ in range(nk):
            C.op("pe", lambda e, dt_=dt_: e.matmul(banks[bk][:, :], lhsT=lhs[:, dt_, :], rhs=wt[:, dt_, :], start=(dt_ == 0), stop=(dt_ == nk - 1)),
                 r=(wres, lres), w=(bres[bk],))

    def wsrc2(wbf, c0):
        return wbf[:, c0:c0 + 512].rearrange("(dt p) c -> p dt c", p=128)

    for job in jobs:
        n_own, n_oth = job["n_own"], job["n_oth"]
        isS = job["kind"] == "S"
        nst = n_own + n_oth
        order = [("o", t) for t in range(n_own)] + [("x", t) for t in range(n_oth)]
        xsrc = [(job["xx"] if kd == "x" else job["xo"])[t * 128:(t + 1) * 128, :] for kd, t in order]
        xs = Stream(C, "sp", xts, xsrc)
        ws = Stream(C, "sp", wb2, [wB[2].rearrange("p (dt c) -> p dt c", dt=NDT) for _ in order], r_extra=(wc,))
        for n, (kd, t) in enumerate(order):
            xt, xres = xs.get(n)
            front(xt, xres, False)
            make_rope("x" if kd == "x" else ("s" if isS else "p"), t)
            wt, wres = ws.get(n)
            bk = psL[n % 2]
            lin_tok(wt, wres, bk, hT, hres)
            qk_norm_rope(bk, 2, knw)
            C.op("act", lambda e, bk=bk, n=n: e.activation(out=V1[:, n, :, 0:128], in_=banks[bk][:, 256:512].rearrange("p (h d) -> p h d", h=2), func=AF.Copy, scale=stat[:, 0:1]),
                 r=(bres[bk], stres), w=(kvres,))
            tb = psT[n % 2]
            tbv = banks[tb][:, 0:128].bitcast(BF16)
            for h in range(2):
                C.op("pe", lambda e, h=h, tbv=tbv: e.transpose(tbv[:, h * 128:(h + 1) * 128], qr[:, h, :], identb[:]), r=(qres, cst), w=(bres[tb],))
            C.op("dve", lambda e, tbv=tbv, n=n: e.tensor_copy(out=KT[:, :, n * 128:(n + 1) * 128], in_=tbv.rearrange("p (h t) -> p h t", h=2)), r=(bres[tb],), w=(kvres,))
        xs = Stream(C, "sp", xts, [job["xo"][t * 128:(t + 1) * 128, :] for t in range(n_own)])
        srcs = []
        for _ in range(n_own):
            srcs += [wB[b].rearrange("p (dt c) -> p dt c", dt=NDT) for b in (0, 1, 3, 4)]
            srcs += [wO[b].rearrange("p (dt c) -> p dt c", dt=NDT) for b in range(4)]
        ws = Stream(C, "sp", wb2, srcs, r_extra=(wc,))
        pt_i = 0
        for t in range(n_own):
            xt, xres = xs.get(t)
            front(xt, xres, False)
            make_rope("s" if isS else "p", t)
            for qb in range(2):
                wt, wres = ws.get(8 * t + qb)
                bk = psL[qb]
                lin_tok(wt, wres, bk, hT, hres)
                qk_norm_rope(bk, 4, qnw)
                tb = psT[qb]
                tbv = banks[tb][:, 0:256].bitcast(BF16)
                for h in range(4):
                    C.op("pe", lambda e, h=h, tbv=tbv: e.transpose(tbv[:, h * 128:(h + 1) * 128], qr[:, h, :], identb[:]), r=(qres, cst), w=(bres[tb],))
                C.op("dve", lambda e, tbv=tbv, qb=qb: e.tensor_copy(out=QT[:, qb, :], in_=tbv), r=(bres[tb],), w=(qtres,))
            for g_ in range(2):
                wt, wres = ws.get(8 * t + 2 + g_)
                bk = psL[g_]
                lin_tok(wt, wres, bk, hT, hres)
                C.op("act", lambda e, bk=bk, g_=g_: e.activation(out=ga[:, g_ * 512:(g_ + 1) * 512], in_=banks[bk][:, :], func=AF.Silu, scale=stat[:, 0:1]),
                     r=(bres[bk], stres), w=(gares,))
            for kvh in range(2):
                for st in range(nst):
                    sbk = psS[st % 2]
                    C.op("pe", lambda e, sbk=sbk, st=st, kvh=kvh: e.matmul(banks[sbk][:, :], lhsT=KT[:, kvh, st * 128:(st + 1) * 128], rhs=QT[:, kvh, :], start=True, stop=True),
                         r=(kvres, qtres), w=(bres[sbk],))
                    pt, ptres = PTs[pt_i % 3]; pt_i += 1
                    C.op("act", lambda e, sbk=sbk, pt=pt: e.activation(out=pt[:], in_=banks[sbk][:, :], func=AF.Exp, scale=SCALE), r=(bres[sbk],), w=(ptres,))
                    for h in range(4):
                        ob = psO[h // 2]; off = (h % 2) * 129
                        C.op("pe", lambda e, ob=ob, off=off, h=h, pt=pt, st=st, kvh=kvh: e.matmul(
                            banks[ob][:, off:off + 129], lhsT=pt[:, h * 128:(h + 1) * 128], rhs=V1[:, st, kvh, 0:129],
                            start=(st == 0 and h % 2 == 0), stop=(st == nst - 1), skip_group_check=True), r=(ptres, kvres), w=(bres[ob],))
                for h in range(4):
                    ob = psO[h // 2]; off = (h % 2) * 129; head = 4 * kvh + h
                    C.op("dve", lambda e, ob=ob, off=off, h=h: e.reciprocal(out=stat[:, 5 + (h % 2):6 + (h % 2)], in_=banks[ob][:, off + 128:off + 129]), r=(bres[ob],), w=(stres,))
                    C.op("dve", lambda e, ob=ob, off=off, h=h, head=head: e.scalar_tensor_tensor(
                        out=oa[:, head * 128:(head + 1) * 128], in0=banks[ob][:, off:off + 128], scalar=stat[:, 5 + (h % 2):6 + (h % 2)],
                        in1=ga[:, head * 128:(head + 1) * 128], op0=ALU.mult, op1=ALU.mult), r=(bres[ob], gares, stres), w=(oares,))
            dump("KT_%s" % job["kind"], KT[:], kvres); dump("V1_%s" % job["kind"], V1[:], kvres); dump("QT_%s%d" % (job["kind"], t), QT[:], qtres)
            dump("ga_%s%d" % (job["kind"], t), ga[:], gares); dump("oa_%s%d" % (job["kind"], t), oa[:], oares)
            C.op("act", lambda e: e.activation(out=Mb[:], in_=oa[:], func=AF.Square, accum_out=stat[:, 7:8]), r=(oares,), w=(mtres, stres))
            C.op("dve", lambda e: e.tensor_scalar(out=stat[:, 7:8], in0=stat[:, 7:8], scalar1=1.0 / 1024, scalar2=EPS, op0=ALU.mult, op1=ALU.add), w=(stres,))
            C.op("act", lambda e: e.activation(out=stat[:, 7:8], in_=stat[:, 7:8], func=AF.Sqrt), w=(stres,))
            C.op("dve", lambda e: e.reciprocal(out=stat[:, 7:8], in_=stat[:, 7:8]), w=(stres,))
            C.op("dve", lambda e: e.scalar_tensor_tensor(out=Mb[:], in0=oa[:], scalar=stat[:, 7:8], in1=anw[:], op0=ALU.mult, op1=ALU.mult), r=(oares, ld), w=(mtres,))
            for q in range(2):
                tb = psT[q]
                tbv = banks[tb][:, 0:256].bitcast(BF16)
                for kk in range(4):
                    et = 4 * q + kk
                    C.op("pe", lambda e, tbv=tbv, kk=kk, et=et: e.transpose(tbv[:, kk * 128:(kk + 1) * 128], Mb[:, et * 128:(et + 1) * 128], identb[:]), r=(mtres, cst), w=(bres[tb],))
                C.op("dve", lambda e, tbv=tbv, q=q: e.tensor_copy(out=MT[:, 4 * q:4 * q + 4, :], in_=tbv.rearrange("p (a t) -> p a t", a=4)), r=(bres[tb],), w=(mtres,))
            tok = job["tok0"] + t * 128
            C.dma("sp", MT[:, 8:16, :], ms_d[:, :, tok:tok + 128].rearrange("e p t -> p e t"), w=(mt2res,))
            dump("MT_%s%d" % (job["kind"], t), MT[:], mt2res)
            yo, yres_ = youts[t % 2]
            for cb in range(4):
                wt, wres = ws.get(8 * t + 4 + cb)
                bk = psL[cb % 2]
                for dt_ in range(NDT):
                    C.op("pe", lambda e, dt_=dt_, bk=bk, wt=wt: e.matmul(banks[bk][:, :], lhsT=MT[:, dt_, :], rhs=wt[:, dt_, :], start=(dt_ == 0), stop=(dt_ == NDT - 1)),
                         r=(wres, mtres, mt2res), w=(bres[bk],))
                C.op("dve", lambda e, bk=bk, cb=cb, yo=yo, xt=xt: e.tensor_tensor(out=yo[:, cb * 512:(cb + 1) * 512], in0=banks[bk][:, :], in1=xt[:, cb * 512:(cb + 1) * 512], op=ALU.add),
                     r=(bres[bk], xres), w=(yres_,))
            C.dma("sp", y_all[tok:tok + 128, :], yo[:], r=(yres_,))
    C.barrier()
    return nc


_CACHE = {}


def _get_nc(cfg_key, cfg):
    if cfg_key not in _CACHE:
        _CACHE[cfg_key] = build(cfg)
    return _CACHE[cfg_key]


def run_layer(x_prompt, x_sample, weights, n_cores=8, debug=False):
    B, LP, _ = x_prompt.shape
    BS, LS, _ = x_sample.shape
    NP = B // n_cores
    half = LS // 2
    cfg = dict(np=NP, lp=LP, lso=half, lsx=half, debug=debug)
    nc = build(cfg)
    in_maps = []
    for c in range(n_cores):
        seq, h = c // 2, c % 2
        own = x_sample[seq, h * half:(h + 1) * half]
        oth = x_sample[seq, (1 - h) * half:(2 - h) * half]
        if h == 0:
            oth = oth[::-1]
        m = {"x_p": np.ascontiguousarray(x_prompt[c * NP:(c + 1) * NP].reshape(NP * LP, D)),
             "x_so": np.ascontiguousarray(own), "x_sx": np.ascontiguousarray(oth),
             "flag": np.tile(np.array([[h, 1 - h, 2 * h - 1, 0]], np.float32), (128, 1))}
        for k, v in weights.items():
            m[k] = np.ascontiguousarray(v, dtype=np.float32)
        in_maps.append(m)
    res = run_bass_kernel_spmd(nc, in_maps, core_ids=list(range(n_cores)))
    y_p = np.empty_like(x_prompt)
    y_s = np.empty_like(x_sample)
    for c in range(n_cores):
        seq, h = c // 2, c % 2
        ya = res.results[c]["y_all"]
        y_p[c * NP:(c + 1) * NP] = ya[:NP * LP].reshape(NP, LP, D)
        y_s[seq, h * half:(h + 1) * half] = ya[NP * LP:]
    if debug:
        return y_p, y_s, res.results
    return y_p, y_s


def kernel(x_prompt, x_sample, **weights):
    x_prompt = np.asarray(x_prompt, dtype=np.float32)
    x_sample = np.asarray(x_sample, dtype=np.float32)
    weights = {k: np.asarray(v, dtype=np.float32) for k, v in weights.items()}
    return run_layer(x_prompt, x_sample, weights, n_cores=8)
```

```python
import math
import numpy as np
import concourse.bass as bass
import concourse.mybir as mybir
from concourse.bass_utils import run_bass_kernel_spmd

F32 = mybir.dt.float32
BF16 = mybir.dt.bfloat16
I32 = mybir.dt.int32
AF = mybir.ActivationFunctionType
ALU = mybir.AluOpType

D = 2048
NDT = 16
IN_COLS = 4608
EPS = 1e-6
TWO_PI_S = 6.283185
TWO_PI = 2.0 * math.pi
C_Q, C_K, C_V, C_GA, C_U, C_GS = 0, 1024, 1280, 1536, 2560, 3584
AFREE = 2 * 2 * 32 * 65


class Res:
    __slots__ = ("name", "w", "rd", "sem", "cnt")

    def __init__(self, name):
        self.name = name
        self.w = {}
        self.rd = {}
        self.sem = None
        self.cnt = 0


class Ctx:
    def __init__(self, nc):
        self.nc = nc
        self.eng = {"pe": nc.tensor, "act": nc.scalar, "dve": nc.vector, "pool": nc.gpsimd, "sp": nc.sync}
        self.sem = {k: nc.alloc_semaphore("sem_" + k) for k in self.eng}
        self.cnt = {k: 0 for k in self.eng}
        self.waited = {k: {} for k in self.eng}
        self.nres = 0
        self.allres = []

    def res(self, name=None):
        self.nres += 1
        r = Res(name or "r%d" % self.nres)
        self.allres.append(r)
        return r

    def _wait(self, eng, tok, same_ok=True):
        sem, val = tok
        if sem is self.sem[eng] and not same_ok:
            return
        k = id(sem)
        if self.waited[eng].get(k, 0) >= val:
            return
        self.eng[eng].wait_ge(sem, val)
        self.waited[eng][k] = val

    def _deps(self, eng, r, w, wo=()):
        for x in r:
            for t in x.w.values():
                self._wait(eng, t)
        for x in w:
            for t in x.w.values():
                self._wait(eng, t)
            for t in x.rd.values():
                self._wait(eng, t, same_ok=False)
        for x in wo:
            for t in x.w.values():
                self._wait(eng, t, same_ok=False)
            for t in x.rd.values():
                self._wait(eng, t, same_ok=False)

    def _mark(self, tok, r, w):
        k = id(tok[0])
        for x in r:
            if x in w:
                continue
            x.rd[k] = tok
        for x in w:
            x.w = {k: tok}
            x.rd = {}

    def op(self, eng, fn, r=(), w=(), wo=()):
        self._deps(eng, r, w, wo)
        ins = fn(self.eng[eng])
        self.cnt[eng] += 1
        ins.then_inc(self.sem[eng], 1)
        self._mark((self.sem[eng], self.cnt[eng]), r, tuple(w) + tuple(wo))

    def dma(self, q, out, in_, r=(), w=(), **kw):
        self._deps(q, r, w)
        own = w[0] if w else r[0]
        if own.sem is None:
            own.sem = self.nc.alloc_semaphore("dsem_" + own.name)
        ins = self.eng[q].dma_start(out=out, in_=in_, **kw)
        own.cnt += 16
        ins.then_inc(own.sem, 16)
        self._mark((own.sem, own.cnt), r, w)

    def barrier(self):
        toks = [(self.sem[k], self.cnt[k]) for k in self.eng if self.cnt[k] > 0]
        for x in self.allres:
            if x.sem is not None and x.cnt > 0:
                toks.append((x.sem, x.cnt))
        for e in self.eng:
            for t in toks:
                self._wait(e, t)


class Stream:
    def __init__(self, C, q, slots, srcs, r_extra=(), **kw):
        self.C, self.q, self.slots, self.srcs, self.r_extra, self.kw = C, q, slots, srcs, tuple(r_extra), kw
        self.issued = 0

    def get(self, k):
        n = len(self.slots)
        while self.issued < min(len(self.srcs), k + n):
            j = self.issued
            t, res = self.slots[j % n]
            src = self.srcs[j]
            view = None
            if isinstance(src, tuple):
                src, view = src
            dst = view(t) if view else t[:]
            for x in self.r_extra:
                for tok in x.w.values():
                    self.C._wait(self.q, tok)
            self.C.dma(self.q, dst, src, w=(res,), **self.kw)
            self.issued += 1
        return self.slots[k % n]


def dram_ap(t, offset, pat):
    return bass.AP(t.tensor, offset, [list(p) for p in pat])


def build(cfg):
    NP, LP, LSO, LSX = cfg["np"], cfg["lp"], cfg["lso"], cfg["lsx"]
    nc = bass.Bass("TRN2", target_bir_lowering=False)
    C = Ctx(nc)
    din = lambda name, shape: nc.dram_tensor(name, list(shape), F32, kind="ExternalInput").ap()
    x_p = din("x_p", [NP * LP, D])
    x_so = din("x_so", [LSO, D])
    x_sx = din("x_sx", [max(LSX, 128), D])
    flag_d = din("flag", [128, 4])
    norm_w = din("norm_w", [1, D])
    w_in = din("w_in", [1, D, IN_COLS])
    q_norm_w = din("q_norm_w", [1, 128])
    k_norm_w = din("k_norm_w", [1, 128])
    lambda_re = din("lambda_re", [1, 2, 64, 64])
    lambda_im = din("lambda_im", [1, 2, 64, 64])
    log_step = din("log_step", [1, 2, 64])
    b_re = din("b_re", [1, 64, 64, 16])
    b_im = din("b_im", [1, 64, 64, 16])
    c_re = din("c_re", [1, 2, 64, 16, 64])
    c_im = din("c_im", [1, 2, 64, 16, 64])
    d_skip = din("d_skip", [1, 1024])
    w_glu = din("w_glu", [1, 1024, 1024])
    attn_out_norm_w = din("attn_out_norm_w", [1, 1024])
    ssm_out_norm_w = din("ssm_out_norm_w", [1, 1024])
    w_out = din("w_out", [1, D, D])
    NTOK = NP * LP + LSO
    y_all = nc.dram_tensor("y_all", [NTOK, D], F32, kind="ExternalOutput").ap()
    wA = nc.dram_tensor("wA", [8, 128, NDT * 256], BF16, kind="Internal").ap()
    wB = nc.dram_tensor("wB", [5, 128, NDT * 512], BF16, kind="Internal").ap()
    wO = nc.dram_tensor("wO", [4, 128, NDT * 512], BF16, kind="Internal").ap()
    wG = nc.dram_tensor("wG", [4, 128, 8 * 256], BF16, kind="Internal").ap()
    ms_d = nc.dram_tensor("ms_scr", [8, 128, NTOK], BF16, kind="Internal").ap()

    jobs = []
    for j in range(NP):
        jobs.append(dict(kind="P", xo=x_p[j * LP:(j + 1) * LP, :], xx=None, n_own=LP // 128, n_oth=0, tok0=j * LP))
    jobs.append(dict(kind="S", xo=x_so, xx=x_sx, n_own=LSO // 128, n_oth=LSX // 128, tok0=NP * LP))
    MAXU = max(j["n_own"] for j in jobs) * 2
    MAXS = max(j["n_own"] + j["n_oth"] for j in jobs)

    sb = lambda name, shape, dt=F32: nc.alloc_sbuf_tensor(name, list(shape), dt)
    dbg = cfg.get("debug", False)
    dumped = {}

    def dump(name, ap, res):
        if not dbg or name in dumped:
            return
        shp = list(ap.shape)
        t_ = nc.dram_tensor("dbg_" + name, shp, ap.dtype, kind="ExternalOutput").ap()
        dumped[name] = t_
        C.dma("sp", t_, ap, r=(res,))
    banks = [nc.alloc_psum_tensor("bank%d" % i, [128, 512], F32) for i in range(8)]
    bres = [C.res("bank%d" % i) for i in range(8)]

    def bcast(ap, shape, axis):
        return ap.unsqueeze(axis).broadcast_to(list(shape))

    wc = C.res("wcast")
    for b in range(8):
        c0 = (C_U + 256 * b) if b < 4 else (C_GS + 256 * (b - 4))
        C.dma("pool", wA[b].rearrange("p (dt c) -> p dt c", dt=NDT), w_in[0, :, c0:c0 + 256].rearrange("(dt p) c -> p dt c", p=128), w=(wc,))
    for b, c0 in enumerate((C_Q, C_Q + 512, C_K, C_GA, C_GA + 512)):
        C.dma("pool", wB[b].rearrange("p (dt c) -> p dt c", dt=NDT), w_in[0, :, c0:c0 + 512].rearrange("(dt p) c -> p dt c", p=128), w=(wc,))
    for b in range(4):
        C.dma("pool", wO[b].rearrange("p (dt c) -> p dt c", dt=NDT), w_out[0, :, 512 * b:512 * b + 512].rearrange("(dt p) c -> p dt c", p=128), w=(wc,))
        C.dma("pool", wG[b].rearrange("p (et c) -> p et c", et=8), w_glu[0, :, 256 * b:256 * b + 256].rearrange("(et p) c -> p et c", p=128), w=(wc,))

    cst = C.res("const")
    ident = sb("ident", [128, 128]); identb = sb("identb", [128, 128], BF16)
    ones_t = sb("ones_t", [128, 128]); onesb = sb("onesb", [128, 128], BF16)
    C.op("pool", lambda e: e.memset(ones_t[:], 1.0), w=(cst,))
    C.op("pool", lambda e: e.memset(onesb[:], 1.0), w=(cst,))
    C.op("pool", lambda e: e.affine_select(out=ident[:], in_=ones_t[:], pattern=[[-1, 128]], compare_op=ALU.is_equal,
                                           fill=0.0, base=0, channel_multiplier=1), w=(cst,))
    C.op("dve", lambda e: e.tensor_copy(out=identb[:], in_=ident[:]), r=(), w=(cst,))
    flag = sb("flag_sb", [128, 4]); nwcol = sb("nwcol", [128, NDT])
    qnw = sb("qnw", [128, 128]); knw = sb("knw", [128, 128])
    dcol = sb("dcol", [128, 8]); snwcol = sb("snwcol", [128, 8])
    mhalf = sb("mhalf", [128, 1])
    C.op("pool", lambda e: e.memset(mhalf[:], -0.5), w=(cst,))
    ld = C.res("cload")
    C.dma("sp", flag[:], flag_d[:, :], w=(ld,))
    C.dma("sp", nwcol[:], dram_ap(norm_w, 0, [[1, 128], [128, NDT]]), w=(ld,), allow_slow_non_contiguous=True)
    C.dma("sp", qnw[:], dram_ap(q_norm_w, 0, [[0, 128], [1, 128]]), w=(ld,))
    C.dma("sp", knw[:], dram_ap(k_norm_w, 0, [[0, 128], [1, 128]]), w=(ld,))
    C.dma("sp", dcol[:], dram_ap(d_skip, 0, [[1, 128], [128, 8]]), w=(ld,), allow_slow_non_contiguous=True)
    C.dma("sp", snwcol[:], dram_ap(ssm_out_norm_w, 0, [[1, 128], [128, 8]]), w=(ld,), allow_slow_non_contiguous=True)

    fq = sb("fq", [128, 32]); jt = sb("jt", [128, 32]); pidx = sb("pidx", [128, 1])
    hicol = sb("hicol", [128, 1]); colpos = sb("colpos", [128, 1])
    C.op("pool", lambda e: e.iota(jt[:], pattern=[[1, 32]], base=0, channel_multiplier=0, allow_small_or_imprecise_dtypes=True), w=(cst,))
    C.op("pool", lambda e: e.iota(pidx[:], pattern=[[0, 1]], base=0, channel_multiplier=1, allow_small_or_imprecise_dtypes=True), w=(cst,))
    C.op("act", lambda e: e.activation(out=fq[:], in_=jt[:], func=AF.Exp, scale=-math.log(10000.0) / 32.0), w=(cst,))
    C.op("dve", lambda e: e.tensor_scalar_mul(out=fq[:], in0=fq[:], scalar1=1.0 / TWO_PI), w=(cst,))
    C.op("dve", lambda e: e.tensor_single_scalar(out=hicol[:], in_=pidx[:], scalar=64.0, op=ALU.is_ge), w=(cst,))
    C.op("dve", lambda e: e.scalar_tensor_tensor(out=colpos[:], in0=hicol[:], scalar=-64.0, in1=pidx[:], op0=ALU.mult, op1=ALU.add), w=(cst,))

    def cyc_sin(eng_w, dst, q, tf, ti, shift=0.0):
        if shift != 0.0:
            C.op("dve", lambda e: e.tensor_scalar_add(out=q, in0=q, scalar1=shift), w=eng_w)
        C.op("dve", lambda e: e.tensor_copy(out=ti, in_=q), w=eng_w)
        C.op("dve", lambda e: e.tensor_copy(out=tf, in_=ti), w=eng_w)
        C.op("dve", lambda e: e.tensor_sub(out=q, in0=q, in1=tf), w=eng_w)
        C.op("act", lambda e: e.activation(out=dst, in_=q, func=AF.Sin, scale=TWO_PI_S), w=eng_w)

    xts = [(sb("xt%d" % i, [128, D]), C.res("xt%d" % i)) for i in range(2)]
    hT = sb("hT", [128, NDT, 128], BF16); hres = C.res("hT")
    stat = sb("stat", [128, 8]); stres = C.res("stat")
    dg = sb("dg", [128, 128]); rrep = sb("rrep", [128, 128]); rres = C.res("rrep")
    import contextlib
    ssm_stack = contextlib.ExitStack()
    ssb = lambda name, shape, dt=F32: ssm_stack.enter_context(nc.sbuf_tensor(name, list(shape), dt))
    Bp = ssb("Bp", [128, 32, 2, 2, 128], BF16)
    Cp = ssb("Cp", [128, 32, 2, 2, 128], BF16)
    Lr2 = ssb("Lr2", [128, 2, 2, 32]); LiN = ssb("LiN", [128, 2, 32]); LiP = ssb("LiP", [128, 2, 32])
    Ls_r = ssb("Ls_r", [128, 2, 32]); Ls_n = ssb("Ls_n", [128, 32]); Ls_p = ssb("Ls_p", [128, 32])
    LiS = ssb("LiS", [128, 2, 2, 32]); Ls_s = ssb("Ls_s", [128, 2, 32])
    sres = C.res("ssmw")
    with contextlib.ExitStack() as st_:
        lre = st_.enter_context(nc.sbuf_tensor("lre", [128, 32, 2], F32))
        lim = st_.enter_context(nc.sbuf_tensor("lim", [128, 32, 2], F32))
        lst = st_.enter_context(nc.sbuf_tensor("lst", [128, 32, 2], F32))
        Br = st_.enter_context(nc.sbuf_tensor("Br", [128, 32, 16], F32))
        Bi = st_.enter_context(nc.sbuf_tensor("Bi", [128, 32, 16], F32))
        Cn = st_.enter_context(nc.sbuf_tensor("Cn", [128, 2, 2, 8, 2, 64], F32))
        s1 = st_.enter_context(nc.sbuf_tensor("s1", [128, 32, 2], F32))
        s2 = st_.enter_context(nc.sbuf_tensor("s2", [128, 32, 2], F32))
        s3 = st_.enter_context(nc.sbuf_tensor("s3", [128, 32, 2], F32))
        si = st_.enter_context(nc.sbuf_tensor("si", [128, 32, 2], I32))
        lbr = st_.enter_context(nc.sbuf_tensor("lbr", [128, 32, 2], F32))
        lbi = st_.enter_context(nc.sbuf_tensor("lbi", [128, 32, 2], F32))
        bsr = st_.enter_context(nc.sbuf_tensor("bsr", [128, 32, 2], F32))
        bsi = st_.enter_context(nc.sbuf_tensor("bsi", [128, 32, 2], F32))
        Bpr = st_.enter_context(nc.sbuf_tensor("Bpr", [128, 32, 2, 2, 16], F32))
        t16 = st_.enter_context(nc.sbuf_tensor("t16", [128, 32, 2, 16], F32))
        Mc = st_.enter_context(nc.sbuf_tensor("Mc", [128, 4, 8, 16], F32))
        onesm = st_.enter_context(nc.sbuf_tensor("onesm", [128, 4, 8, 16], F32))
        pre = st_.enter_context(nc.sbuf_tensor("pre", [128, 4, 128], F32))
        Trep = st_.enter_context(nc.sbuf_tensor("Trep", [128, 2, 8, 2, 128], F32))
        for gp in range(2):
            ps_ = slice(64 * gp, 64 * gp + 64)
            for d in range(2):
                C.dma("sp", lre[ps_, :, d], dram_ap(lambda_re, gp * 64 + d * 4096, [[1, 64], [128, 32]]), w=(ld,), allow_slow_non_contiguous=True)
                C.dma("sp", lim[ps_, :, d], dram_ap(lambda_im, gp * 64 + d * 4096, [[1, 64], [128, 32]]), w=(ld,), allow_slow_non_contiguous=True)
                C.dma("sp", lst[ps_, :, d], dram_ap(log_step, gp + d * 64, [[0, 64], [2, 32]]), w=(ld,), allow_slow_non_contiguous=True)
            C.dma("sp", Br[ps_, :, :], dram_ap(b_re, gp * 1024, [[16, 64], [2048, 32], [1, 16]]), w=(ld,))
            C.dma("sp", Bi[ps_, :, :], dram_ap(b_im, gp * 1024, [[16, 64], [2048, 32], [1, 16]]), w=(ld,))
        for ri, csrc in enumerate((c_re, c_im)):
            for dup in range(2):
                for d in range(2):
                    C.dma("sp", Cn[:, ri, d, :, dup, :], dram_ap(csrc, d * 65536, [[64, 128], [8192, 8], [1, 64]]), w=(ld,))
        W = (sres,)
        R = (ld, cst)
        C.op("act", lambda e: e.activation(out=s1[:], in_=lst[:], func=AF.Exp), r=R, w=W)
        C.op("dve", lambda e: e.tensor_mul(out=s2[:], in0=lre[:], in1=s1[:]), r=R, w=W)
        C.op("dve", lambda e: e.scalar_tensor_tensor(out=s3[:], in0=lim[:], scalar=1.0 / TWO_PI, in1=s1[:], op0=ALU.mult, op1=ALU.mult), r=R, w=W)
        C.op("act", lambda e: e.activation(out=s2[:], in_=s2[:], func=AF.Exp), w=W)
        C.op("dve", lambda e: e.tensor_copy(out=lbi[:], in_=s3[:]), w=W)
        cyc_sin(W, lbi[:], lbi[:], s1[:], si[:])
        cyc_sin(W, lbr[:], s3[:], s1[:], si[:], shift=0.25)
        C.op("dve", lambda e: e.tensor_mul(out=lbr[:], in0=lbr[:], in1=s2[:]), w=W)
        C.op("dve", lambda e: e.tensor_mul(out=lbi[:], in0=lbi[:], in1=s2[:]), w=W)
        C.op("dve", lambda e: e.tensor_mul(out=s1[:], in0=lre[:], in1=lre[:]), w=W)
        C.op("dve", lambda e: e.tensor_mul(out=s2[:], in0=lim[:], in1=lim[:]), w=W)
        C.op("dve", lambda e: e.tensor_add(out=s1[:], in0=s1[:], in1=s2[:]), w=W)
        C.op("dve", lambda e: e.reciprocal(out=s1[:], in_=s1[:]), w=W)
        C.op("dve", lambda e: e.tensor_scalar_add(out=s2[:], in0=lbr[:], scalar1=-1.0), w=W)
        C.op("dve", lambda e: e.tensor_mul(out=bsr[:], in0=s2[:], in1=lre[:]), w=W)
        C.op("dve", lambda e: e.tensor_mul(out=s3[:], in0=lbi[:], in1=lim[:]), w=W)
        C.op("dve", lambda e: e.tensor_add(out=bsr[:], in0=bsr[:], in1=s3[:]), w=W)
        C.op("dve", lambda e: e.tensor_mul(out=bsr[:], in0=bsr[:], in1=s1[:]), w=W)
        C.op("dve", lambda e: e.tensor_mul(out=bsi[:], in0=lbi[:], in1=lre[:]), w=W)
        C.op("dve", lambda e: e.tensor_mul(out=s3[:], in0=s2[:], in1=lim[:]), w=W)
        C.op("dve", lambda e: e.tensor_sub(out=bsi[:], in0=bsi[:], in1=s3[:]), w=W)
        C.op("dve", lambda e: e.tensor_mul(out=bsi[:], in0=bsi[:], in1=s1[:]), w=W)
        for d in range(2):
            for r in range(2):
                C.op("dve", lambda e, d=d, r=r: e.tensor_copy(out=Lr2[:, d, r, :], in_=lbr[:, :, d]), w=W)
            C.op("dve", lambda e, d=d: e.tensor_copy(out=LiP[:, d, :], in_=lbi[:, :, d]), w=W)
            C.op("dve", lambda e, d=d: e.tensor_scalar_mul(out=LiN[:, d, :], in0=lbi[:, :, d], scalar1=-1.0), w=W)
        for r in range(2):
            C.op("dve", lambda e, r=r: e.tensor_scalar_mul(out=Ls_r[:, r, :], in0=lbr[:, :, 0], scalar1=flag[:, 0:1]), w=W)
            C.op("dve", lambda e, r=r: e.scalar_tensor_tensor(out=Ls_r[:, r, :], in0=lbr[:, :, 1], scalar=flag[:, 1:2], in1=Ls_r[:, r, :],
                                                              op0=ALU.mult, op1=ALU.add), w=W)
        C.op("dve", lambda e: e.tensor_scalar_mul(out=Ls_p[:], in0=lbi[:, :, 0], scalar1=flag[:, 0:1]), w=W)
        C.op("dve", lambda e: e.scalar_tensor_tensor(out=Ls_p[:], in0=lbi[:, :, 1], scalar=flag[:, 1:2], in1=Ls_p[:], op0=ALU.mult, op1=ALU.add), w=W)
        C.op("dve", lambda e: e.tensor_scalar_mul(out=Ls_n[:], in0=Ls_p[:], scalar1=-1.0), w=W)
        for d in range(2):
            C.op("dve", lambda e, d=d: e.tensor_copy(out=LiS[:, d, 0, :], in_=LiN[:, d, :]), w=W)
            C.op("dve", lambda e, d=d: e.tensor_copy(out=LiS[:, d, 1, :], in_=LiP[:, d, :]), w=W)
        C.op("dve", lambda e: e.tensor_copy(out=Ls_s[:, 0, :], in_=Ls_n[:]), w=W)
        C.op("dve", lambda e: e.tensor_copy(out=Ls_s[:, 1, :], in_=Ls_p[:]), w=W)
        for d in range(2):
            bsr_b = bcast(bsr[:, :, d], [128, 32, 16], 2); bsi_b = bcast(bsi[:, :, d], [128, 32, 16], 2)
            C.op("dve", lambda e, d=d, a=bsr_b: e.tensor_mul(out=Bpr[:, :, d, 0, :], in0=Br[:], in1=a), w=W)
            C.op("dve", lambda e, d=d, a=bsi_b: e.tensor_mul(out=t16[:, :, d, :], in0=Bi[:], in1=a), w=W)
            C.op("dve", lambda e, d=d: e.tensor_sub(out=Bpr[:, :, d, 0, :], in0=Bpr[:, :, d, 0, :], in1=t16[:, :, d, :]), w=W)
            C.op("dve", lambda e, d=d, a=bsr_b: e.tensor_mul(out=Bpr[:, :, d, 1, :], in0=Bi[:], in1=a), w=W)
            C.op("dve", lambda e, d=d, a=bsi_b: e.tensor_mul(out=t16[:, :, d, :], in0=Br[:], in1=a), w=W)
            C.op("dve", lambda e, d=d: e.tensor_add(out=Bpr[:, :, d, 1, :], in0=Bpr[:, :, d, 1, :], in1=t16[:, :, d, :]), w=W)
        C.op("pool", lambda e: e.memset(onesm[:], 1.0), w=W)
        for gp in range(2):
            ps_ = slice(64 * gp, 64 * gp + 64)
            C.op("pool", lambda e, ps_=ps_, gp=gp: e.affine_select(out=Mc[ps_], in_=onesm[ps_], pattern=[[-2, 4], [1, 8], [0, 16]],
                                                                   compare_op=ALU.is_equal, fill=0.0, base=-gp, channel_multiplier=0), w=W)
        k = 0
        for d in range(2):
            for r in range(2):
                for i in range(32):
                    slot = k % 4
                    C.op("dve", lambda e, i=i, d=d, r=r, slot=slot: e.tensor_mul(
                        out=pre[:, slot, :].rearrange("p (g c) -> p g c", g=8), in0=Mc[:, i % 4, :, :],
                        in1=bcast(Bpr[:, i, d, r, :], [128, 8, 16], 1)), w=W)
                    bk = (k // 4) % 2
                    C.op("pe", lambda e, slot=slot, bk=bk: e.transpose(banks[bk][:, slot * 128:(slot + 1) * 128], pre[:, slot, :], ident[:]),
                         r=W + (cst,), w=(bres[bk],))
                    if slot == 3:
                        i0 = i - 3
                        C.op("act", lambda e, bk=bk, i0=i0, d=d, r=r: e.activation(
                            out=Bp[:, i0:i0 + 4, d, r, :], in_=banks[bk][:, :].rearrange("p (a x) -> p a x", a=4), func=AF.Copy),
                            r=(bres[bk],), w=W)
                    k += 1
        C.op("dve", lambda e: e.tensor_scalar_mul(out=Cn[:, 1], in0=Cn[:, 1], scalar1=-1.0), r=R, w=W)
        k = 0
        for d in range(2):
            for gt in range(8):
                for r in range(2):
                    bk = (k // 4) % 2; slot = k % 4
                    C.op("pe", lambda e, bk=bk, slot=slot, r=r, d=d, gt=gt: e.transpose(
                        banks[bk][:, slot * 128:(slot + 1) * 128], Cn[:, r, d, gt, :, :].rearrange("p a b -> p (a b)"), ident[:]),
                        r=W + (cst,), w=(bres[bk],))
                    C.op("act", lambda e, bk=bk, slot=slot, r=r, d=d, gt=gt: e.activation(
                        out=Trep[:, d, gt, r, :], in_=banks[bk][:, slot * 128:(slot + 1) * 128], func=AF.Copy), r=(bres[bk],), w=W)
                    k += 1
        for d in range(2):
            for r in range(2):
                for i in range(32):
                    C.op("dve", lambda e, i=i, d=d, r=r: e.tensor_mul(
                        out=Cp[:, i, d, r, :].rearrange("p (g c) -> p g c", g=8),
                        in0=Trep[:, d, i // 4, r, :].rearrange("p (g c) -> p g c", g=8), in1=Mc[:, i % 4, :, :]), w=W)
        dump("pidx", pidx[:], cst); dump("hicol", hicol[:], cst); dump("colpos", colpos[:], cst); dump("fq", fq[:], cst)
        dump("lbr", lbr[:], sres); dump("lbi", lbi[:], sres); dump("bsr", bsr[:], sres); dump("bsi", bsi[:], sres)
        dump("Bpr", Bpr[:], sres); dump("Mc", Mc[:], sres); dump("Trep", Trep[:], sres)
        dump("Bp", Bp[:], sres); dump("Cp", Cp[:], sres)
        C.barrier()

    UT = 32
    NU = 128 // UT
    ACOL = UT + 1
    RS, DS, IS = 32 * ACOL, 2 * 32 * ACOL, ACOL
    AFR = 2 * DS
    MAXU2 = max(j["n_own"] for j in jobs) * NU
    Abuf = [ssb("A%d" % k, [128, 2, 2, 32, ACOL]) for k in range(2)]
    ares = [C.res("A%d" % k) for k in range(2)]
    T1 = ssb("T1", [128, 2, 2, 32]); T2 = ssb("T2", [128, 2, 2, 32])
    Xbf = ssb("Xbf", [128, 2, 2, 32, UT], BF16)
    bnd = ssb("bnd", [128, MAXU2 + 1, 2, 32], BF16)
    Ef = ssb("Ef", [128, 2, 32]); Sx = ssb("Sx", [128, 2, 32])
    wbs = [(ssb("wb%d" % i, [128, NDT, 256], BF16), C.res("wb%d" % i)) for i in range(3)]
    uTs = [ssb("uT%d" % k, [128, 8, 128], BF16) for k in range(2)]; ures = [C.res("uT%d" % k) for k in range(2)]
    uTx = [ssb("uTx%d" % k, [128, 8, 128], BF16) for k in range(2)]; uxres = [C.res("uTx%d" % k) for k in range(2)]
    gsTs = [ssb("gsT%d" % k, [128, 8, 128], BF16) for k in range(2)]; gres = [C.res("gsT%d" % k) for k in range(2)]
    yT = ssb("yT", [128, 8, 128]); yres = C.res("yT")
    yS = ssb("yS", [128, 8, 128]); ysres = C.res("yS")
    g1 = ssb("g1", [128, 8, 128]); g2 = ssb("g2", [128, 8, 128]); gb = ssb("gb", [128, 8, 128], BF16); g3 = gb
    gwres = C.res("gelu")
    msT = ssb("msT", [128, 8, 128], BF16); mres = C.res("msT")
    xbres = C.res("Xbf"); eres = C.res("carry")
    psT, psL, psB, psY, psM = (0, 1), (2, 3), (4, 5), 6, 7
    junk = hT[:].rearrange("p a b -> p (a b)")

    def a_ap(ab, off, pat):
        return bass.AP(Abuf[ab][:].tensor, off, [[AFR, 128]] + [list(p) for p in pat])

    def front(xt, xres, need_rep):
        C.op("act", lambda e: e.activation(out=junk, in_=xt[:], func=AF.Square, accum_out=stat[:, 0:1]), r=(xres,), w=(hres, stres))
        C.op("dve", lambda e: e.tensor_scalar(out=stat[:, 0:1], in0=stat[:, 0:1], scalar1=1.0 / D, scalar2=EPS, op0=ALU.mult, op1=ALU.add), w=(stres,))
        C.op("act", lambda e: e.activation(out=stat[:, 0:1], in_=stat[:, 0:1], func=AF.Sqrt), w=(stres,))
        C.op("dve", lambda e: e.reciprocal(out=stat[:, 0:1], in_=stat[:, 0:1]), w=(stres,))
        for q in range(4):
            bk = psT[q % 2]
            for kk in range(4):
                dt_ = 4 * q + kk
                C.op("pe", lambda e, bk=bk, kk=kk, dt_=dt_: e.transpose(banks[bk][:, kk * 128:(kk + 1) * 128], xt[:, dt_ * 128:(dt_ + 1) * 128], ident[:]),
                     r=(xres, cst), w=(bres[bk],))
            C.op("dve", lambda e, bk=bk, q=q: e.tensor_tensor(out=hT[:, 4 * q:4 * q + 4, :], in0=banks[bk][:, :].rearrange("p (a t) -> p a t", a=4),
                                                              in1=bcast(nwcol[:, 4 * q:4 * q + 4], [128, 4, 128], 2), op=ALU.mult),
                 r=(bres[bk], ld), w=(hres,))
        if need_rep:
            C.op("dve", lambda e: e.tensor_scalar_mul(out=dg[:], in0=ident[:], scalar1=stat[:, 0:1]), r=(stres, cst), w=(rres,))
            C.op("pe", lambda e: e.matmul(banks[psM][:, 0:128], lhsT=ones_t[:], rhs=dg[:], start=True, stop=True), r=(rres, cst), w=(bres[psM],))
            C.op("act", lambda e: e.activation(out=rrep[:], in_=banks[psM][:, 0:128], func=AF.Copy), r=(bres[psM],), w=(rres,))

    def front_n(xt, xres, split=False):
        C.op("act", lambda e: e.activation(out=junk, in_=xt[:], func=AF.Square, accum_out=stat[:, 0:1]), r=(xres,), w=(hres, stres))
        C.op("pool", lambda e: e.tensor_scalar(out=stat[:, 0:1], in0=stat[:, 0:1], scalar1=1.0 / D, scalar2=EPS, op0=ALU.mult, op1=ALU.add), w=(stres,))
        C.op("pool", lambda e: e.tensor_tensor(out=stat[:, 0:1], in0=stat[:, 0:1], in1=mhalf[:], op=ALU.pow), r=(cst,), w=(stres,))
        C.op("act", lambda e: e.activation(out=xt[:], in_=xt[:], func=AF.Copy, scale=stat[:, 0:1]), r=(stres,), w=(xres,))
        if not split:
            front_nB(xt, xres)

    def front_nB(xt, xres):
        for q in range(4):
            bk = psT[q % 2]
            for kk in range(4):
                dt_ = 4 * q + kk
                C.op("pe", lambda e, bk=bk, kk=kk, dt_=dt_: e.transpose(banks[bk][:, kk * 128:(kk + 1) * 128], xt[:, dt_ * 128:(dt_ + 1) * 128], ident[:]),
                     r=(xres, cst), w=(bres[bk],))
            for kk in range(4):
                dt_ = 4 * q + kk
                C.op("act", lambda e, bk=bk, kk=kk, dt_=dt_: e.activation(out=hT[:, dt_, :], in_=banks[bk][:, kk * 128:(kk + 1) * 128], func=AF.Copy, scale=nwcol[:, dt_:dt_ + 1]),
                     r=(bres[bk], ld), w=(hres,))

    def lin_feat(wt, wres, bk, e_off, ne, nk=NDT, rhs=None, rres_=None):
        rhs = hT if rhs is None else rhs
        rres_ = hres if rres_ is None else rres_
        for e_ in range(ne):
            for dt_ in range(nk):
                C.op("pe", lambda e, e_=e_, dt_=dt_: e.matmul(banks[bk][:, (e_off + e_) * 128:(e_off + e_ + 1) * 128], lhsT=wt[:, dt_, e_ * 128:(e_ + 1) * 128],
                                                               rhs=rhs[:, dt_, :], start=(dt_ == 0), stop=(dt_ == nk - 1)),
                     r=(wres, rres_), w=(bres[bk],))

    def wsrc(wbf, c0, n=256):
        return wbf[:, c0:c0 + n].rearrange("(dt p) c -> p dt c", p=128)

    def glu_view(t):
        return t[:, 0:8, :]

    wk = [0]

    def feat_1024(ws, dsts, func=AF.Copy):
        for half in range(2):
            bk = psL[half]
            for b2 in range(2):
                wt, wres = ws.get(wk[0]); wk[0] += 1
                lin_feat(wt, wres, bk, 2 * b2, 2)
            for dst, dres, sc in dsts:
                if sc is None:
                    C.op("act", lambda e, bk=bk, half=half, dst=dst: e.activation(out=dst[:, 4 * half:4 * half + 4, :], in_=banks[bk][:, :].rearrange("p (a t) -> p a t", a=4), func=func),
                         r=(bres[bk],), w=(dres,))
                else:
                    C.op("act", lambda e, bk=bk, half=half, dst=dst, sc=sc: e.activation(out=dst[:, 4 * half:4 * half + 4, :], in_=banks[bk][:, :].rearrange("p (a t) -> p a t", a=4), func=func, scale=sc),
                         r=(bres[bk], ld), w=(dres,))

    bk_tog = [0]

    def bmm(ab, uT_, ures_, tc0, d, dslot, col0, mode, uT_b=None, ures_b=None):
        PB = 512 // UT
        for r in range(2):
            for pb in range(32 // PB):
                bk = psB[bk_tog[0] % 2]; bk_tog[0] += 1
                for ii in range(PB):
                    i = pb * PB + ii
                    if mode == "set":
                        C.op("pe", lambda e, bk=bk, ii=ii, i=i, r=r: e.matmul(banks[bk][:, ii * UT:(ii + 1) * UT], lhsT=Bp[:, i, d, r, :],
                                                                              rhs=uT_[:, i // 4, tc0:tc0 + UT], start=True, stop=True),
                             r=(sres, ures_), w=(bres[bk],))
                    else:
                        C.op("pe", lambda e, bk=bk, ii=ii, i=i, r=r: e.matmul(banks[bk][:, ii * UT:(ii + 1) * UT], lhsT=Bp[:, i, 0, r, :],
                                                                              rhs=uT_[:, i // 4, tc0:tc0 + UT], start=True, stop=False),
                             r=(sres, ures_), w=(bres[bk],))
                        C.op("pe", lambda e, bk=bk, ii=ii, i=i, r=r: e.matmul(banks[bk][:, ii * UT:(ii + 1) * UT], lhsT=Bp[:, i, 1, r, :],
                                                                              rhs=uT_b[:, i // 4, tc0:tc0 + UT], start=False, stop=True),
                             r=(sres, ures_b), w=(bres[bk],))
                dst = Abuf[ab][:, dslot, r, pb * PB:(pb + 1) * PB, col0:col0 + UT]
                src = banks[bk][:, :].rearrange("p (a t) -> p a t", a=PB)
                C.op("act", lambda e, dst=dst, src=src: e.activation(out=dst, in_=src, func=AF.Copy), r=(bres[bk],), w=(ares[ab],))

    rT1 = C.res("T1"); rT2a = C.res("T2a"); rT2b = C.res("T2b")

    def step_single(ab, dslot, cur, nxt, lr, lsw):
        base = dslot * DS
        c = a_ap(ab, base + cur, [[RS, 2], [IS, 32]]); n = a_ap(ab, base + nxt, [[RS, 2], [IS, 32]])
        csw = a_ap(ab, base + RS + cur, [[-RS, 2], [IS, 32]])
        t1 = T1[:, 0]; t2 = T2[:, 0]
        A_ = (ares[ab],)
        C.op("dve", lambda e: e.tensor_mul(out=t1, in0=c, in1=lr), r=A_, wo=(rT1,))
        C.op("dve", lambda e: e.tensor_mul(out=t2, in0=csw, in1=lsw), r=A_, wo=(rT2a,))
        C.op("dve", lambda e: e.tensor_add(out=n, in0=n, in1=t1), r=(rT1,), w=A_)
        C.op("dve", lambda e: e.tensor_add(out=n, in0=n, in1=t2), r=(rT2a,), w=A_)

    def step_both(ab, j):
        sc = DS + UT - 2 * j; sn_ = DS + UT - 2 - 2 * j
        c = a_ap(ab, j, [[sc, 2], [RS, 2], [IS, 32]]); n = a_ap(ab, j + 1, [[sn_, 2], [RS, 2], [IS, 32]])
        csw = a_ap(ab, j + RS, [[sc, 2], [-RS, 2], [IS, 32]])
        A_ = (ares[ab],)
        C.op("dve", lambda e: e.tensor_mul(out=T1[:], in0=c, in1=Lr2[:]), r=A_, wo=(rT1,))
        C.op("dve", lambda e: e.tensor_mul(out=T2[:], in0=csw, in1=LiS[:]), r=A_, wo=(rT2a,))
        C.op("dve", lambda e: e.tensor_add(out=n, in0=n, in1=T1[:]), r=(rT1,), w=A_)
        C.op("dve", lambda e: e.tensor_add(out=n, in0=n, in1=T2[:]), r=(rT2a,), w=A_)

    for job in jobs:
        n_own, n_oth = job["n_own"], job["n_oth"]
        isS = job["kind"] == "S"
        JK = job["kind"] + str(job["tok0"])
        order = [("x", t) for t in range(n_oth)] + [("o", t) for t in range(n_own - 1, -1, -1)]
        xsrc = [(job["xx"] if kd == "x" else job["xo"])[t * 128:(t + 1) * 128, :] for kd, t in order]
        xs = Stream(C, "sp", xts, xsrc)
        ws = Stream(C, "sp", wbs, [wA[b].rearrange("p (dt c) -> p dt c", dt=NDT) for _ in order for b in range(4)], r_extra=(wc,))
        wk[0] = 0
        C.op("dve", lambda e: e.memset(Sx[:], 0.0), w=(eres,))
        seq = []
        for n, (kd, t) in enumerate(order):
            for u in (range(NU) if kd == "x" else range(NU - 1, -1, -1)):
                seq.append((n, kd, t, u))

        xslot = {}

        def pre1A(n):
            xslot[n] = xs.get(n)
            front_n(xslot[n][0], xslot[n][1], split=True)

        def pre1B(n):
            front_nB(xslot[n][0], xslot[n][1])

        def pre1(n, staged=False):
            if not staged:
                xt, xres = xs.get(n)
                front_n(xt, xres)
            if order[n][0] == "x":
                feat_1024(ws, [(uTs[n % 2], ures[n % 2], flag[:, 0:1]), (uTx[n % 2], uxres[n % 2], flag[:, 1:2])])
            else:
                feat_1024(ws, [(uTs[n % 2], ures[n % 2], None)])

        def B1(k):
            n, kd, t, u = seq[k]
            ab = k % 2
            if kd == "x":
                bmm(ab, uTs[n % 2], ures[n % 2], u * UT, 0, 0, 1, "acc", uTx[n % 2], uxres[n % 2])
            else:
                bmm(ab, uTs[n % 2], ures[n % 2], u * UT, 1, 1, 0, "set")

        pre1(0)
        B1(0)
        bnd_init = False
        for k, (n, kd, t, u) in enumerate(seq):
            ab = k % 2
            if k + 1 < len(seq):
                B1(k + 1)
            stage1 = (k % NU, n + 1)
            if kd == "x":
                C.op("dve", lambda e, ab=ab: e.tensor_copy(out=Abuf[ab][:, 0, :, :, 0], in_=Sx[:]), r=(eres,), w=(ares[ab],))
                for j in range(UT):
                    step_single(ab, 0, j, j + 1, Ls_r[:], Ls_s[:])
                C.op("dve", lambda e, ab=ab: e.tensor_copy(out=Sx[:], in_=Abuf[ab][:, 0, :, :, UT]), r=(ares[ab],), w=(eres,))
            else:
                U = NU * t + u
                if not bnd_init:
                    U_last = NU * n_own
                    if isS and n_oth > 0:
                        C.op("dve", lambda e, U_last=U_last: e.tensor_scalar_mul(out=Sx[:], in0=Sx[:], scalar1=flag[:, 1:2]), r=(ld,), w=(eres,))
                        C.op("dve", lambda e, U_last=U_last: e.tensor_copy(out=bnd[:, U_last, :, :], in_=Sx[:]), w=(eres,))
                    else:
                        C.op("dve", lambda e, U_last=U_last: e.memset(Sx[:], 0.0), w=(eres,))
                        C.op("dve", lambda e, U_last=U_last: e.tensor_copy(out=bnd[:, U_last, :, :], in_=Sx[:]), w=(eres,))
                    bnd_init = True
                C.op("dve", lambda e, ab=ab: e.tensor_copy(out=Abuf[ab][:, 1, :, :, UT], in_=Sx[:]), r=(eres,), w=(ares[ab],))
                for j in range(UT):
                    step_single(ab, 1, UT - j, UT - 1 - j, Lr2[:, 1], LiS[:, 1])
                C.op("dve", lambda e, ab=ab: e.tensor_copy(out=Sx[:], in_=Abuf[ab][:, 1, :, :, 0]), r=(ares[ab],), w=(eres,))
                C.op("dve", lambda e, U=U: e.tensor_copy(out=bnd[:, U, :, :], in_=Sx[:]), w=(eres,))
            if kd == "x" and (k + 1 == len(seq) or seq[k + 1][1] != "x"):
                C.op("dve", lambda e: e.tensor_scalar_mul(out=Ef[:], in0=Sx[:], scalar1=flag[:, 0:1]), r=(ld,), w=(eres,))
            if stage1[1] < len(order):
                if stage1[0] == 0:
                    pre1A(stage1[1])
                elif stage1[0] == 1:
                    pre1B(stage1[1])
                elif stage1[0] == 2:
                    pre1(stage1[1], staged=True)
        if not (isS and n_oth > 0):
            C.op("dve", lambda e: e.memset(Ef[:], 0.0), w=(eres,))
        xs = Stream(C, "sp", xts, [job["xo"][t * 128:(t + 1) * 128, :] for t in range(n_own)])
        blkU = [(wA[b].rearrange("p (dt c) -> p dt c", dt=NDT), None) for b in range(4)]
        blkG = [(wA[4 + b].rearrange("p (dt c) -> p dt c", dt=NDT), None) for b in range(4)]
        blkL = [(wG[b].rearrange("p (et c) -> p et c", et=8), glu_view) for b in range(4)]
        srcs = blkU + blkG
        for t in range(n_own):
            if t >= 1:
                srcs = srcs + blkL
            if t + 1 < n_own:
                srcs = srcs + blkU + blkG
        srcs = srcs + blkL
        ws = Stream(C, "sp", wbs, srcs, r_extra=(wc,))
        wk[0] = 0
        xslot2 = {}

        def p2A(t):
            xslot2[t] = xs.get(t)
            front_n(xslot2[t][0], xslot2[t][1], split=True)

        def p2B(t):
            front_nB(xslot2[t][0], xslot2[t][1])

        def p2C(t):
            feat_1024(ws, [(uTs[t % 2], ures[t % 2], None)])

        def p2D(t):
            feat_1024(ws, [(gsTs[t % 2], gres[t % 2], None)], func=AF.Silu)

        def pre2(t):
            xt, xres = xs.get(t)
            front_n(xt, xres)
            feat_1024(ws, [(uTs[t % 2], ures[t % 2], None)])
            feat_1024(ws, [(gsTs[t % 2], gres[t % 2], None)], func=AF.Silu)

        def B2(k):
            t, u = divmod(k, NU)
            bmm(k % 2, uTs[t % 2], ures[t % 2], u * UT, 0, 0, 1, "set")
            bmm(k % 2, uTs[t % 2], ures[t % 2], u * UT, 1, 1, 0, "set")

        GW = (gwres,)
        PQ = "pool"

        def gA(t):
            uT = uTs[t % 2]
            C.op(PQ, lambda e: e.tensor_tensor(out=g1[:], in0=uT[:], in1=bcast(dcol[:], [128, 8, 128], 2), op=ALU.mult), r=(ures[t % 2], ld), w=GW)
            C.op(PQ, lambda e: e.tensor_add(out=yS[:], in0=yT[:], in1=g1[:]), r=GW + (yres,), w=(ysres,))
            C.op("act", lambda e: e.activation(out=g1[:], in_=yS[:], func=AF.Square), r=(ysres,), w=GW)
            C.op(PQ, lambda e: e.tensor_scalar(out=g1[:], in0=g1[:], scalar1=0.044715, scalar2=1.0, op0=ALU.mult, op1=ALU.add), w=GW)
            C.op(PQ, lambda e: e.tensor_mul(out=g1[:], in0=g1[:], in1=yS[:]), r=(ysres,), w=GW)

        def gB(t):
            C.op("act", lambda e: e.activation(out=g1[:], in_=g1[:], func=AF.Sigmoid, scale=1.5957691216057308), w=GW)
            C.op(PQ, lambda e: e.tensor_mul(out=g2[:], in0=g1[:], in1=yS[:]), r=(ysres,), w=GW)
            C.op("act", lambda e: e.activation(out=gb[:], in_=g2[:], func=AF.Copy), w=GW)

        def gC(t):
            for half in range(2):
                bk = psL[half]
                for b2 in range(2):
                    wt, wres = ws.get(wk[0]); wk[0] += 1
                    lin_feat(wt, wres, bk, 2 * b2, 2, nk=8, rhs=gb, rres_=gwres)
                C.op("act", lambda e, bk=bk, half=half: e.activation(out=g1[:, 4 * half:4 * half + 4, :], in_=banks[bk][:, :].rearrange("p (a t) -> p a t", a=4), func=AF.Sigmoid),
                     r=(bres[bk],), w=GW)
            C.op(PQ, lambda e: e.tensor_mul(out=g2[:], in0=g2[:], in1=g1[:]), w=GW)
            C.op(PQ, lambda e: e.tensor_mul(out=g2[:], in0=g2[:], in1=gsTs[t % 2][:]), r=(gres[t % 2],), w=GW)
            C.op("act", lambda e: e.activation(out=g3[:], in_=g2[:], func=AF.Square), w=GW)

        def gD(t):
            for et in range(8):
                C.op("pe", lambda e, et=et: e.matmul(banks[psM][:, 128:256], lhsT=onesb[:], rhs=g3[:, et, :], start=(et == 0), stop=(et == 7)),
                     r=(gwres, cst), w=(bres[psM],))
            C.op("act", lambda e: e.activation(out=dg[:], in_=banks[psM][:, 128:256], func=AF.Copy, scale=1.0 / 1024), r=(bres[psM],), w=(rres,))
            C.op(PQ, lambda e: e.tensor_scalar_add(out=dg[:], in0=dg[:], scalar1=EPS), w=(rres,))
            C.op(PQ, lambda e: e.tensor_tensor(out=dg[:], in0=dg[:], in1=mhalf[:].broadcast_to([128, 128]), op=ALU.pow), r=(cst,), w=(rres,))
            C.op(PQ, lambda e: e.tensor_tensor(out=g2[:], in0=g2[:], in1=bcast(dg[:], [128, 8, 128], 1), op=ALU.mult), r=(rres,), w=GW)
            C.op(PQ, lambda e: e.tensor_tensor(out=msT[:], in0=g2[:], in1=bcast(snwcol[:], [128, 8, 128], 2), op=ALU.mult), r=(gwres, ld), w=(mres,))
            tok = job["tok0"] + t * 128
            C.dma("sp", ms_d[:, :, tok:tok + 128].rearrange("e p t -> p e t"), msT[:], r=(mres,))

        pre2(0)
        B2(0)
        nk2 = n_own * NU
        for k in range(nk2):
            t, u = divmod(k, NU)
            ab = k % 2
            if k + 1 < nk2:
                B2(k + 1)
            C.op("dve", lambda e, ab=ab: e.tensor_copy(out=Abuf[ab][:, 0, :, :, 0], in_=Ef[:]), r=(eres,), w=(ares[ab],))
            C.op("dve", lambda e, ab=ab, k=k: e.tensor_copy(out=Abuf[ab][:, 1, :, :, UT], in_=bnd[:, k + 1, :, :]), r=(eres,), w=(ares[ab],))
            for j in range(UT):
                step_both(ab, j)
            C.op("dve", lambda e, ab=ab: e.tensor_copy(out=Ef[:], in_=Abuf[ab][:, 0, :, :, UT]), r=(ares[ab],), w=(eres,))
            C.op("pool", lambda e, ab=ab: e.tensor_copy(out=Xbf[:, 0], in_=Abuf[ab][:, 0, :, :, 1:UT + 1]), r=(ares[ab],), w=(xbres,))
            C.op("pool", lambda e, ab=ab: e.tensor_copy(out=Xbf[:, 1], in_=Abuf[ab][:, 1, :, :, 0:UT]), r=(ares[ab],), w=(xbres,))
            for gt in range(8):
                kk = 0
                for i in range(4 * gt, 4 * gt + 4):
                    for d in range(2):
                        for r in range(2):
                            C.op("pe", lambda e, gt=gt, i=i, d=d, r=r, kk=kk: e.matmul(banks[psY][:, gt * UT:(gt + 1) * UT], lhsT=Cp[:, i, d, r, :],
                                                                                      rhs=Xbf[:, d, r, i, :], start=(kk == 0), stop=(kk == 15)),
                                 r=(sres, xbres), w=(bres[psY],))
                            kk += 1
            C.op("act", lambda e, u=u: e.activation(out=yT[:, :, u * UT:(u + 1) * UT], in_=banks[psY][:, 0:8 * UT].rearrange("p (a t) -> p a t", a=8), func=AF.Copy),
                 r=(bres[psY],), w=(yres,))
            if u == 0:
                if t >= 1:
                    gB(t - 1)
                if t + 1 < n_own:
                    p2A(t + 1)
            elif u == 1:
                if t >= 1:
                    gC(t - 1)
                if t + 1 < n_own:
                    p2B(t + 1)
            elif u == 2:
                if t >= 1:
                    gD(t - 1)
                if t + 1 < n_own:
                    p2C(t + 1)
            else:
                gA(t)
                if t + 1 < n_own:
                    p2D(t + 1)
        gB(n_own - 1)
        gC(n_own - 1)
        gD(n_own - 1)

    C.barrier()
    ssm_stack.close()

    KT = sb("KT", [128, 2, MAXS * 128], BF16)
    anw = sb("anw", [128, 1024])
    C.dma("sp", anw[:], dram_ap(attn_out_norm_w, 0, [[0, 128], [1, 1024]]), w=(ld,))
    V1 = sb("V1", [128, MAXS, 2, 130], BF16)
    wb2 = [(sb("wc%d" % i, [128, NDT, 512], BF16), C.res("wc%d" % i)) for i in range(2)]
    QT = sb("QT", [128, 2, 512], BF16); qtres = C.res("QT")
    qf = sb("qf", [128, 4, 128]); qn = sb("qn", [128, 4, 128]); qt2 = sb("qt2", [128, 4, 128]); qr = sb("qr", [128, 4, 128], BF16)
    qres = C.res("qwork")
    jk = sb("jk", [128, 128], BF16)
    ga = sb("ga", [128, 1024], BF16); gares = C.res("ga")
    oa = sb("oa", [128, 1024]); oares = C.res("oa")
    Mb = sb("Mb", [128, 1024], BF16)
    MT = sb("MT", [128, NDT, 128], BF16); mtres = C.res("MT"); mt2res = C.res("MT2")
    PTs = [(sb("PT%d" % i, [128, 512], BF16), C.res("PT%d" % i)) for i in range(3)]
    youts = [(sb("yo%d" % i, [128, D]), C.res("yo%d" % i)) for i in range(2)]
    cs = sb("cs", [128, 128]); sn = sb("sn", [128, 128]); rpres = C.res("rope")
    tq = sb("tq", [128, 64]); tf = sb("tf", [128, 64]); ti = sb("ti", [128, 64], I32); pc = sb("pc", [128, 2])
    kvres = C.res("kv")
    psS, psO = (4, 5), (6, 7)
    C.op("pool", lambda e: e.memset(V1[:], 1.0), w=(kvres,))
    SCALE = 128.0 ** -0.5

    def make_rope(kind, t):
        W = (rpres,)
        R = (cst, ld)
        if kind == "x":
            C.op("dve", lambda e: e.tensor_scalar_add(out=pc[:, 0:1], in0=hicol[:], scalar1=float(2 * t)), r=R, w=W)
            C.op("dve", lambda e: e.tensor_mul(out=pc[:, 0:1], in0=pc[:, 0:1], in1=flag[:, 2:3]), r=R, w=W)
            C.op("dve", lambda e: e.scalar_tensor_tensor(out=pc[:, 0:1], in0=flag[:, 1:2], scalar=float((LSO + LSX) // 64 - 1), in1=pc[:, 0:1], op0=ALU.mult, op1=ALU.add), r=R, w=W)
            C.op("dve", lambda e: e.tensor_mul(out=pc[:, 1:2], in0=colpos[:], in1=flag[:, 2:3]), r=R, w=W)
            C.op("dve", lambda e: e.scalar_tensor_tensor(out=pc[:, 1:2], in0=flag[:, 1:2], scalar=63.0, in1=pc[:, 1:2], op0=ALU.mult, op1=ALU.add), r=R, w=W)
        else:
            C.op("dve", lambda e: e.tensor_scalar_add(out=pc[:, 0:1], in0=hicol[:], scalar1=float(2 * t)), r=R, w=W)
            if kind == "s":
                C.op("dve", lambda e: e.scalar_tensor_tensor(out=pc[:, 0:1], in0=flag[:, 0:1], scalar=float(2 * (LSO // 128)), in1=pc[:, 0:1], op0=ALU.mult, op1=ALU.add), r=R, w=W)
            C.op("dve", lambda e: e.tensor_copy(out=pc[:, 1:2], in_=colpos[:]), r=R, w=W)
        for shift, which in ((0.0, "sin"), (0.25, "cos")):
            C.op("dve", lambda e: e.tensor_scalar(out=tq[:, 0:32], in0=fq[:], scalar1=pc[:, 0:1], scalar2=shift, op0=ALU.mult, op1=ALU.add), r=R, w=W)
            C.op("dve", lambda e: e.tensor_scalar(out=tq[:, 32:64], in0=fq[:], scalar1=pc[:, 1:2], scalar2=shift, op0=ALU.mult, op1=ALU.add), r=R, w=W)
            C.op("dve", lambda e: e.tensor_copy(out=ti[:], in_=tq[:]), w=W)
            C.op("dve", lambda e: e.tensor_copy(out=tf[:], in_=ti[:]), w=W)
            C.op("dve", lambda e: e.tensor_sub(out=tq[:], in0=tq[:], in1=tf[:]), w=W)
            C.op("act", lambda e: e.activation(out=tf[:], in_=tq[:], func=AF.Sin, scale=TWO_PI_S), w=W)
            tf3 = tf[:].rearrange("p (a j) -> p a j", a=2)
            if which == "cos":
                c5 = cs[:].rearrange("p (a x j) -> p a x j", a=2, x=2)
                for x_ in range(2):
                    C.op("dve", lambda e, x_=x_: e.tensor_copy(out=c5[:, :, x_, :], in_=tf3), w=W)
            else:
                s5 = sn[:].rearrange("p (a x j) -> p a x j", a=2, x=2)
                C.op("dve", lambda e: e.tensor_scalar_mul(out=s5[:, :, 0, :], in0=tf3, scalar1=-1.0), w=W)
                C.op("dve", lambda e: e.tensor_copy(out=s5[:, :, 1, :], in_=tf3), w=W)

    def qk_norm_rope(bk, nh, wrow):
        W = (qres,)
        C.op("act", lambda e: e.activation(out=qf[:, 0:nh, :], in_=banks[bk][:, 0:nh * 128].rearrange("p (h d) -> p h d", h=nh), func=AF.Copy, scale=stat[:, 0:1]),
             r=(bres[bk], stres), w=W)
        for h in range(nh):
            C.op("act", lambda e, h=h: e.activation(out=jk[:], in_=qf[:, h, :], func=AF.Square, accum_out=stat[:, 1 + h:2 + h]), w=W + (stres,))
        C.op("dve", lambda e: e.tensor_scalar(out=stat[:, 1:1 + nh], in0=stat[:, 1:1 + nh], scalar1=1.0 / 128, scalar2=EPS, op0=ALU.mult, op1=ALU.add), w=W + (stres,))
        C.op("act", lambda e: e.activation(out=stat[:, 1:1 + nh], in_=stat[:, 1:1 + nh], func=AF.Sqrt), w=W + (stres,))
        C.op("dve", lambda e: e.reciprocal(out=stat[:, 1:1 + nh], in_=stat[:, 1:1 + nh]), w=W + (stres,))
        for h in range(nh):
            C.op("dve", lambda e, h=h: e.scalar_tensor_tensor(out=qn[:, h, :], in0=qf[:, h, :], scalar=stat[:, 1 + h:2 + h], in1=wrow[:], op0=ALU.mult, op1=ALU.mult),
                 r=(ld,), w=W)
        qn5 = qn[:, 0:nh, :].rearrange("p h (a x j) -> p h a x j", a=2, x=2)
        t25 = qt2[:, 0:nh, :].rearrange("p h (a x j) -> p h a x j", a=2, x=2)
        s5 = sn[:].rearrange("p (a x j) -> p a x j", a=2, x=2)
        C.op("dve", lambda e: e.tensor_tensor(out=t25[:, :, :, 0, :], in0=qn5[:, :, :, 1, :], in1=bcast(s5[:, :, 0, :], [128, nh, 2, 32], 1), op=ALU.mult), r=(rpres,), w=W)
        C.op("dve", lambda e: e.tensor_tensor(out=t25[:, :, :, 1, :], in0=qn5[:, :, :, 0, :], in1=bcast(s5[:, :, 1, :], [128, nh, 2, 32], 1), op=ALU.mult), r=(rpres,), w=W)
        C.op("dve", lambda e: e.tensor_tensor(out=qn[:, 0:nh, :], in0=qn[:, 0:nh, :], in1=bcast(cs[:], [128, nh, 128], 1), op=ALU.mult), r=(rpres,), w=W)
        dump("qf_%d" % nh, qf[:], qres); dump("qnc_%d" % nh, qn[:], qres); dump("qt2_%d" % nh, qt2[:], qres)
        dump("cs_%d" % nh, cs[:], rpres); dump("sn_%d" % nh, sn[:], rpres); dump("pc_%d" % nh, pc[:], rpres)
        C.op("dve", lambda e: e.tensor_add(out=qr[:, 0:nh, :], in0=qn[:, 0:nh, :], in1=qt2[:, 0:nh, :]), w=W)
        dump("qr_%d" % nh, qr[:], qres)

    def lin_tok(wt, wres, bk, lhs, lres, nk=NDT):
        for dt_ in range(nk):
            C.op("pe", lambda e, dt_=dt_: e.matmul(banks[bk][:, :], lhsT=lhs[:, dt_, :], rhs=wt[:, dt_, :], start=(dt_ == 0), stop=(dt_ == nk - 1)),
                 r=(wres, lres), w=(bres[bk],))

    def wsrc2(wbf, c0):
        return wbf[:, c0:c0 + 512].rearrange("(dt p) c -> p dt c", p=128)

    for job in jobs:
        n_own, n_oth = job["n_own"], job["n_oth"]
        isS = job["kind"] == "S"
        nst = n_own + n_oth
        order = [("o", t) for t in range(n_own)] + [("x", t) for t in range(n_oth)]
        xsrc = [(job["xx"] if kd == "x" else job["xo"])[t * 128:(t + 1) * 128, :] for kd, t in order]
        xs = Stream(C, "sp", xts, xsrc)
        ws = Stream(C, "sp", wb2, [wB[2].rearrange("p (dt c) -> p dt c", dt=NDT) for _ in order], r_extra=(wc,))
        for n, (kd, t) in enumerate(order):
            xt, xres = xs.get(n)
            front(xt, xres, False)
            make_rope("x" if kd == "x" else ("s" if isS else "p"), t)
            wt, wres = ws.get(n)
            bk = psL[n % 2]
            lin_tok(wt, wres, bk, hT, hres)
            qk_norm_rope(bk, 2, knw)
            C.op("act", lambda e, bk=bk, n=n: e.activation(out=V1[:, n, :, 0:128], in_=banks[bk][:, 256:512].rearrange("p (h d) -> p h d", h=2), func=AF.Copy, scale=stat[:, 0:1]),
                 r=(bres[bk], stres), w=(kvres,))
            tb = psT[n % 2]
            tbv = banks[tb][:, 0:128].bitcast(BF16)
            for h in range(2):
                C.op("pe", lambda e, h=h, tbv=tbv: e.transpose(tbv[:, h * 128:(h + 1) * 128], qr[:, h, :], identb[:]), r=(qres, cst), w=(bres[tb],))
            C.op("dve", lambda e, tbv=tbv, n=n: e.tensor_copy(out=KT[:, :, n * 128:(n + 1) * 128], in_=tbv.rearrange("p (h t) -> p h t", h=2)), r=(bres[tb],), w=(kvres,))
        xs = Stream(C, "sp", xts, [job["xo"][t * 128:(t + 1) * 128, :] for t in range(n_own)])
        srcs = []
        for _ in range(n_own):
            srcs += [wB[b].rearrange("p (dt c) -> p dt c", dt=NDT) for b in (0, 1, 3, 4)]
            srcs += [wO[b].rearrange("p (dt c) -> p dt c", dt=NDT) for b in range(4)]
        ws = Stream(C, "sp", wb2, srcs, r_extra=(wc,))
        pt_i = 0
        for t in range(n_own):
            xt, xres = xs.get(t)
            front(xt, xres, False)
            make_rope("s" if isS else "p", t)
            for qb in range(2):
                wt, wres = ws.get(8 * t + qb)
                bk = psL[qb]
                lin_tok(wt, wres, bk, hT, hres)
                qk_norm_rope(bk, 4, qnw)
                tb = psT[qb]
                tbv = banks[tb][:, 0:256].bitcast(BF16)
                for h in range(4):
                    C.op("pe", lambda e, h=h, tbv=tbv: e.transpose(tbv[:, h * 128:(h + 1) * 128], qr[:, h, :], identb[:]), r=(qres, cst), w=(bres[tb],))
                C.op("dve", lambda e, tbv=tbv, qb=qb: e.tensor_copy(out=QT[:, qb, :], in_=tbv), r=(bres[tb],), w=(qtres,))
            for g_ in range(2):
                wt, wres = ws.get(8 * t + 2 + g_)
                bk = psL[g_]
                lin_tok(wt, wres, bk, hT, hres)
                C.op("act", lambda e, bk=bk, g_=g_: e.activation(out=ga[:, g_ * 512:(g_ + 1) * 512], in_=banks[bk][:, :], func=AF.Silu, scale=stat[:, 0:1]),
                     r=(bres[bk], stres), w=(gares,))
            for kvh in range(2):
                for st in range(nst):
                    sbk = psS[st % 2]
                    C.op("pe", lambda e, sbk=sbk, st=st, kvh=kvh: e.matmul(banks[sbk][:, :], lhsT=KT[:, kvh, st * 128:(st + 1) * 128], rhs=QT[:, kvh, :], start=True, stop=True),
                         r=(kvres, qtres), w=(bres[sbk],))
                    pt, ptres = PTs[pt_i % 3]; pt_i += 1
                    C.op("act", lambda e, sbk=sbk, pt=pt: e.activation(out=pt[:], in_=banks[sbk][:, :], func=AF.Exp, scale=SCALE), r=(bres[sbk],), w=(ptres,))
                    for h in range(4):
                        ob = psO[h // 2]; off = (h % 2) * 129
                        C.op("pe", lambda e, ob=ob, off=off, h=h, pt=pt, st=st, kvh=kvh: e.matmul(
                            banks[ob][:, off:off + 129], lhsT=pt[:, h * 128:(h + 1) * 128], rhs=V1[:, st, kvh, 0:129],
                            start=(st == 0 and h % 2 == 0), stop=(st == nst - 1), skip_group_check=True), r=(ptres, kvres), w=(bres[ob],))
                for h in range(4):
                    ob = psO[h // 2]; off = (h % 2) * 129; head = 4 * kvh + h
                    C.op("dve", lambda e, ob=ob, off=off, h=h: e.reciprocal(out=stat[:, 5 + (h % 2):6 + (h % 2)], in_=banks[ob][:, off + 128:off + 129]), r=(bres[ob],), w=(stres,))
                    C.op("dve", lambda e, ob=ob, off=off, h=h, head=head: e.scalar_tensor_tensor(
                        out=oa[:, head * 128:(head + 1) * 128], in0=banks[ob][:, off:off + 128], scalar=stat[:, 5 + (h % 2):6 + (h % 2)],
                        in1=ga[:, head * 128:(head + 1) * 128], op0=ALU.mult, op1=ALU.mult), r=(bres[ob], gares, stres), w=(oares,))
            dump("KT_%s" % job["kind"], KT[:], kvres); dump("V1_%s" % job["kind"], V1[:], kvres); dump("QT_%s%d" % (job["kind"], t), QT[:], qtres)
            dump("ga_%s%d" % (job["kind"], t), ga[:], gares); dump("oa_%s%d" % (job["kind"], t), oa[:], oares)
            C.op("act", lambda e: e.activation(out=Mb[:], in_=oa[:], func=AF.Square, accum_out=stat[:, 7:8]), r=(oares,), w=(mtres, stres))
            C.op("dve", lambda e: e.tensor_scalar(out=stat[:, 7:8], in0=stat[:, 7:8], scalar1=1.0 / 1024, scalar2=EPS, op0=ALU.mult, op1=ALU.add), w=(stres,))
            C.op("act", lambda e: e.activation(out=stat[:, 7:8], in_=stat[:, 7:8], func=AF.Sqrt), w=(stres,))
            C.op("dve", lambda e: e.reciprocal(out=stat[:, 7:8], in_=stat[:, 7:8]), w=(stres,))
            C.op("dve", lambda e: e.scalar_tensor_tensor(out=Mb[:], in0=oa[:], scalar=stat[:, 7:8], in1=anw[:], op0=ALU.mult, op1=ALU.mult), r=(oares, ld), w=(mtres,))
            for q in range(2):
                tb = psT[q]
                tbv = banks[tb][:, 0:256].bitcast(BF16)
                for kk in range(4):
                    et = 4 * q + kk
                    C.op("pe", lambda e, tbv=tbv, kk=kk, et=et: e.transpose(tbv[:, kk * 128:(kk + 1) * 128], Mb[:, et * 128:(et + 1) * 128], identb[:]), r=(mtres, cst), w=(bres[tb],))
                C.op("dve", lambda e, tbv=tbv, q=q: e.tensor_copy(out=MT[:, 4 * q:4 * q + 4, :], in_=tbv.rearrange("p (a t) -> p a t", a=4)), r=(bres[tb],), w=(mtres,))
            tok = job["tok0"] + t * 128
            C.dma("sp", MT[:, 8:16, :], ms_d[:, :, tok:tok + 128].rearrange("e p t -> p e t"), w=(mt2res,))
            dump("MT_%s%d" % (job["kind"], t), MT[:], mt2res)
            yo, yres_ = youts[t % 2]
            for cb in range(4):
                wt, wres = ws.get(8 * t + 4 + cb)
                bk = psL[cb % 2]
                for dt_ in range(NDT):
                    C.op("pe", lambda e, dt_=dt_, bk=bk, wt=wt: e.matmul(banks[bk][:, :], lhsT=MT[:, dt_, :], rhs=wt[:, dt_, :], start=(dt_ == 0), stop=(dt_ == NDT - 1)),
                         r=(wres, mtres, mt2res), w=(bres[bk],))
                C.op("dve", lambda e, bk=bk, cb=cb, yo=yo, xt=xt: e.tensor_tensor(out=yo[:, cb * 512:(cb + 1) * 512], in0=banks[bk][:, :], in1=xt[:, cb * 512:(cb + 1) * 512], op=ALU.add),
                     r=(bres[bk], xres), w=(yres_,))
            C.dma("sp", y_all[tok:tok + 128, :], yo[:], r=(yres_,))
    C.barrier()
    return nc


_CACHE = {}


def _get_nc(cfg_key, cfg):
    if cfg_key not in _CACHE:
        _CACHE[cfg_key] = build(cfg)
    return _CACHE[cfg_key]


def run_layer(x_prompt, x_sample, weights, n_cores=8, debug=False):
    B, LP, _ = x_prompt.shape
    BS, LS, _ = x_sample.shape
    NP = B // n_cores
    half = LS // 2
    cfg = dict(np=NP, lp=LP, lso=half, lsx=half, debug=debug)
    nc = build(cfg)
    in_maps = []
    for c in range(n_cores):
        seq, h = c // 2, c % 2
        own = x_sample[seq, h * half:(h + 1) * half]
        oth = x_sample[seq, (1 - h) * half:(2 - h) * half]
        if h == 0:
            oth = oth[::-1]
        m = {"x_p": np.ascontiguousarray(x_prompt[c * NP:(c + 1) * NP].reshape(NP * LP, D)),
             "x_so": np.ascontiguousarray(own), "x_sx": np.ascontiguousarray(oth),
             "flag": np.tile(np.array([[h, 1 - h, 2 * h - 1, 0]], np.float32), (128, 1))}
        for k, v in weights.items():
            m[k] = np.ascontiguousarray(v, dtype=np.float32)
        in_maps.append(m)
    res = run_bass_kernel_spmd(nc, in_maps, core_ids=list(range(n_cores)))
    y_p = np.empty_like(x_prompt)
    y_s = np.empty_like(x_sample)
    for c in range(n_cores):
        seq, h = c // 2, c % 2
        ya = res.results[c]["y_all"]
        y_p[c * NP:(c + 1) * NP] = ya[:NP * LP].reshape(NP, LP, D)
        y_s[seq, h * half:(h + 1) * half] = ya[NP * LP:]
    if debug:
        return y_p, y_s, res.results
    return y_p, y_s


def kernel(x_prompt, x_sample, **weights):
    x_prompt = np.asarray(x_prompt, dtype=np.float32)
    x_sample = np.asarray(x_sample, dtype=np.float32)
    weights = {k: np.asarray(v, dtype=np.float32) for k, v in weights.items()}
    return run_layer(x_prompt, x_sample, weights, n_cores=8)
```

```python
import math
import numpy as np
import concourse.bass as bass
import concourse.mybir as mybir
from concourse.bass_utils import run_bass_kernel_spmd

F32 = mybir.dt.float32
BF16 = mybir.dt.bfloat16
I32 = mybir.dt.int32
AF = mybir.ActivationFunctionType
ALU = mybir.AluOpType

D = 2048
NDT = 16
IN_COLS = 4608
EPS = 1e-6
TWO_PI_S = 6.283185
TWO_PI = 2.0 * math.pi
C_Q, C_K, C_V, C_GA, C_U, C_GS = 0, 1024, 1280, 1536, 2560, 3584
AFREE = 2 * 2 * 32 * 65


class Res:
    __slots__ = ("name", "w", "rd", "sem", "cnt")

    def __init__(self, name):
        self.name = name
        self.w = {}
        self.rd = {}
        self.sem = None
        self.cnt = 0


class Ctx:
    def __init__(self, nc):
        self.nc = nc
        self.eng = {"pe": nc.tensor, "act": nc.scalar, "dve": nc.vector, "pool": nc.gpsimd, "sp": nc.sync}
        self.sem = {k: nc.alloc_semaphore("sem_" + k) for k in self.eng}
        self.cnt = {k: 0 for k in self.eng}
        self.waited = {k: {} for k in self.eng}
        self.nres = 0
        self.allres = []

    def res(self, name=None):
        self.nres += 1
        r = Res(name or "r%d" % self.nres)
        self.allres.append(r)
        return r

    def _wait(self, eng, tok, same_ok=True):
        sem, val = tok
        if sem is self.sem[eng] and not same_ok:
            return
        k = id(sem)
        if self.waited[eng].get(k, 0) >= val:
            return
        self.eng[eng].wait_ge(sem, val)
        self.waited[eng][k] = val

    def _deps(self, eng, r, w, wo=()):
        for x in r:
            for t in x.w.values():
                self._wait(eng, t)
        for x in w:
            for t in x.w.values():
                self._wait(eng, t)
            for t in x.rd.values():
                self._wait(eng, t, same_ok=False)
        for x in wo:
            for t in x.w.values():
                self._wait(eng, t, same_ok=False)
            for t in x.rd.values():
                self._wait(eng, t, same_ok=False)

    def _mark(self, tok, r, w):
        k = id(tok[0])
        for x in r:
            if x in w:
                continue
            x.rd[k] = tok
        for x in w:
            x.w = {k: tok}
            x.rd = {}

    def op(self, eng, fn, r=(), w=(), wo=()):
        self._deps(eng, r, w, wo)
        ins = fn(self.eng[eng])
        self.cnt[eng] += 1
        ins.then_inc(self.sem[eng], 1)
        self._mark((self.sem[eng], self.cnt[eng]), r, tuple(w) + tuple(wo))

    def dma(self, q, out, in_, r=(), w=(), **kw):
        self._deps(q, r, w)
        own = w[0] if w else r[0]
        if own.sem is None:
            own.sem = self.nc.alloc_semaphore("dsem_" + own.name)
        ins = self.eng[q].dma_start(out=out, in_=in_, **kw)
        own.cnt += 16
        ins.then_inc(own.sem, 16)
        self._mark((own.sem, own.cnt), r, w)

    def barrier(self):
        toks = [(self.sem[k], self.cnt[k]) for k in self.eng if self.cnt[k] > 0]
        for x in self.allres:
            if x.sem is not None and x.cnt > 0:
                toks.append((x.sem, x.cnt))
        for e in self.eng:
            for t in toks:
                self._wait(e, t)


class Stream:
    def __init__(self, C, q, slots, srcs, r_extra=(), **kw):
        self.C, self.q, self.slots, self.srcs, self.r_extra, self.kw = C, q, slots, srcs, tuple(r_extra), kw
        self.issued = 0

    def get(self, k):
        n = len(self.slots)
        while self.issued < min(len(self.srcs), k + n):
            j = self.issued
            t, res = self.slots[j % n]
            src = self.srcs[j]
            view = None
            if isinstance(src, tuple):
                src, view = src
            dst = view(t) if view else t[:]
            for x in self.r_extra:
                for tok in x.w.values():
                    self.C._wait(self.q, tok)
            self.C.dma(self.q, dst, src, w=(res,), **self.kw)
            self.issued += 1
        return self.slots[k % n]


def dram_ap(t, offset, pat):
    return bass.AP(t.tensor, offset, [list(p) for p in pat])


def build(cfg):
    NP, LP, LSO, LSX = cfg["np"], cfg["lp"], cfg["lso"], cfg["lsx"]
    nc = bass.Bass("TRN2", target_bir_lowering=False)
    C = Ctx(nc)
    din = lambda name, shape: nc.dram_tensor(name, list(shape), F32, kind="ExternalInput").ap()
    x_p = din("x_p", [NP * LP, D])
    x_so = din("x_so", [LSO, D])
    x_sx = din("x_sx", [max(LSX, 128), D])
    flag_d = din("flag", [128, 4])
    norm_w = din("norm_w", [1, D])
    w_in = din("w_in", [1, D, IN_COLS])
    q_norm_w = din("q_norm_w", [1, 128])
    k_norm_w = din("k_norm_w", [1, 128])
    lambda_re = din("lambda_re", [1, 2, 64, 64])
    lambda_im = din("lambda_im", [1, 2, 64, 64])
    log_step = din("log_step", [1, 2, 64])
    b_re = din("b_re", [1, 64, 64, 16])
    b_im = din("b_im", [1, 64, 64, 16])
    c_re = din("c_re", [1, 2, 64, 16, 64])
    c_im = din("c_im", [1, 2, 64, 16, 64])
    d_skip = din("d_skip", [1, 1024])
    w_glu = din("w_glu", [1, 1024, 1024])
    attn_out_norm_w = din("attn_out_norm_w", [1, 1024])
    ssm_out_norm_w = din("ssm_out_norm_w", [1, 1024])
    w_out = din("w_out", [1, D, D])
    NTOK = NP * LP + LSO
    y_all = nc.dram_tensor("y_all", [NTOK, D], F32, kind="ExternalOutput").ap()
    wA = nc.dram_tensor("wA", [8, 128, NDT * 256], BF16, kind="Internal").ap()
    wB = nc.dram_tensor("wB", [5, 128, NDT * 512], BF16, kind="Internal").ap()
    wO = nc.dram_tensor("wO", [4, 128, NDT * 512], BF16, kind="Internal").ap()
    wG = nc.dram_tensor("wG", [4, 128, 8 * 256], BF16, kind="Internal").ap()
    ms_d = nc.dram_tensor("ms_scr", [8, 128, NTOK], BF16, kind="Internal").ap()

    jobs = []
    for j in range(NP):
        jobs.append(dict(kind="P", xo=x_p[j * LP:(j + 1) * LP, :], xx=None, n_own=LP // 128, n_oth=0, tok0=j * LP))
    jobs.append(dict(kind="S", xo=x_so, xx=x_sx, n_own=LSO // 128, n_oth=LSX // 128, tok0=NP * LP))
    MAXU = max(j["n_own"] for j in jobs) * 2
    MAXS = max(j["n_own"] + j["n_oth"] for j in jobs)

    sb = lambda name, shape, dt=F32: nc.alloc_sbuf_tensor(name, list(shape), dt)
    dbg = cfg.get("debug", False)
    dumped = {}

    def dump(name, ap, res):
        if not dbg or name in dumped:
            return
        shp = list(ap.shape)
        t_ = nc.dram_tensor("dbg_" + name, shp, ap.dtype, kind="ExternalOutput").ap()
        dumped[name] = t_
        C.dma("sp", t_, ap, r=(res,))
    banks = [nc.alloc_psum_tensor("bank%d" % i, [128, 512], F32) for i in range(8)]
    bres = [C.res("bank%d" % i) for i in range(8)]

    def bcast(ap, shape, axis):
        return ap.unsqueeze(axis).broadcast_to(list(shape))

    wc = C.res("wcast")
    for b in range(8):
        c0 = (C_U + 256 * b) if b < 4 else (C_GS + 256 * (b - 4))
        C.dma("pool", wA[b].rearrange("p (dt c) -> p dt c", dt=NDT), w_in[0, :, c0:c0 + 256].rearrange("(dt p) c -> p dt c", p=128), w=(wc,))
    for b, c0 in enumerate((C_Q, C_Q + 512, C_K, C_GA, C_GA + 512)):
        C.dma("pool", wB[b].rearrange("p (dt c) -> p dt c", dt=NDT), w_in[0, :, c0:c0 + 512].rearrange("(dt p) c -> p dt c", p=128), w=(wc,))
    for b in range(4):
        C.dma("pool", wO[b].rearrange("p (dt c) -> p dt c", dt=NDT), w_out[0, :, 512 * b:512 * b + 512].rearrange("(dt p) c -> p dt c", p=128), w=(wc,))
        C.dma("pool", wG[b].rearrange("p (et c) -> p et c", et=8), w_glu[0, :, 256 * b:256 * b + 256].rearrange("(et p) c -> p et c", p=128), w=(wc,))

    cst = C.res("const")
    ident = sb("ident", [128, 128]); identb = sb("identb", [128, 128], BF16)
    ones_t = sb("ones_t", [128, 128]); onesb = sb("onesb", [128, 128], BF16)
    C.op("pool", lambda e: e.memset(ones_t[:], 1.0), w=(cst,))
    C.op("pool", lambda e: e.memset(onesb[:], 1.0), w=(cst,))
    C.op("pool", lambda e: e.affine_select(out=ident[:], in_=ones_t[:], pattern=[[-1, 128]], compare_op=ALU.is_equal,
                                           fill=0.0, base=0, channel_multiplier=1), w=(cst,))
    C.op("dve", lambda e: e.tensor_copy(out=identb[:], in_=ident[:]), r=(), w=(cst,))
    flag = sb("flag_sb", [128, 4]); nwcol = sb("nwcol", [128, NDT])
    qnw = sb("qnw", [128, 128]); knw = sb("knw", [128, 128])
    dcol = sb("dcol", [128, 8]); snwcol = sb("snwcol", [128, 8])
    mhalf = sb("mhalf", [128, 1])
    C.op("pool", lambda e: e.memset(mhalf[:], -0.5), w=(cst,))
    ld = C.res("cload")
    C.dma("sp", flag[:], flag_d[:, :], w=(ld,))
    C.dma("sp", nwcol[:], dram_ap(norm_w, 0, [[1, 128], [128, NDT]]), w=(ld,), allow_slow_non_contiguous=True)
    C.dma("sp", qnw[:], dram_ap(q_norm_w, 0, [[0, 128], [1, 128]]), w=(ld,))
    C.dma("sp", knw[:], dram_ap(k_norm_w, 0, [[0, 128], [1, 128]]), w=(ld,))
    C.dma("sp", dcol[:], dram_ap(d_skip, 0, [[1, 128], [128, 8]]), w=(ld,), allow_slow_non_contiguous=True)
    C.dma("sp", snwcol[:], dram_ap(ssm_out_norm_w, 0, [[1, 128], [128, 8]]), w=(ld,), allow_slow_non_contiguous=True)

    fq = sb("fq", [128, 32]); jt = sb("jt", [128, 32]); pidx = sb("pidx", [128, 1])
    hicol = sb("hicol", [128, 1]); colpos = sb("colpos", [128, 1])
    C.op("pool", lambda e: e.iota(jt[:], pattern=[[1, 32]], base=0, channel_multiplier=0, allow_small_or_imprecise_dtypes=True), w=(cst,))
    C.op("pool", lambda e: e.iota(pidx[:], pattern=[[0, 1]], base=0, channel_multiplier=1, allow_small_or_imprecise_dtypes=True), w=(cst,))
    C.op("act", lambda e: e.activation(out=fq[:], in_=jt[:], func=AF.Exp, scale=-math.log(10000.0) / 32.0), w=(cst,))
    C.op("dve", lambda e: e.tensor_scalar_mul(out=fq[:], in0=fq[:], scalar1=1.0 / TWO_PI), w=(cst,))
    C.op("dve", lambda e: e.tensor_single_scalar(out=hicol[:], in_=pidx[:], scalar=64.0, op=ALU.is_ge), w=(cst,))
    C.op("dve", lambda e: e.scalar_tensor_tensor(out=colpos[:], in0=hicol[:], scalar=-64.0, in1=pidx[:], op0=ALU.mult, op1=ALU.add), w=(cst,))

    def cyc_sin(eng_w, dst, q, tf, ti, shift=0.0):
        if shift != 0.0:
            C.op("dve", lambda e: e.tensor_scalar_add(out=q, in0=q, scalar1=shift), w=eng_w)
        C.op("dve", lambda e: e.tensor_copy(out=ti, in_=q), w=eng_w)
        C.op("dve", lambda e: e.tensor_copy(out=tf, in_=ti), w=eng_w)
        C.op("dve", lambda e: e.tensor_sub(out=q, in0=q, in1=tf), w=eng_w)
        C.op("act", lambda e: e.activation(out=dst, in_=q, func=AF.Sin, scale=TWO_PI_S), w=eng_w)

    xts = [(sb("xt%d" % i, [128, D]), C.res("xt%d" % i)) for i in range(2)]
    hT = sb("hT", [128, NDT, 128], BF16); hres = C.res("hT")
    stat = sb("stat", [128, 8]); stres = C.res("stat")
    dg = sb("dg", [128, 128]); rrep = sb("rrep", [128, 128]); rres = C.res("rrep")
    import contextlib
    ssm_stack = contextlib.ExitStack()
    ssb = lambda name, shape, dt=F32: ssm_stack.enter_context(nc.sbuf_tensor(name, list(shape), dt))
    Bp = ssb("Bp", [128, 32, 2, 2, 128], BF16)
    Cp = ssb("Cp", [128, 32, 2, 2, 128], BF16)
    Lr2 = ssb("Lr2", [128, 2, 2, 32]); LiN = ssb("LiN", [128, 2, 32]); LiP = ssb("LiP", [128, 2, 32])
    Ls_r = ssb("Ls_r", [128, 2, 32]); Ls_n = ssb("Ls_n", [128, 32]); Ls_p = ssb("Ls_p", [128, 32])
    LiS = ssb("LiS", [128, 2, 2, 32]); Ls_s = ssb("Ls_s", [128, 2, 32])
    sres = C.res("ssmw")
    with contextlib.ExitStack() as st_:
        lre = st_.enter_context(nc.sbuf_tensor("lre", [128, 32, 2], F32))
        lim = st_.enter_context(nc.sbuf_tensor("lim", [128, 32, 2], F32))
        lst = st_.enter_context(nc.sbuf_tensor("lst", [128, 32, 2], F32))
        Br = st_.enter_context(nc.sbuf_tensor("Br", [128, 32, 16], F32))
        Bi = st_.enter_context(nc.sbuf_tensor("Bi", [128, 32, 16], F32))
        Cn = st_.enter_context(nc.sbuf_tensor("Cn", [128, 2, 2, 8, 2, 64], F32))
        s1 = st_.enter_context(nc.sbuf_tensor("s1", [128, 32, 2], F32))
        s2 = st_.enter_context(nc.sbuf_tensor("s2", [128, 32, 2], F32))
        s3 = st_.enter_context(nc.sbuf_tensor("s3", [128, 32, 2], F32))
        si = st_.enter_context(nc.sbuf_tensor("si", [128, 32, 2], I32))
        lbr = st_.enter_context(nc.sbuf_tensor("lbr", [128, 32, 2], F32))
        lbi = st_.enter_context(nc.sbuf_tensor("lbi", [128, 32, 2], F32))
        bsr = st_.enter_context(nc.sbuf_tensor("bsr", [128, 32, 2], F32))
        bsi = st_.enter_context(nc.sbuf_tensor("bsi", [128, 32, 2], F32))
        Bpr = st_.enter_context(nc.sbuf_tensor("Bpr", [128, 32, 2, 2, 16], F32))
        t16 = st_.enter_context(nc.sbuf_tensor("t16", [128, 32, 2, 16], F32))
        Mc = st_.enter_context(nc.sbuf_tensor("Mc", [128, 4, 8, 16], F32))
        onesm = st_.enter_context(nc.sbuf_tensor("onesm", [128, 4, 8, 16], F32))
        pre = st_.enter_context(nc.sbuf_tensor("pre", [128, 4, 128], F32))
        Trep = st_.enter_context(nc.sbuf_tensor("Trep", [128, 2, 8, 2, 128], F32))
        for gp in range(2):
            ps_ = slice(64 * gp, 64 * gp + 64)
            for d in range(2):
                C.dma("sp", lre[ps_, :, d], dram_ap(lambda_re, gp * 64 + d * 4096, [[1, 64], [128, 32]]), w=(ld,), allow_slow_non_contiguous=True)
                C.dma("sp", lim[ps_, :, d], dram_ap(lambda_im, gp * 64 + d * 4096, [[1, 64], [128, 32]]), w=(ld,), allow_slow_non_contiguous=True)
                C.dma("sp", lst[ps_, :, d], dram_ap(log_step, gp + d * 64, [[0, 64], [2, 32]]), w=(ld,), allow_slow_non_contiguous=True)
            C.dma("sp", Br[ps_, :, :], dram_ap(b_re, gp * 1024, [[16, 64], [2048, 32], [1, 16]]), w=(ld,))
            C.dma("sp", Bi[ps_, :, :], dram_ap(b_im, gp * 1024, [[16, 64], [2048, 32], [1, 16]]), w=(ld,))
        for ri, csrc in enumerate((c_re, c_im)):
            for dup in range(2):
                for d in range(2):
                    C.dma("sp", Cn[:, ri, d, :, dup, :], dram_ap(csrc, d * 65536, [[64, 128], [8192, 8], [1, 64]]), w=(ld,))
        W = (sres,)
        R = (ld, cst)
        C.op("act", lambda e: e.activation(out=s1[:], in_=lst[:], func=AF.Exp), r=R, w=W)
        C.op("dve", lambda e: e.tensor_mul(out=s2[:], in0=lre[:], in1=s1[:]), r=R, w=W)
        C.op("dve", lambda e: e.scalar_tensor_tensor(out=s3[:], in0=lim[:], scalar=1.0 / TWO_PI, in1=s1[:], op0=ALU.mult, op1=ALU.mult), r=R, w=W)
        C.op("act", lambda e: e.activation(out=s2[:], in_=s2[:], func=AF.Exp), w=W)
        C.op("dve", lambda e: e.tensor_copy(out=lbi[:], in_=s3[:]), w=W)
        cyc_sin(W, lbi[:], lbi[:], s1[:], si[:])
        cyc_sin(W, lbr[:], s3[:], s1[:], si[:], shift=0.25)
        C.op("dve", lambda e: e.tensor_mul(out=lbr[:], in0=lbr[:], in1=s2[:]), w=W)
        C.op("dve", lambda e: e.tensor_mul(out=lbi[:], in0=lbi[:], in1=s2[:]), w=W)
        C.op("dve", lambda e: e.tensor_mul(out=s1[:], in0=lre[:], in1=lre[:]), w=W)
        C.op("dve", lambda e: e.tensor_mul(out=s2[:], in0=lim[:], in1=lim[:]), w=W)
        C.op("dve", lambda e: e.tensor_add(out=s1[:], in0=s1[:], in1=s2[:]), w=W)
        C.op("dve", lambda e: e.reciprocal(out=s1[:], in_=s1[:]), w=W)
        C.op("dve", lambda e: e.tensor_scalar_add(out=s2[:], in0=lbr[:], scalar1=-1.0), w=W)
        C.op("dve", lambda e: e.tensor_mul(out=bsr[:], in0=s2[:], in1=lre[:]), w=W)
        C.op("dve", lambda e: e.tensor_mul(out=s3[:], in0=lbi[:], in1=lim[:]), w=W)
        C.op("dve", lambda e: e.tensor_add(out=bsr[:], in0=bsr[:], in1=s3[:]), w=W)
        C.op("dve", lambda e: e.tensor_mul(out=bsr[:], in0=bsr[:], in1=s1[:]), w=W)
        C.op("dve", lambda e: e.tensor_mul(out=bsi[:], in0=lbi[:], in1=lre[:]), w=W)
        C.op("dve", lambda e: e.tensor_mul(out=s3[:], in0=s2[:], in1=lim[:]), w=W)
        C.op("dve", lambda e: e.tensor_sub(out=bsi[:], in0=bsi[:], in1=s3[:]), w=W)
        C.op("dve", lambda e: e.tensor_mul(out=bsi[:], in0=bsi[:], in1=s1[:]), w=W)
        for d in range(2):
            for r in range(2):
                C.op("dve", lambda e, d=d, r=r: e.tensor_copy(out=Lr2[:, d, r, :], in_=lbr[:, :, d]), w=W)
            C.op("dve", lambda e, d=d: e.tensor_copy(out=LiP[:, d, :], in_=lbi[:, :, d]), w=W)
            C.op("dve", lambda e, d=d: e.tensor_scalar_mul(out=LiN[:, d, :], in0=lbi[:, :, d], scalar1=-1.0), w=W)
        for r in range(2):
            C.op("dve", lambda e, r=r: e.tensor_scalar_mul(out=Ls_r[:, r, :], in0=lbr[:, :, 0], scalar1=flag[:, 0:1]), w=W)
            C.op("dve", lambda e, r=r: e.scalar_tensor_tensor(out=Ls_r[:, r, :], in0=lbr[:, :, 1], scalar=flag[:, 1:2], in1=Ls_r[:, r, :],
                                                              op0=ALU.mult, op1=ALU.add), w=W)
        C.op("dve", lambda e: e.tensor_scalar_mul(out=Ls_p[:], in0=lbi[:, :, 0], scalar1=flag[:, 0:1]), w=W)
        C.op("dve", lambda e: e.scalar_tensor_tensor(out=Ls_p[:], in0=lbi[:, :, 1], scalar=flag[:, 1:2], in1=Ls_p[:], op0=ALU.mult, op1=ALU.add), w=W)
        C.op("dve", lambda e: e.tensor_scalar_mul(out=Ls_n[:], in0=Ls_p[:], scalar1=-1.0), w=W)
        for d in range(2):
            C.op("dve", lambda e, d=d: e.tensor_copy(out=LiS[:, d, 0, :], in_=LiN[:, d, :]), w=W)
            C.op("dve", lambda e, d=d: e.tensor_copy(out=LiS[:, d, 1, :], in_=LiP[:, d, :]), w=W)
        C.op("dve", lambda e: e.tensor_copy(out=Ls_s[:, 0, :], in_=Ls_n[:]), w=W)
        C.op("dve", lambda e: e.tensor_copy(out=Ls_s[:, 1, :], in_=Ls_p[:]), w=W)
        for d in range(2):
            bsr_b = bcast(bsr[:, :, d], [128, 32, 16], 2); bsi_b = bcast(bsi[:, :, d], [128, 32, 16], 2)
            C.op("dve", lambda e, d=d, a=bsr_b: e.tensor_mul(out=Bpr[:, :, d, 0, :], in0=Br[:], in1=a), w=W)
            C.op("dve", lambda e, d=d, a=bsi_b: e.tensor_mul(out=t16[:, :, d, :], in0=Bi[:], in1=a), w=W)
            C.op("dve", lambda e, d=d: e.tensor_sub(out=Bpr[:, :, d, 0, :], in0=Bpr[:, :, d, 0, :], in1=t16[:, :, d, :]), w=W)
            C.op("dve", lambda e, d=d, a=bsr_b: e.tensor_mul(out=Bpr[:, :, d, 1, :], in0=Bi[:], in1=a), w=W)
            C.op("dve", lambda e, d=d, a=bsi_b: e.tensor_mul(out=t16[:, :, d, :], in0=Br[:], in1=a), w=W)
            C.op("dve", lambda e, d=d: e.tensor_add(out=Bpr[:, :, d, 1, :], in0=Bpr[:, :, d, 1, :], in1=t16[:, :, d, :]), w=W)
        C.op("pool", lambda e: e.memset(onesm[:], 1.0), w=W)
        for gp in range(2):
            ps_ = slice(64 * gp, 64 * gp + 64)
            C.op("pool", lambda e, ps_=ps_, gp=gp: e.affine_select(out=Mc[ps_], in_=onesm[ps_], pattern=[[-2, 4], [1, 8], [0, 16]],
                                                                   compare_op=ALU.is_equal, fill=0.0, base=-gp, channel_multiplier=0), w=W)
        k = 0
        for d in range(2):
            for r in range(2):
                for i in range(32):
                    slot = k % 4
                    C.op("dve", lambda e, i=i, d=d, r=r, slot=slot: e.tensor_mul(
                        out=pre[:, slot, :].rearrange("p (g c) -> p g c", g=8), in0=Mc[:, i % 4, :, :],
                        in1=bcast(Bpr[:, i, d, r, :], [128, 8, 16], 1)), w=W)
                    bk = (k // 4) % 2
                    C.op("pe", lambda e, slot=slot, bk=bk: e.transpose(banks[bk][:, slot * 128:(slot + 1) * 128], pre[:, slot, :], ident[:]),
                         r=W + (cst,), w=(bres[bk],))
                    if slot == 3:
                        i0 = i - 3
                        C.op("act", lambda e, bk=bk, i0=i0, d=d, r=r: e.activation(
                            out=Bp[:, i0:i0 + 4, d, r, :], in_=banks[bk][:, :].rearrange("p (a x) -> p a x", a=4), func=AF.Copy),
                            r=(bres[bk],), w=W)
                    k += 1
        C.op("dve", lambda e: e.tensor_scalar_mul(out=Cn[:, 1], in0=Cn[:, 1], scalar1=-1.0), r=R, w=W)
        k = 0
        for d in range(2):
            for gt in range(8):
                for r in range(2):
                    bk = (k // 4) % 2; slot = k % 4
                    C.op("pe", lambda e, bk=bk, slot=slot, r=r, d=d, gt=gt: e.transpose(
                        banks[bk][:, slot * 128:(slot + 1) * 128], Cn[:, r, d, gt, :, :].rearrange("p a b -> p (a b)"), ident[:]),
                        r=W + (cst,), w=(bres[bk],))
                    C.op("act", lambda e, bk=bk, slot=slot, r=r, d=d, gt=gt: e.activation(
                        out=Trep[:, d, gt, r, :], in_=banks[bk][:, slot * 128:(slot + 1) * 128], func=AF.Copy), r=(bres[bk],), w=W)
                    k += 1
        for d in range(2):
            for r in range(2):
                for i in range(32):
                    C.op("dve", lambda e, i=i, d=d, r=r: e.tensor_mul(
                        out=Cp[:, i, d, r, :].rearrange("p (g c) -> p g c", g=8),
                        in0=Trep[:, d, i // 4, r, :].rearrange("p (g c) -> p g c", g=8), in1=Mc[:, i % 4, :, :]), w=W)
        dump("pidx", pidx[:], cst); dump("hicol", hicol[:], cst); dump("colpos", colpos[:], cst); dump("fq", fq[:], cst)
        dump("lbr", lbr[:], sres); dump("lbi", lbi[:], sres); dump("bsr", bsr[:], sres); dump("bsi", bsi[:], sres)
        dump("Bpr", Bpr[:], sres); dump("Mc", Mc[:], sres); dump("Trep", Trep[:], sres)
        dump("Bp", Bp[:], sres); dump("Cp", Cp[:], sres)
        C.barrier()

    UT = 32
    NU = 128 // UT
    ACOL = UT + 1
    RS, DS, IS = 32 * ACOL, 2 * 32 * ACOL, ACOL
    AFR = 2 * DS
    MAXU2 = max(j["n_own"] for j in jobs) * NU
    Abuf = [ssb("A%d" % k, [128, 2, 2, 32, ACOL]) for k in range(2)]
    ares = [C.res("A%d" % k) for k in range(2)]
    T1 = ssb("T1", [128, 2, 2, 32]); T2 = ssb("T2", [128, 2, 2, 32])
    Xbf = ssb("Xbf", [128, 2, 2, 32, UT], BF16)
    bnd = ssb("bnd", [128, MAXU2 + 1, 2, 32], BF16)
    Ef = ssb("Ef", [128, 2, 32]); Sx = ssb("Sx", [128, 2, 32])
    wbs = [(ssb("wb%d" % i, [128, NDT, 256], BF16), C.res("wb%d" % i)) for i in range(3)]
    uTs = [ssb("uT%d" % k, [128, 8, 128], BF16) for k in range(2)]; ures = [C.res("uT%d" % k) for k in range(2)]
    uTx = [ssb("uTx%d" % k, [128, 8, 128], BF16) for k in range(2)]; uxres = [C.res("uTx%d" % k) for k in range(2)]
    gsTs = [ssb("gsT%d" % k, [128, 8, 128], BF16) for k in range(2)]; gres = [C.res("gsT%d" % k) for k in range(2)]
    yT = ssb("yT", [128, 8, 128]); yres = C.res("yT")
    yS = ssb("yS", [128, 8, 128]); ysres = C.res("yS")
    g1 = ssb("g1", [128, 8, 128]); g2 = ssb("g2", [128, 8, 128]); gb = ssb("gb", [128, 8, 128], BF16); g3 = gb
    gwres = C.res("gelu")
    msT = ssb("msT", [128, 8, 128], BF16); mres = C.res("msT")
    xbres = C.res("Xbf"); eres = C.res("carry")
    psT, psL, psB, psY, psM = (0, 1), (2, 3), (4, 5), 6, 7
    junk = hT[:].rearrange("p a b -> p (a b)")

    def a_ap(ab, off, pat):
        return bass.AP(Abuf[ab][:].tensor, off, [[AFR, 128]] + [list(p) for p in pat])

    def front(xt, xres, need_rep):
        C.op("act", lambda e: e.activation(out=junk, in_=xt[:], func=AF.Square, accum_out=stat[:, 0:1]), r=(xres,), w=(hres, stres))
        C.op("dve", lambda e: e.tensor_scalar(out=stat[:, 0:1], in0=stat[:, 0:1], scalar1=1.0 / D, scalar2=EPS, op0=ALU.mult, op1=ALU.add), w=(stres,))
        C.op("act", lambda e: e.activation(out=stat[:, 0:1], in_=stat[:, 0:1], func=AF.Sqrt), w=(stres,))
        C.op("dve", lambda e: e.reciprocal(out=stat[:, 0:1], in_=stat[:, 0:1]), w=(stres,))
        for q in range(4):
            bk = psT[q % 2]
            for kk in range(4):
                dt_ = 4 * q + kk
                C.op("pe", lambda e, bk=bk, kk=kk, dt_=dt_: e.transpose(banks[bk][:, kk * 128:(kk + 1) * 128], xt[:, dt_ * 128:(dt_ + 1) * 128], ident[:]),
                     r=(xres, cst), w=(bres[bk],))
            C.op("dve", lambda e, bk=bk, q=q: e.tensor_tensor(out=hT[:, 4 * q:4 * q + 4, :], in0=banks[bk][:, :].rearrange("p (a t) -> p a t", a=4),
                                                              in1=bcast(nwcol[:, 4 * q:4 * q + 4], [128, 4, 128], 2), op=ALU.mult),
                 r=(bres[bk], ld), w=(hres,))
        if need_rep:
            C.op("dve", lambda e: e.tensor_scalar_mul(out=dg[:], in0=ident[:], scalar1=stat[:, 0:1]), r=(stres, cst), w=(rres,))
            C.op("pe", lambda e: e.matmul(banks[psM][:, 0:128], lhsT=ones_t[:], rhs=dg[:], start=True, stop=True), r=(rres, cst), w=(bres[psM],))
            C.op("act", lambda e: e.activation(out=rrep[:], in_=banks[psM][:, 0:128], func=AF.Copy), r=(bres[psM],), w=(rres,))

    def front_n(xt, xres, split=False):
        C.op("act", lambda e: e.activation(out=junk, in_=xt[:], func=AF.Square, accum_out=stat[:, 0:1]), r=(xres,), w=(hres, stres))
        C.op("pool", lambda e: e.tensor_scalar(out=stat[:, 0:1], in0=stat[:, 0:1], scalar1=1.0 / D, scalar2=EPS, op0=ALU.mult, op1=ALU.add), w=(stres,))
        C.op("pool", lambda e: e.tensor_tensor(out=stat[:, 0:1], in0=stat[:, 0:1], in1=mhalf[:], op=ALU.pow), r=(cst,), w=(stres,))
        C.op("act", lambda e: e.activation(out=xt[:], in_=xt[:], func=AF.Copy, scale=stat[:, 0:1]), r=(stres,), w=(xres,))
        if not split:
            front_nB(xt, xres)

    def front_nB(xt, xres):
        for q in range(4):
            bk = psT[q % 2]
            for kk in range(4):
                dt_ = 4 * q + kk
                C.op("pe", lambda e, bk=bk, kk=kk, dt_=dt_: e.transpose(banks[bk][:, kk * 128:(kk + 1) * 128], xt[:, dt_ * 128:(dt_ + 1) * 128], ident[:]),
                     r=(xres, cst), w=(bres[bk],))
            for kk in range(4):
                dt_ = 4 * q + kk
                C.op("act", lambda e, bk=bk, kk=kk, dt_=dt_: e.activation(out=hT[:, dt_, :], in_=banks[bk][:, kk * 128:(kk + 1) * 128], func=AF.Copy, scale=nwcol[:, dt_:dt_ + 1]),
                     r=(bres[bk], ld), w=(hres,))

    def lin_feat(wt, wres, bk, e_off, ne, nk=NDT, rhs=None, rres_=None):
        rhs = hT if rhs is None else rhs
        rres_ = hres if rres_ is None else rres_
        for e_ in range(ne):
            for dt_ in range(nk):
                C.op("pe", lambda e, e_=e_, dt_=dt_: e.matmul(banks[bk][:, (e_off + e_) * 128:(e_off + e_ + 1) * 128], lhsT=wt[:, dt_, e_ * 128:(e_ + 1) * 128],
                                                               rhs=rhs[:, dt_, :], start=(dt_ == 0), stop=(dt_ == nk - 1)),
                     r=(wres, rres_), w=(bres[bk],))

    def wsrc(wbf, c0, n=256):
        return wbf[:, c0:c0 + n].rearrange("(dt p) c -> p dt c", p=128)

    def glu_view(t):
        return t[:, 0:8, :]

    wk = [0]

    def feat_1024(ws, dsts, func=AF.Copy):
        for half in range(2):
            bk = psL[half]
            for b2 in range(2):
                wt, wres = ws.get(wk[0]); wk[0] += 1
                lin_feat(wt, wres, bk, 2 * b2, 2)
            for dst, dres, sc in dsts:
                if sc is None:
                    C.op("act", lambda e, bk=bk, half=half, dst=dst: e.activation(out=dst[:, 4 * half:4 * half + 4, :], in_=banks[bk][:, :].rearrange("p (a t) -> p a t", a=4), func=func),
                         r=(bres[bk],), w=(dres,))
                else:
                    C.op("act", lambda e, bk=bk, half=half, dst=dst, sc=sc: e.activation(out=dst[:, 4 * half:4 * half + 4, :], in_=banks[bk][:, :].rearrange("p (a t) -> p a t", a=4), func=func, scale=sc),
                         r=(bres[bk], ld), w=(dres,))

    bk_tog = [0]

    def bmm(ab, uT_, ures_, tc0, d, dslot, col0, mode, uT_b=None, ures_b=None):
        PB = 512 // UT
        for r in range(2):
            for pb in range(32 // PB):
                bk = psB[bk_tog[0] % 2]; bk_tog[0] += 1
                for ii in range(PB):
                    i = pb * PB + ii
                    if mode == "set":
                        C.op("pe", lambda e, bk=bk, ii=ii, i=i, r=r: e.matmul(banks[bk][:, ii * UT:(ii + 1) * UT], lhsT=Bp[:, i, d, r, :],
                                                                              rhs=uT_[:, i // 4, tc0:tc0 + UT], start=True, stop=True),
                             r=(sres, ures_), w=(bres[bk],))
                    else:
                        C.op("pe", lambda e, bk=bk, ii=ii, i=i, r=r: e.matmul(banks[bk][:, ii * UT:(ii + 1) * UT], lhsT=Bp[:, i, 0, r, :],
                                                                              rhs=uT_[:, i // 4, tc0:tc0 + UT], start=True, stop=False),
                             r=(sres, ures_), w=(bres[bk],))
                        C.op("pe", lambda e, bk=bk, ii=ii, i=i, r=r: e.matmul(banks[bk][:, ii * UT:(ii + 1) * UT], lhsT=Bp[:, i, 1, r, :],
                                                                              rhs=uT_b[:, i // 4, tc0:tc0 + UT], start=False, stop=True),
                             r=(sres, ures_b), w=(bres[bk],))
                dst = Abuf[ab][:, dslot, r, pb * PB:(pb + 1) * PB, col0:col0 + UT]
                src = banks[bk][:, :].rearrange("p (a t) -> p a t", a=PB)
                C.op("act", lambda e, dst=dst, src=src: e.activation(out=dst, in_=src, func=AF.Copy), r=(bres[bk],), w=(ares[ab],))

    rT1 = C.res("T1"); rT2a = C.res("T2a"); rT2b = C.res("T2b")

    def step_single(ab, dslot, cur, nxt, lr, lsw):
        base = dslot * DS
        c = a_ap(ab, base + cur, [[RS, 2], [IS, 32]]); n = a_ap(ab, base + nxt, [[RS, 2], [IS, 32]])
        csw = a_ap(ab, base + RS + cur, [[-RS, 2], [IS, 32]])
        t1 = T1[:, 0]; t2 = T2[:, 0]
        A_ = (ares[ab],)
        C.op("dve", lambda e: e.tensor_mul(out=t1, in0=c, in1=lr), r=A_, wo=(rT1,))
        C.op("dve", lambda e: e.tensor_mul(out=t2, in0=csw, in1=lsw), r=A_, wo=(rT2a,))
        C.op("dve", lambda e: e.tensor_add(out=n, in0=n, in1=t1), r=(rT1,), w=A_)
        C.op("dve", lambda e: e.tensor_add(out=n, in0=n, in1=t2), r=(rT2a,), w=A_)

    def step_both(ab, j):
        sc = DS + UT - 2 * j; sn_ = DS + UT - 2 - 2 * j
        c = a_ap(ab, j, [[sc, 2], [RS, 2], [IS, 32]]); n = a_ap(ab, j + 1, [[sn_, 2], [RS, 2], [IS, 32]])
        csw = a_ap(ab, j + RS, [[sc, 2], [-RS, 2], [IS, 32]])
        A_ = (ares[ab],)
        C.op("dve", lambda e: e.tensor_mul(out=T1[:], in0=c, in1=Lr2[:]), r=A_, wo=(rT1,))
        C.op("dve", lambda e: e.tensor_mul(out=T2[:], in0=csw, in1=LiS[:]), r=A_, wo=(rT2a,))
        C.op("dve", lambda e: e.tensor_add(out=n, in0=n, in1=T1[:]), r=(rT1,), w=A_)
        C.op("dve", lambda e: e.tensor_add(out=n, in0=n, in1=T2[:]), r=(rT2a,), w=A_)

    for job in jobs:
        n_own, n_oth = job["n_own"], job["n_oth"]
        isS = job["kind"] == "S"
        JK = job["kind"] + str(job["tok0"])
        order = [("x", t) for t in range(n_oth)] + [("o", t) for t in range(n_own - 1, -1, -1)]
        xsrc = [(job["xx"] if kd == "x" else job["xo"])[t * 128:(t + 1) * 128, :] for kd, t in order]
        xs = Stream(C, "sp", xts, xsrc)
        ws = Stream(C, "sp", wbs, [wA[b].rearrange("p (dt c) -> p dt c", dt=NDT) for _ in order for b in range(4)], r_extra=(wc,))
        wk[0] = 0
        C.op("dve", lambda e: e.memset(Sx[:], 0.0), w=(eres,))
        seq = []
        for n, (kd, t) in enumerate(order):
            for u in (range(NU) if kd == "x" else range(NU - 1, -1, -1)):
                seq.append((n, kd, t, u))

        xslot = {}

        def pre1A(n):
            xslot[n] = xs.get(n)
            front_n(xslot[n][0], xslot[n][1], split=True)

        def pre1B(n):
            front_nB(xslot[n][0], xslot[n][1])

        def pre1(n, staged=False):
            if not staged:
                xt, xres = xs.get(n)
                front_n(xt, xres)
            if order[n][0] == "x":
                feat_1024(ws, [(uTs[n % 2], ures[n % 2], flag[:, 0:1]), (uTx[n % 2], uxres[n % 2], flag[:, 1:2])])
            else:
                feat_1024(ws, [(uTs[n % 2], ures[n % 2], None)])

        def B1(k):
            n, kd, t, u = seq[k]
            ab = k % 2
            if kd == "x":
                bmm(ab, uTs[n % 2], ures[n % 2], u * UT, 0, 0, 1, "acc", uTx[n % 2], uxres[n % 2])
            else:
                bmm(ab, uTs[n % 2], ures[n % 2], u * UT, 1, 1, 0, "set")

        pre1(0)
        B1(0)
        bnd_init = False
        for k, (n, kd, t, u) in enumerate(seq):
            ab = k % 2
            if k + 1 < len(seq):
                B1(k + 1)
            stage1 = (k % NU, n + 1)
            if kd == "x":
                C.op("dve", lambda e, ab=ab: e.tensor_copy(out=Abuf[ab][:, 0, :, :, 0], in_=Sx[:]), r=(eres,), w=(ares[ab],))
                for j in range(UT):
                    step_single(ab, 0, j, j + 1, Ls_r[:], Ls_s[:])
                C.op("dve", lambda e, ab=ab: e.tensor_copy(out=Sx[:], in_=Abuf[ab][:, 0, :, :, UT]), r=(ares[ab],), w=(eres,))
            else:
                U = NU * t + u
                if not bnd_init:
                    U_last = NU * n_own
                    if isS and n_oth > 0:
                        C.op("dve", lambda e, U_last=U_last: e.tensor_scalar_mul(out=Sx[:], in0=Sx[:], scalar1=flag[:, 1:2]), r=(ld,), w=(eres,))
                        C.op("dve", lambda e, U_last=U_last: e.tensor_copy(out=bnd[:, U_last, :, :], in_=Sx[:]), w=(eres,))
                    else:
                        C.op("dve", lambda e, U_last=U_last: e.memset(Sx[:], 0.0), w=(eres,))
                        C.op("dve", lambda e, U_last=U_last: e.tensor_copy(out=bnd[:, U_last, :, :], in_=Sx[:]), w=(eres,))
                    bnd_init = True
                C.op("dve", lambda e, ab=ab: e.tensor_copy(out=Abuf[ab][:, 1, :, :, UT], in_=Sx[:]), r=(eres,), w=(ares[ab],))
                for j in range(UT):
                    step_single(ab, 1, UT - j, UT - 1 - j, Lr2[:, 1], LiS[:, 1])
                C.op("dve", lambda e, ab=ab: e.tensor_copy(out=Sx[:], in_=Abuf[ab][:, 1, :, :, 0]), r=(ares[ab],), w=(eres,))
                C.op("dve", lambda e, U=U: e.tensor_copy(out=bnd[:, U, :, :], in_=Sx[:]), w=(eres,))
            if kd == "x" and (k + 1 == len(seq) or seq[k + 1][1] != "x"):
                C.op("dve", lambda e: e.tensor_scalar_mul(out=Ef[:], in0=Sx[:], scalar1=flag[:, 0:1]), r=(ld,), w=(eres,))
            if stage1[1] < len(order):
                if stage1[0] == 0:
                    pre1A(stage1[1])
                elif stage1[0] == 1:
                    pre1B(stage1[1])
                elif stage1[0] == 2:
                    pre1(stage1[1], staged=True)
        if not (isS and n_oth > 0):
            C.op("dve", lambda e: e.memset(Ef[:], 0.0), w=(eres,))
        xs = Stream(C, "sp", xts, [job["xo"][t * 128:(t + 1) * 128, :] for t in range(n_own)])
        blkU = [(wA[b].rearrange("p (dt c) -> p dt c", dt=NDT), None) for b in range(4)]
        blkG = [(wA[4 + b].rearrange("p (dt c) -> p dt c", dt=NDT), None) for b in range(4)]
        blkL = [(wG[b].rearrange("p (et c) -> p et c", et=8), glu_view) for b in range(4)]
        srcs = blkU + blkG
        for t in range(n_own):
            if t >= 1:
                srcs = srcs + blkL
            if t + 1 < n_own:
                srcs = srcs + blkU + blkG
        srcs = srcs + blkL
        ws = Stream(C, "sp", wbs, srcs, r_extra=(wc,))
        wk[0] = 0
        xslot2 = {}

        def p2A(t):
            xslot2[t] = xs.get(t)
            front_n(xslot2[t][0], xslot2[t][1], split=True)

        def p2B(t):
            front_nB(xslot2[t][0], xslot2[t][1])

        def p2C(t):
            feat_1024(ws, [(uTs[t % 2], ures[t % 2], None)])

        def p2D(t):
            feat_1024(ws, [(gsTs[t % 2], gres[t % 2], None)], func=AF.Silu)

        def pre2(t):
            xt, xres = xs.get(t)
            front_n(xt, xres)
            feat_1024(ws, [(uTs[t % 2], ures[t % 2], None)])
            feat_1024(ws, [(gsTs[t % 2], gres[t % 2], None)], func=AF.Silu)

        def B2(k):
            t, u = divmod(k, NU)
            bmm(k % 2, uTs[t % 2], ures[t % 2], u * UT, 0, 0, 1, "set")
            bmm(k % 2, uTs[t % 2], ures[t % 2], u * UT, 1, 1, 0, "set")

        GW = (gwres,)
        PQ = "pool"

        def gA(t):
            uT = uTs[t % 2]
            C.op(PQ, lambda e: e.tensor_tensor(out=g1[:], in0=uT[:], in1=bcast(dcol[:], [128, 8, 128], 2), op=ALU.mult), r=(ures[t % 2], ld), w=GW)
            C.op(PQ, lambda e: e.tensor_add(out=yS[:], in0=yT[:], in1=g1[:]), r=GW + (yres,), w=(ysres,))
            C.op("act", lambda e: e.activation(out=g1[:], in_=yS[:], func=AF.Square), r=(ysres,), w=GW)
            C.op(PQ, lambda e: e.tensor_scalar(out=g1[:], in0=g1[:], scalar1=0.044715, scalar2=1.0, op0=ALU.mult, op1=ALU.add), w=GW)
            C.op(PQ, lambda e: e.tensor_mul(out=g1[:], in0=g1[:], in1=yS[:]), r=(ysres,), w=GW)

        def gB(t):
            C.op("act", lambda e: e.activation(out=g1[:], in_=g1[:], func=AF.Sigmoid, scale=1.5957691216057308), w=GW)
            C.op(PQ, lambda e: e.tensor_mul(out=g2[:], in0=g1[:], in1=yS[:]), r=(ysres,), w=GW)
            C.op("act", lambda e: e.activation(out=gb[:], in_=g2[:], func=AF.Copy), w=GW)

        def gC(t):
            for half in range(2):
                bk = psL[half]
                for b2 in range(2):
                    wt, wres = ws.get(wk[0]); wk[0] += 1
                    lin_feat(wt, wres, bk, 2 * b2, 2, nk=8, rhs=gb, rres_=gwres)
                C.op("act", lambda e, bk=bk, half=half: e.activation(out=g1[:, 4 * half:4 * half + 4, :], in_=banks[bk][:, :].rearrange("p (a t) -> p a t", a=4), func=AF.Sigmoid),
                     r=(bres[bk],), w=GW)
            C.op(PQ, lambda e: e.tensor_mul(out=g2[:], in0=g2[:], in1=g1[:]), w=GW)
            C.op(PQ, lambda e: e.tensor_mul(out=g2[:], in0=g2[:], in1=gsTs[t % 2][:]), r=(gres[t % 2],), w=GW)
            C.op("act", lambda e: e.activation(out=g3[:], in_=g2[:], func=AF.Square), w=GW)

        def gD(t):
            for et in range(8):
                C.op("pe", lambda e, et=et: e.matmul(banks[psM][:, 128:256], lhsT=onesb[:], rhs=g3[:, et, :], start=(et == 0), stop=(et == 7)),
                     r=(gwres, cst), w=(bres[psM],))
            C.op("act", lambda e: e.activation(out=dg[:], in_=banks[psM][:, 128:256], func=AF.Copy, scale=1.0 / 1024), r=(bres[psM],), w=(rres,))
            C.op(PQ, lambda e: e.tensor_scalar_add(out=dg[:], in0=dg[:], scalar1=EPS), w=(rres,))
            C.op(PQ, lambda e: e.tensor_tensor(out=dg[:], in0=dg[:], in1=mhalf[:].broadcast_to([128, 128]), op=ALU.pow), r=(cst,), w=(rres,))
            C.op(PQ, lambda e: e.tensor_tensor(out=g2[:], in0=g2[:], in1=bcast(dg[:], [128, 8, 128], 1), op=ALU.mult), r=(rres,), w=GW)
            C.op(PQ, lambda e: e.tensor_tensor(out=msT[:], in0=g2[:], in1=bcast(snwcol[:], [128, 8, 128], 2), op=ALU.mult), r=(gwres, ld), w=(mres,))
            tok = job["tok0"] + t * 128
            C.dma("sp", ms_d[:, :, tok:tok + 128].rearrange("e p t -> p e t"), msT[:], r=(mres,))

        pre2(0)
        B2(0)
        nk2 = n_own * NU
        for k in range(nk2):
            t, u = divmod(k, NU)
            ab = k % 2
            if k + 1 < nk2:
                B2(k + 1)
            C.op("dve", lambda e, ab=ab: e.tensor_copy(out=Abuf[ab][:, 0, :, :, 0], in_=Ef[:]), r=(eres,), w=(ares[ab],))
            C.op("dve", lambda e, ab=ab, k=k: e.tensor_copy(out=Abuf[ab][:, 1, :, :, UT], in_=bnd[:, k + 1, :, :]), r=(eres,), w=(ares[ab],))
            for j in range(UT):
                step_both(ab, j)
            C.op("dve", lambda e, ab=ab: e.tensor_copy(out=Ef[:], in_=Abuf[ab][:, 0, :, :, UT]), r=(ares[ab],), w=(eres,))
            C.op("pool", lambda e, ab=ab: e.tensor_copy(out=Xbf[:, 0], in_=Abuf[ab][:, 0, :, :, 1:UT + 1]), r=(ares[ab],), w=(xbres,))
            C.op("pool", lambda e, ab=ab: e.tensor_copy(out=Xbf[:, 1], in_=Abuf[ab][:, 1, :, :, 0:UT]), r=(ares[ab],), w=(xbres,))
            for gt in range(8):
                for q in range(4):
                    i = 4 * gt + q
                    tp = dict(tile_position=(0, 96)) if q == 3 else {}
                    kk = 0
                    for d in range(2):
                        for r in range(2):
                            C.op("pe", lambda e, gt=gt, i=i, q=q, d=d, r=r, kk=kk, tp=tp: e.matmul(
                                banks[psY][32 * q:32 * q + 32, gt * UT:(gt + 1) * UT], lhsT=Cp[:, i, d, r, 32 * q:32 * q + 32],
                                rhs=Xbf[:, d, r, i, :], start=(kk == 0), stop=(kk == 3), **tp),
                                 r=(sres, xbres), w=(bres[psY],))
                            kk += 1
            C.op("act", lambda e, u=u: e.activation(out=yT[:, :, u * UT:(u + 1) * UT], in_=banks[psY][:, 0:8 * UT].rearrange("p (a t) -> p a t", a=8), func=AF.Copy),
                 r=(bres[psY],), w=(yres,))
            if u == 0:
                if t >= 1:
                    gB(t - 1)
                if t + 1 < n_own:
                    p2A(t + 1)
            elif u == 1:
                if t >= 1:
                    gC(t - 1)
                if t + 1 < n_own:
                    p2B(t + 1)
            elif u == 2:
                if t >= 1:
                    gD(t - 1)
                if t + 1 < n_own:
                    p2C(t + 1)
            else:
                gA(t)
                if t + 1 < n_own:
                    p2D(t + 1)
        gB(n_own - 1)
        gC(n_own - 1)
        gD(n_own - 1)

    C.barrier()
    ssm_stack.close()

    KT = sb("KT", [128, 2, MAXS * 128], BF16)
    anw = sb("anw", [128, 1024])
    C.dma("sp", anw[:], dram_ap(attn_out_norm_w, 0, [[0, 128], [1, 1024]]), w=(ld,))
    V1 = sb("V1", [128, MAXS, 2, 130], BF16)
    wb2 = [(sb("wc%d" % i, [128, NDT, 512], BF16), C.res("wc%d" % i)) for i in range(2)]
    QT = sb("QT", [128, 2, 512], BF16); qtres = C.res("QT")
    qf = sb("qf", [128, 4, 128]); qn = sb("qn", [128, 4, 128]); qt2 = sb("qt2", [128, 4, 128]); qr = sb("qr", [128, 4, 128], BF16)
    qres = C.res("qwork")
    jk = sb("jk", [128, 128], BF16)
    ga = sb("ga", [128, 1024], BF16); gares = C.res("ga")
    oa = sb("oa", [128, 1024]); oares = C.res("oa")
    Mb = sb("Mb", [128, 1024], BF16)
    MT = sb("MT", [128, NDT, 128], BF16); mtres = C.res("MT"); mt2res = C.res("MT2")
    PTs = [(sb("PT%d" % i, [128, 512], BF16), C.res("PT%d" % i)) for i in range(3)]
    youts = [(sb("yo%d" % i, [128, D]), C.res("yo%d" % i)) for i in range(2)]
    cs = sb("cs", [128, 128]); sn = sb("sn", [128, 128]); rpres = C.res("rope")
    tq = sb("tq", [128, 64]); tf = sb("tf", [128, 64]); ti = sb("ti", [128, 64], I32); pc = sb("pc", [128, 2])
    kvres = C.res("kv")
    psS, psO = (4, 5), (6, 7)
    C.op("pool", lambda e: e.memset(V1[:], 1.0), w=(kvres,))
    SCALE = 128.0 ** -0.5

    def make_rope(kind, t):
        W = (rpres,)
        R = (cst, ld)
        if kind == "x":
            C.op("dve", lambda e: e.tensor_scalar_add(out=pc[:, 0:1], in0=hicol[:], scalar1=float(2 * t)), r=R, w=W)
            C.op("dve", lambda e: e.tensor_mul(out=pc[:, 0:1], in0=pc[:, 0:1], in1=flag[:, 2:3]), r=R, w=W)
            C.op("dve", lambda e: e.scalar_tensor_tensor(out=pc[:, 0:1], in0=flag[:, 1:2], scalar=float((LSO + LSX) // 64 - 1), in1=pc[:, 0:1], op0=ALU.mult, op1=ALU.add), r=R, w=W)
            C.op("dve", lambda e: e.tensor_mul(out=pc[:, 1:2], in0=colpos[:], in1=flag[:, 2:3]), r=R, w=W)
            C.op("dve", lambda e: e.scalar_tensor_tensor(out=pc[:, 1:2], in0=flag[:, 1:2], scalar=63.0, in1=pc[:, 1:2], op0=ALU.mult, op1=ALU.add), r=R, w=W)
        else:
            C.op("dve", lambda e: e.tensor_scalar_add(out=pc[:, 0:1], in0=hicol[:], scalar1=float(2 * t)), r=R, w=W)
            if kind == "s":
                C.op("dve", lambda e: e.scalar_tensor_tensor(out=pc[:, 0:1], in0=flag[:, 0:1], scalar=float(2 * (LSO // 128)), in1=pc[:, 0:1], op0=ALU.mult, op1=ALU.add), r=R, w=W)
            C.op("dve", lambda e: e.tensor_copy(out=pc[:, 1:2], in_=colpos[:]), r=R, w=W)
        for shift, which in ((0.0, "sin"), (0.25, "cos")):
            C.op("dve", lambda e: e.tensor_scalar(out=tq[:, 0:32], in0=fq[:], scalar1=pc[:, 0:1], scalar2=shift, op0=ALU.mult, op1=ALU.add), r=R, w=W)
            C.op("dve", lambda e: e.tensor_scalar(out=tq[:, 32:64], in0=fq[:], scalar1=pc[:, 1:2], scalar2=shift, op0=ALU.mult, op1=ALU.add), r=R, w=W)
            C.op("dve", lambda e: e.tensor_copy(out=ti[:], in_=tq[:]), w=W)
            C.op("dve", lambda e: e.tensor_copy(out=tf[:], in_=ti[:]), w=W)
            C.op("dve", lambda e: e.tensor_sub(out=tq[:], in0=tq[:], in1=tf[:]), w=W)
            C.op("act", lambda e: e.activation(out=tf[:], in_=tq[:], func=AF.Sin, scale=TWO_PI_S), w=W)
            tf3 = tf[:].rearrange("p (a j) -> p a j", a=2)
            if which == "cos":
                c5 = cs[:].rearrange("p (a x j) -> p a x j", a=2, x=2)
                for x_ in range(2):
                    C.op("dve", lambda e, x_=x_: e.tensor_copy(out=c5[:, :, x_, :], in_=tf3), w=W)
            else:
                s5 = sn[:].rearrange("p (a x j) -> p a x j", a=2, x=2)
                C.op("dve", lambda e: e.tensor_scalar_mul(out=s5[:, :, 0, :], in0=tf3, scalar1=-1.0), w=W)
                C.op("dve", lambda e: e.tensor_copy(out=s5[:, :, 1, :], in_=tf3), w=W)

    def qk_norm_rope(bk, nh, wrow):
        W = (qres,)
        C.op("act", lambda e: e.activation(out=qf[:, 0:nh, :], in_=banks[bk][:, 0:nh * 128].rearrange("p (h d) -> p h d", h=nh), func=AF.Copy, scale=stat[:, 0:1]),
             r=(bres[bk], stres), w=W)
        for h in range(nh):
            C.op("act", lambda e, h=h: e.activation(out=jk[:], in_=qf[:, h, :], func=AF.Square, accum_out=stat[:, 1 + h:2 + h]), w=W + (stres,))
        C.op("dve", lambda e: e.tensor_scalar(out=stat[:, 1:1 + nh], in0=stat[:, 1:1 + nh], scalar1=1.0 / 128, scalar2=EPS, op0=ALU.mult, op1=ALU.add), w=W + (stres,))
        C.op("act", lambda e: e.activation(out=stat[:, 1:1 + nh], in_=stat[:, 1:1 + nh], func=AF.Sqrt), w=W + (stres,))
        C.op("dve", lambda e: e.reciprocal(out=stat[:, 1:1 + nh], in_=stat[:, 1:1 + nh]), w=W + (stres,))
        for h in range(nh):
            C.op("dve", lambda e, h=h: e.scalar_tensor_tensor(out=qn[:, h, :], in0=qf[:, h, :], scalar=stat[:, 1 + h:2 + h], in1=wrow[:], op0=ALU.mult, op1=ALU.mult),
                 r=(ld,), w=W)
        qn5 = qn[:, 0:nh, :].rearrange("p h (a x j) -> p h a x j", a=2, x=2)
        t25 = qt2[:, 0:nh, :].rearrange("p h (a x j) -> p h a x j", a=2, x=2)
        s5 = sn[:].rearrange("p (a x j) -> p a x j", a=2, x=2)
        C.op("dve", lambda e: e.tensor_tensor(out=t25[:, :, :, 0, :], in0=qn5[:, :, :, 1, :], in1=bcast(s5[:, :, 0, :], [128, nh, 2, 32], 1), op=ALU.mult), r=(rpres,), w=W)
        C.op("dve", lambda e: e.tensor_tensor(out=t25[:, :, :, 1, :], in0=qn5[:, :, :, 0, :], in1=bcast(s5[:, :, 1, :], [128, nh, 2, 32], 1), op=ALU.mult), r=(rpres,), w=W)
        C.op("dve", lambda e: e.tensor_tensor(out=qn[:, 0:nh, :], in0=qn[:, 0:nh, :], in1=bcast(cs[:], [128, nh, 128], 1), op=ALU.mult), r=(rpres,), w=W)
        dump("qf_%d" % nh, qf[:], qres); dump("qnc_%d" % nh, qn[:], qres); dump("qt2_%d" % nh, qt2[:], qres)
        dump("cs_%d" % nh, cs[:], rpres); dump("sn_%d" % nh, sn[:], rpres); dump("pc_%d" % nh, pc[:], rpres)
        C.op("dve", lambda e: e.tensor_add(out=qr[:, 0:nh, :], in0=qn[:, 0:nh, :], in1=qt2[:, 0:nh, :]), w=W)
        dump("qr_%d" % nh, qr[:], qres)

    def lin_tok(wt, wres, bk, lhs, lres, nk=NDT):
        for dt_ in range(nk):
            C.op("pe", lambda e, dt_=dt_: e.matmul(banks[bk][:, :], lhsT=lhs[:, dt_, :], rhs=wt[:, dt_, :], start=(dt_ == 0), stop=(dt_ == nk - 1)),
                 r=(wres, lres), w=(bres[bk],))

    def wsrc2(wbf, c0):
        return wbf[:, c0:c0 + 512].rearrange("(dt p) c -> p dt c", p=128)

    for job in jobs:
        n_own, n_oth = job["n_own"], job["n_oth"]
        isS = job["kind"] == "S"
        nst = n_own + n_oth
        order = [("o", t) for t in range(n_own)] + [("x", t) for t in range(n_oth)]
        xsrc = [(job["xx"] if kd == "x" else job["xo"])[t * 128:(t + 1) * 128, :] for kd, t in order]
        xs = Stream(C, "sp", xts, xsrc)
        ws = Stream(C, "sp", wb2, [wB[2].rearrange("p (dt c) -> p dt c", dt=NDT) for _ in order], r_extra=(wc,))
        for n, (kd, t) in enumerate(order):
            xt, xres = xs.get(n)
            front(xt, xres, False)
            make_rope("x" if kd == "x" else ("s" if isS else "p"), t)
            wt, wres = ws.get(n)
            bk = psL[n % 2]
            lin_tok(wt, wres, bk, hT, hres)
            qk_norm_rope(bk, 2, knw)
            C.op("act", lambda e, bk=bk, n=n: e.activation(out=V1[:, n, :, 0:128], in_=banks[bk][:, 256:512].rearrange("p (h d) -> p h d", h=2), func=AF.Copy, scale=stat[:, 0:1]),
                 r=(bres[bk], stres), w=(kvres,))
            tb = psT[n % 2]
            tbv = banks[tb][:, 0:128].bitcast(BF16)
            for h in range(2):
                C.op("pe", lambda e, h=h, tbv=tbv: e.transpose(tbv[:, h * 128:(h + 1) * 128], qr[:, h, :], identb[:]), r=(qres, cst), w=(bres[tb],))
            C.op("dve", lambda e, tbv=tbv, n=n: e.tensor_copy(out=KT[:, :, n * 128:(n + 1) * 128], in_=tbv.rearrange("p (h t) -> p h t", h=2)), r=(bres[tb],), w=(kvres,))
        xs = Stream(C, "sp", xts, [job["xo"][t * 128:(t + 1) * 128, :] for t in range(n_own)])
        srcs = []
        for _ in range(n_own):
            srcs += [wB[b].rearrange("p (dt c) -> p dt c", dt=NDT) for b in (0, 1, 3, 4)]
            srcs += [wO[b].rearrange("p (dt c) -> p dt c", dt=NDT) for b in range(4)]
        ws = Stream(C, "sp", wb2, srcs, r_extra=(wc,))
        pt_i = 0
        for t in range(n_own):
            xt, xres = xs.get(t)
            front(xt, xres, False)
            make_rope("s" if isS else "p", t)
            for qb in range(2):
                wt, wres = ws.get(8 * t + qb)
                bk = psL[qb]
                lin_tok(wt, wres, bk, hT, hres)
                qk_norm_rope(bk, 4, qnw)
                tb = psT[qb]
                tbv = banks[tb][:, 0:256].bitcast(BF16)
                for h in range(4):
                    C.op("pe", lambda e, h=h, tbv=tbv: e.transpose(tbv[:, h * 128:(h + 1) * 128], qr[:, h, :], identb[:]), r=(qres, cst), w=(bres[tb],))
                C.op("dve", lambda e, tbv=tbv, qb=qb: e.tensor_copy(out=QT[:, qb, :], in_=tbv), r=(bres[tb],), w=(qtres,))
            for g_ in range(2):
                wt, wres = ws.get(8 * t + 2 + g_)
                bk = psL[g_]
                lin_tok(wt, wres, bk, hT, hres)
                C.op("act", lambda e, bk=bk, g_=g_: e.activation(out=ga[:, g_ * 512:(g_ + 1) * 512], in_=banks[bk][:, :], func=AF.Silu, scale=stat[:, 0:1]),
                     r=(bres[bk], stres), w=(gares,))
            for kvh in range(2):
                for st in range(nst):
                    sbk = psS[st % 2]
                    C.op("pe", lambda e, sbk=sbk, st=st, kvh=kvh: e.matmul(banks[sbk][:, :], lhsT=KT[:, kvh, st * 128:(st + 1) * 128], rhs=QT[:, kvh, :], start=True, stop=True),
                         r=(kvres, qtres), w=(bres[sbk],))
                    pt, ptres = PTs[pt_i % 3]; pt_i += 1
                    C.op("act", lambda e, sbk=sbk, pt=pt: e.activation(out=pt[:], in_=banks[sbk][:, :], func=AF.Exp, scale=SCALE), r=(bres[sbk],), w=(ptres,))
                    for h in range(4):
                        ob = psO[h // 2]; off = (h % 2) * 129
                        C.op("pe", lambda e, ob=ob, off=off, h=h, pt=pt, st=st, kvh=kvh: e.matmul(
                            banks[ob][:, off:off + 129], lhsT=pt[:, h * 128:(h + 1) * 128], rhs=V1[:, st, kvh, 0:129],
                            start=(st == 0 and h % 2 == 0), stop=(st == nst - 1), skip_group_check=True), r=(ptres, kvres), w=(bres[ob],))
                for h in range(4):
                    ob = psO[h // 2]; off = (h % 2) * 129; head = 4 * kvh + h
                    C.op("dve", lambda e, ob=ob, off=off, h=h: e.reciprocal(out=stat[:, 5 + (h % 2):6 + (h % 2)], in_=banks[ob][:, off + 128:off + 129]), r=(bres[ob],), w=(stres,))
                    C.op("dve", lambda e, ob=ob, off=off, h=h, head=head: e.scalar_tensor_tensor(
                        out=oa[:, head * 128:(head + 1) * 128], in0=banks[ob][:, off:off + 128], scalar=stat[:, 5 + (h % 2):6 + (h % 2)],
                        in1=ga[:, head * 128:(head + 1) * 128], op0=ALU.mult, op1=ALU.mult), r=(bres[ob], gares, stres), w=(oares,))
            dump("KT_%s" % job["kind"], KT[:], kvres); dump("V1_%s" % job["kind"], V1[:], kvres); dump("QT_%s%d" % (job["kind"], t), QT[:], qtres)
            dump("ga_%s%d" % (job["kind"], t), ga[:], gares); dump("oa_%s%d" % (job["kind"], t), oa[:], oares)
            C.op("act", lambda e: e.activation(out=Mb[:], in_=oa[:], func=AF.Square, accum_out=stat[:, 7:8]), r=(oares,), w=(mtres, stres))
            C.op("dve", lambda e: e.tensor_scalar(out=stat[:, 7:8], in0=stat[:, 7:8], scalar1=1.0 / 1024, scalar2=EPS, op0=ALU.mult, op1=ALU.add), w=(stres,))
            C.op("act", lambda e: e.activation(out=stat[:, 7:8], in_=stat[:, 7:8], func=AF.Sqrt), w=(stres,))
            C.op("dve", lambda e: e.reciprocal(out=stat[:, 7:8], in_=stat[:, 7:8]), w=(stres,))
            C.op("dve", lambda e: e.scalar_tensor_tensor(out=Mb[:], in0=oa[:], scalar=stat[:, 7:8], in1=anw[:], op0=ALU.mult, op1=ALU.mult), r=(oares, ld), w=(mtres,))
            for q in range(2):
                tb = psT[q]
                tbv = banks[tb][:, 0:256].bitcast(BF16)
                for kk in range(4):
                    et = 4 * q + kk
                    C.op("pe", lambda e, tbv=tbv, kk=kk, et=et: e.transpose(tbv[:, kk * 128:(kk + 1) * 128], Mb[:, et * 128:(et + 1) * 128], identb[:]), r=(mtres, cst), w=(bres[tb],))
                C.op("dve", lambda e, tbv=tbv, q=q: e.tensor_copy(out=MT[:, 4 * q:4 * q + 4, :], in_=tbv.rearrange("p (a t) -> p a t", a=4)), r=(bres[tb],), w=(mtres,))
            tok = job["tok0"] + t * 128
            C.dma("sp", MT[:, 8:16, :], ms_d[:, :, tok:tok + 128].rearrange("e p t -> p e t"), w=(mt2res,))
            dump("MT_%s%d" % (job["kind"], t), MT[:], mt2res)
            yo, yres_ = youts[t % 2]
            for cb in range(4):
                wt, wres = ws.get(8 * t + 4 + cb)
                bk = psL[cb % 2]
                for dt_ in range(NDT):
                    C.op("pe", lambda e, dt_=dt_, bk=bk, wt=wt: e.matmul(banks[bk][:, :], lhsT=MT[:, dt_, :], rhs=wt[:, dt_, :], start=(dt_ == 0), stop=(dt_ == NDT - 1)),
                         r=(wres, mtres, mt2res), w=(bres[bk],))
                C.op("dve", lambda e, bk=bk, cb=cb, yo=yo, xt=xt: e.tensor_tensor(out=yo[:, cb * 512:(cb + 1) * 512], in0=banks[bk][:, :], in1=xt[:, cb * 512:(cb + 1) * 512], op=ALU.add),
                     r=(bres[bk], xres), w=(yres_,))
            C.dma("sp", y_all[tok:tok + 128, :], yo[:], r=(yres_,))
    C.barrier()
    return nc


_CACHE = {}


def _get_nc(cfg_key, cfg):
    if cfg_key not in _CACHE:
        _CACHE[cfg_key] = build(cfg)
    return _CACHE[cfg_key]


def run_layer(x_prompt, x_sample, weights, n_cores=8, debug=False):
    B, LP, _ = x_prompt.shape
    BS, LS, _ = x_sample.shape
    NP = B // n_cores
    half = LS // 2
    cfg = dict(np=NP, lp=LP, lso=half, lsx=half, debug=debug)
    nc = build(cfg)
    in_maps = []
    for c in range(n_cores):
        seq, h = c // 2, c % 2
        own = x_sample[seq, h * half:(h + 1) * half]
        oth = x_sample[seq, (1 - h) * half:(2 - h) * half]
        if h == 0:
            oth = oth[::-1]
        m = {"x_p": np.ascontiguousarray(x_prompt[c * NP:(c + 1) * NP].reshape(NP * LP, D)),
             "x_so": np.ascontiguousarray(own), "x_sx": np.ascontiguousarray(oth),
             "flag": np.tile(np.array([[h, 1 - h, 2 * h - 1, 0]], np.float32), (128, 1))}
        for k, v in weights.items():
            m[k] = np.ascontiguousarray(v, dtype=np.float32)
        in_maps.append(m)
    res = run_bass_kernel_spmd(nc, in_maps, core_ids=list(range(n_cores)))
    y_p = np.empty_like(x_prompt)
    y_s = np.empty_like(x_sample)
    for c in range(n_cores):
        seq, h = c // 2, c % 2
        ya = res.results[c]["y_all"]
        y_p[c * NP:(c + 1) * NP] = ya[:NP * LP].reshape(NP, LP, D)
        y_s[seq, h * half:(h + 1) * half] = ya[NP * LP:]
    if debug:
        return y_p, y_s, res.results
    return y_p, y_s


def kernel(x_prompt, x_sample, **weights):
    x_prompt = np.asarray(x_prompt, dtype=np.float32)
    x_sample = np.asarray(x_sample, dtype=np.float32)
    weights = {k: np.asarray(v, dtype=np.float32) for k, v in weights.items()}
    return run_layer(x_prompt, x_sample, weights, n_cores=8)
```
